# Optimizing a Trainium2 kernel written in Bass

```python
import jax, jax.numpy as jnp
from jax import lax
import numpy as np

D_MODEL = 1024
BATCH = 8
SEQ = 4096
DEPTH = 1
DEC_BATCH = 128
DEC_SEQ = 8
PAST_LEN = 16384
PAGE_SIZE = 128

N_HEADS = 8
D_NOPE = 64
D_ROPE = 32
D_V = 64
D_C = 256
D_CQ = 384
D_ATT = N_HEADS * D_V
D_CONV = 512
CONV_W = 31
D_MIX = D_ATT + D_CONV
D_IN = D_CQ + D_C + D_ROPE + 2 * D_CONV
D_FF = 2816
FFN_CONV_W = 3
ROPE_THETA = 10000.0
Q_BLOCK = 128
EPS = 1e-6
ATTN_SCALE = (D_NOPE + D_ROPE) ** -0.5

kernel_name = 'hymba_mla_conformer_convffn_step'


def rmsnorm(x, g):
    xf = x.astype(jnp.float32)
    y = xf * lax.rsqrt(jnp.mean(xf * xf, axis=-1, keepdims=True) + EPS)
    return (y * g.astype(jnp.float32)).astype(x.dtype)


def layernorm(x, g, b):
    xf = x.astype(jnp.float32)
    mu = jnp.mean(xf, axis=-1, keepdims=True)
    xc = xf - mu
    y = xc * lax.rsqrt(jnp.mean(xc * xc, axis=-1, keepdims=True) + EPS)
    return (y * g.astype(jnp.float32) + b.astype(jnp.float32)).astype(x.dtype)


def rope(x, pos):
    half = D_ROPE // 2
    inv = ROPE_THETA ** (-(jnp.arange(half, dtype=jnp.float32) * 2.0 / D_ROPE))
    ang = pos.astype(jnp.float32)[:, None] * inv[None, :]
    cos = jnp.cos(ang)[None, :, None, :]
    sin = jnp.sin(ang)[None, :, None, :]
    xf = x.astype(jnp.float32)
    x1, x2 = xf[..., :half], xf[..., half:]
    return jnp.concatenate([x1 * cos - x2 * sin, x2 * cos + x1 * sin], axis=-1).astype(x.dtype)


def causal_dwconv(x, prev, w, b):
    xp = jnp.concatenate([prev, x], axis=1)
    y = lax.conv_general_dilated(xp, w[:, None, :], window_strides=(1,), padding='VALID',
                                 dimension_numbers=('NWC', 'WIO', 'NWC'),
                                 feature_group_count=x.shape[-1])
    return y + b, xp[:, -(w.shape[0] - 1):]


def mixer_inputs(h, pos, w_in, g_q_norm, w_uq, g_kv_norm, w_uk):
    proj = h @ w_in
    o1, o2, o3 = D_CQ, D_CQ + D_C, D_CQ + D_C + D_ROPE
    cq, ckv, kr, glu_in = proj[..., :o1], proj[..., o1:o2], proj[..., o2:o3], proj[..., o3:]
    q = jnp.einsum('btc,chd->bthd', rmsnorm(cq, g_q_norm), w_uq)
    q_rope = rope(q[..., D_NOPE:], pos)
    q_lat = jnp.einsum('bthn,chn->bthc', q[..., :D_NOPE], w_uk)
    kr = rope(kr[:, :, None, :], pos)[:, :, 0]
    kv_rows = jnp.concatenate([rmsnorm(ckv, g_kv_norm), kr], axis=-1)
    glu = glu_in[..., :D_CONV] * jax.nn.sigmoid(glu_in[..., D_CONV:])
    return q_lat, q_rope, kv_rows, glu


def mla_attend(q_lat, q_rope, kv, q_pos, k_pos):
    ckv, kr = kv[..., :D_C], kv[..., D_C:]
    s = jnp.einsum('bthc,bsc->bhts', q_lat, ckv) + jnp.einsum('bthr,bsr->bhts', q_rope, kr)
    s = s.astype(jnp.float32) * ATTN_SCALE
    mask = k_pos[None, :] <= q_pos[:, None]
    s = jnp.where(mask[None, None], s, -jnp.inf)
    p = jax.nn.softmax(s, axis=-1).astype(ckv.dtype)
    return jnp.einsum('bhts,bsc->bthc', p, ckv)


def prompt_attention(q_lat, q_rope, kv_rows, pos):
    b, s = q_lat.shape[0], q_lat.shape[1]
    nb = s // Q_BLOCK
    ql = q_lat.reshape(b, nb, Q_BLOCK, N_HEADS, D_C).transpose(1, 0, 2, 3, 4)
    qr = q_rope.reshape(b, nb, Q_BLOCK, N_HEADS, D_ROPE).transpose(1, 0, 2, 3, 4)
    qp = pos.reshape(nb, Q_BLOCK)

    def block(args):
        ql_b, qr_b, qp_b = args
        return mla_attend(ql_b, qr_b, kv_rows, qp_b, pos)

    out = lax.map(block, (ql, qr, qp))
    return out.transpose(1, 0, 2, 3, 4).reshape(b, s, N_HEADS, D_C)


def conv_module(glu, prev, w_dw, b_dw, g_ln, b_ln):
    y, new_prev = causal_dwconv(glu, prev, w_dw, b_dw)
    return jax.nn.silu(layernorm(y, g_ln, b_ln)), new_prev


def merge_and_ffn(x, o_lat, conv_y, ffn_prev, w_uv, g_attn_out, g_conv_out, w_o,
                  g_ffn_norm, w_up, w_ffn_dw, b_ffn_dw, w_down):
    b, t = x.shape[0], x.shape[1]
    attn = jnp.einsum('bthc,chv->bthv', o_lat, w_uv).reshape(b, t, D_ATT)
    mixed = jnp.concatenate([rmsnorm(attn, g_attn_out), rmsnorm(conv_y, g_conv_out)], axis=-1)
    x = x + mixed @ w_o
    u, new_prev = causal_dwconv(rmsnorm(x, g_ffn_norm) @ w_up, ffn_prev, w_ffn_dw, b_ffn_dw)
    x = x + (jax.nn.silu(u[..., :D_FF]) * u[..., D_FF:]) @ w_down
    return x, new_prev


def setup_inputs(seed: int = 0) -> dict:
    key = jax.random.key(seed)
    ks = jax.random.split(key, 32)
    f32 = jnp.float32

    def nrm(k, shape, scale):
        return jax.random.normal(k, shape, f32) * scale

    def gain(k, shape):
        return 1.0 + 0.02 * jax.random.normal(k, shape, f32)

    n_pages = PAST_LEN // PAGE_SIZE
    n_used = DEC_BATCH * n_pages
    n_pool = n_used + max(n_used // 4, 1)
    page_table = jax.random.permutation(ks[0], n_pool)[:n_used].reshape(DEC_BATCH, n_pages).astype(jnp.int32)
    L = DEPTH
    return {
        'x_prompt': nrm(ks[1], (BATCH, SEQ, D_MODEL), 1.0),
        'x_sample': nrm(ks[2], (DEC_BATCH, DEC_SEQ, D_MODEL), 1.0),
        'cache_kv_latent': nrm(ks[3], (L, n_pool, PAGE_SIZE, D_C + D_ROPE), 1.0),
        'state_conv': nrm(ks[4], (L, DEC_BATCH, CONV_W - 1, D_CONV), 0.5),
        'state_ffn_conv': nrm(ks[5], (L, DEC_BATCH, FFN_CONV_W - 1, 2 * D_FF), 1.0),
        'page_table': page_table,
        'g_attn_norm': gain(ks[6], (L, D_MODEL)),
        'w_in': nrm(ks[7], (L, D_MODEL, D_IN), D_MODEL ** -0.5),
        'g_q_norm': gain(ks[8], (L, D_CQ)),
        'w_uq': nrm(ks[9], (L, D_CQ, N_HEADS, D_NOPE + D_ROPE), D_CQ ** -0.5),
        'g_kv_norm': gain(ks[10], (L, D_C)),
        'w_uk': nrm(ks[11], (L, D_C, N_HEADS, D_NOPE), D_C ** -0.5),
        'w_uv': nrm(ks[12], (L, D_C, N_HEADS, D_V), D_C ** -0.5),
        'w_dw': nrm(ks[13], (L, CONV_W, D_CONV), CONV_W ** -0.5),
        'b_dw': nrm(ks[14], (L, D_CONV), 0.02),
        'g_conv_ln': gain(ks[15], (L, D_CONV)),
        'b_conv_ln': nrm(ks[16], (L, D_CONV), 0.02),
        'g_attn_out': gain(ks[17], (L, D_ATT)),
        'g_conv_out': gain(ks[18], (L, D_CONV)),
        'w_o': nrm(ks[19], (L, D_MIX, D_MODEL), D_MIX ** -0.5),
        'g_ffn_norm': gain(ks[20], (L, D_MODEL)),
        'w_up': nrm(ks[21], (L, D_MODEL, 2 * D_FF), D_MODEL ** -0.5),
        'w_ffn_dw': nrm(ks[22], (L, FFN_CONV_W, 2 * D_FF), FFN_CONV_W ** -0.5),
        'b_ffn_dw': nrm(ks[23], (L, 2 * D_FF), 0.02),
        'w_down': nrm(ks[24], (L, D_FF, D_MODEL), D_FF ** -0.5),
        'g_final': gain(ks[25], (D_MODEL,)),
    }


def reference(x_prompt, x_sample, cache_kv_latent, state_conv, state_ffn_conv, page_table,
              g_attn_norm, w_in, g_q_norm, w_uq, g_kv_norm, w_uk, w_uv, w_dw, b_dw,
              g_conv_ln, b_conv_ln, g_attn_out, g_conv_out, w_o, g_ffn_norm, w_up,
              w_ffn_dw, b_ffn_dw, w_down, g_final):
    n_pages = page_table.shape[1]
    pos_p = jnp.arange(SEQ, dtype=jnp.int32)
    pos_s = PAST_LEN + jnp.arange(DEC_SEQ, dtype=jnp.int32)
    k_pos_s = jnp.concatenate([jnp.arange(n_pages * PAGE_SIZE, dtype=jnp.int32), pos_s])
    y_p, y_s = x_prompt, x_sample
    kv_p_new, conv_p_new, ffn_p_new = [], [], []
    kv_s_new, conv_s_new, ffn_s_new = [], [], []
    for l in range(DEPTH):
        mix_w = (w_in[l], g_q_norm[l], w_uq[l], g_kv_norm[l], w_uk[l])
        conv_w = (w_dw[l], b_dw[l], g_conv_ln[l], b_conv_ln[l])
        out_w = (w_uv[l], g_attn_out[l], g_conv_out[l], w_o[l], g_ffn_norm[l], w_up[l],
                 w_ffn_dw[l], b_ffn_dw[l], w_down[l])
        q_lat, q_rope, kv_rows, glu = mixer_inputs(rmsnorm(y_p, g_attn_norm[l]), pos_p, *mix_w)
        o_lat = prompt_attention(q_lat, q_rope, kv_rows, pos_p)
        conv_y, conv_st = conv_module(glu, jnp.zeros((y_p.shape[0], CONV_W - 1, D_CONV), glu.dtype), *conv_w)
        y_p, ffn_st = merge_and_ffn(y_p, o_lat, conv_y,
                                    jnp.zeros((y_p.shape[0], FFN_CONV_W - 1, 2 * D_FF), y_p.dtype), *out_w)
        kv_p_new.append(kv_rows)
        conv_p_new.append(conv_st)
        ffn_p_new.append(ffn_st)
        q_lat, q_rope, kv_rows, glu = mixer_inputs(rmsnorm(y_s, g_attn_norm[l]), pos_s, *mix_w)
        past = jnp.take(cache_kv_latent[l], page_table, axis=0).reshape(
            page_table.shape[0], n_pages * PAGE_SIZE, D_C + D_ROPE)
        keys = jnp.concatenate([past, kv_rows], axis=1)
        o_lat = mla_attend(q_lat, q_rope, keys, pos_s, k_pos_s)
        conv_y, conv_st = conv_module(glu, state_conv[l], *conv_w)
        y_s, ffn_st = merge_and_ffn(y_s, o_lat, conv_y, state_ffn_conv[l], *out_w)
        kv_s_new.append(kv_rows)
        conv_s_new.append(conv_st)
        ffn_s_new.append(ffn_st)
    y_prompt = rmsnorm(y_p, g_final)
    y_sample = rmsnorm(y_s, g_final)
    new_kv_prompt = jnp.stack(kv_p_new)
    new_conv_prompt = jnp.stack(conv_p_new)
    new_ffn_prompt = jnp.stack(ffn_p_new)
    new_kv_sample = jnp.stack(kv_s_new)
    new_conv_sample = jnp.stack(conv_s_new)
    new_ffn_sample = jnp.stack(ffn_s_new)
    return (y_prompt, y_sample, new_kv_prompt, new_conv_prompt, new_ffn_prompt,
            new_kv_sample, new_conv_sample, new_ffn_sample)
```

```python
import numpy as np
import ml_dtypes
import concourse.bass as bass
import concourse.mybir as mybir
from concourse.bass_utils import run_bass_kernel_spmd

F32 = mybir.dt.float32
BF = mybir.dt.bfloat16
I32 = mybir.dt.int32
AF = mybir.ActivationFunctionType
ALU = mybir.AluOpType

D = 1024
NH = 8
DC = 256
DCQ = 384
DR = 32
DCONV = 512
CW = 31
DIN = 1696
DFF = 2816
NFC = 22
EPS = 1e-6
SCALE = 96.0 ** -0.5
SPC = 16
DSEQ = 8
PAGE = 128

CFG_FULL = dict(SEQ=4096, NPG=128, NPOOL=20480, PAST=16384)


class Prog:
    ENGS = ("pe", "act", "dve", "pool", "sp")
    RING = 8

    def __init__(self):
        self.ops = []
        self.lastw = {}
        self.readers = {}
        self.cnt = {e: 0 for e in self.ENGS}
        self.dcnt = {e: 0 for e in self.ENGS}
        self.pending_pe = []
        self.cur_barrier = None

    def barrier(self, fn):
        keys = list(set(self.lastw) | set(self.readers))
        o = self.op("pool", fn, writes=keys)
        self.cur_barrier = o["idx"]

    def op(self, eng, fn, reads=(), writes=(), dma=False, sig=True):
        o = dict(eng=eng, fn=fn, dma=dma, sig=sig, deps=set(), idx=len(self.ops))
        deps = set()
        if self.cur_barrier is not None:
            deps.add(self.cur_barrier)
        for r in reads:
            if r in self.lastw:
                deps.add(self.lastw[r])
        for w in writes:
            if w in self.lastw:
                deps.add(self.lastw[w])
            for rd in self.readers.get(w, ()):
                deps.add(rd)
        deps.discard(o["idx"])
        o["deps"] = deps
        if dma:
            i = self.dcnt[eng]
            self.dcnt[eng] += 1
            o["dsem"] = (eng, i % self.RING)
            o["dval"] = 16 * (i // self.RING + 1)
            o["dprev"] = 16 * (i // self.RING)
        else:
            if sig:
                self.cnt[eng] += 1
                o["sval"] = self.cnt[eng]
                if eng == "pe":
                    for p in self.pending_pe:
                        p["sval"] = self.cnt[eng]
                    self.pending_pe = []
            else:
                assert eng == "pe"
                self.pending_pe.append(o)
        self.ops.append(o)
        for w in writes:
            self.lastw[w] = o["idx"]
            self.readers[w] = []
        for r in reads:
            if r not in writes:
                self.readers.setdefault(r, []).append(o["idx"])
        return o

    def check(self):
        assert not self.pending_pe
        ops = self.ops
        queues = {e: [o for o in ops if o["eng"] == e] for e in self.ENGS}
        pos = {e: 0 for e in self.ENGS}
        done = [False] * len(ops)
        sigdone = {e: 0 for e in self.ENGS}
        ddone = {}
        dq = {e: [o for o in queues[e] if o["dma"]] for e in self.ENGS}
        progress = True
        while progress:
            progress = False
            for e in self.ENGS:
                while pos[e] < len(queues[e]):
                    o = queues[e][pos[e]]
                    ok = True
                    for d in o["deps"]:
                        do = ops[d]
                        if do["dma"]:
                            if not done[d]:
                                ok = False
                                break
                        elif do["eng"] == "pe" and e == "pe":
                            continue
                        elif sigdone[do["eng"]] < do["sval"]:
                            ok = False
                            break
                    if not ok:
                        break
                    done[o["idx"]] = True
                    if not o["dma"] and o["sig"]:
                        sigdone[e] = o["sval"]
                    pos[e] += 1
                    progress = True
        stuck = {e: (pos[e], len(queues[e])) for e in self.ENGS if pos[e] < len(queues[e])}
        if stuck:
            for e in stuck:
                o = queues[e][pos[e]]
                print("STUCK", e, o["idx"], [(d, ops[d]["eng"], ops[d].get("sval"), done[d]) for d in o["deps"] if not (done[d] and (ops[d]["dma"] or sigdone[ops[d]["eng"]] >= ops[d].get("sval", 0)))][:6])
            raise RuntimeError("deadlock in program: %s" % stuck)

    def emit(self, nc):
        assert not self.pending_pe
        self.check()
        import contextlib
        with contextlib.ExitStack() as es:
            csem = {e: es.enter_context(nc.semaphore("c_" + e)) for e in ("pe", "act", "dve", "pool")}
            dsem = {}
            for e in ("sp", "pool", "act"):
                if self.dcnt[e]:
                    for r in range(self.RING):
                        dsem[(e, r)] = es.enter_context(nc.semaphore("d_%s%d" % (e, r)))
            block = es.enter_context(nc.Block())
            ops = self.ops

            def run(engname, eng):
                waited = {}

                def wait(sem, key, val):
                    if waited.get(key, 0) >= val:
                        return
                    waited[key] = val
                    eng.wait_ge(sem, val)

                for o in ops:
                    if o["eng"] != engname:
                        continue
                    for d in sorted(o["deps"]):
                        do = ops[d]
                        if do["dma"]:
                            wait(dsem[do["dsem"]], do["dsem"], do["dval"])
                        else:
                            if do["eng"] == "pe" and engname == "pe":
                                continue
                            wait(csem[do["eng"]], do["eng"], do["sval"])
                    if o["dma"]:
                        if o["dprev"] > 0:
                            wait(dsem[o["dsem"]], o["dsem"], o["dprev"])
                        ins = o["fn"](eng)
                        ins.then_inc(dsem[o["dsem"]], 16)
                    else:
                        ins = o["fn"](eng)
                        if o["sig"]:
                            ins.then_inc(csem[engname], 1)
                if engname == "sp":
                    for (e, r), s in dsem.items():
                        n = self.dcnt[e]
                        k = (n - 1 - r) // self.RING + 1 if n > r else 0
                        if k > 0:
                            eng.wait_ge(s, 16 * k)
                    for e in ("pe", "act", "dve", "pool"):
                        if self.cnt[e]:
                            eng.wait_ge(csem[e], self.cnt[e])

            @block.sync
            def _(e):
                run("sp", e)

            @block.tensor
            def _(e):
                run("pe", e)

            @block.scalar
            def _(e):
                run("act", e)

            @block.vector
            def _(e):
                run("dve", e)

            @block.gpsimd
            def _(e):
                run("pool", e)


def build(cfg):
    SEQ, NPG, NPOOL, PAST = cfg["SEQ"], cfg["NPG"], cfg["NPOOL"], cfg["PAST"]
    NPT = SEQ // 128
    NTL = NPT + 1
    NT = NTL * 128
    NST = SEQ // 512
    NPAGES = SPC * NPG

    nc = bass.Bass("TRN2", target_bir_lowering=False)
    P = Prog()

    def din(name, shape, dt=F32):
        return nc.dram_tensor(name, list(shape), dt, kind="ExternalInput").ap()

    def dout(name, shape, dt=F32):
        return nc.dram_tensor(name, list(shape), dt, kind="ExternalOutput").ap()

    xp = din("xp", [SEQ, D])
    xs = din("xs", [128, D])
    cache = din("cache", [NPOOL * PAGE, DC + DR])
    stc = din("stc", [SPC * 30, DCONV])
    stf = din("stf", [SPC * 2, 2 * DFF])
    ptab = din("ptab", [1, NPAGES], I32)
    w_in = din("w_in", [128, 8, DIN])
    w_uq = din("w_uq", [128, 3, 768])
    w_uqs = din("w_uqs", [128, 3, 768])
    w_k = din("w_k", [128, 3, 768])
    w_ukT = din("w_ukT", [64, NH, DC])
    w_uv = din("w_uv", [128, 2, 512])
    w_o = din("w_o", [128, 8, D])
    w_up = din("w_up", [NFC, 128, 8, 256])
    w_dn = din("w_dn", [128, NFC, D])
    wdw = din("wdw", [128, 4, CW])
    vecs = din("vecs", [128, 64])
    wf = din("wf", [128, 44, 4])
    rowv = din("rowv", [1, 3072])
    cstok = din("cstok", [NT, 32])
    cosT = din("cosT", [32, NT])
    sinT = din("sinT", [32, NT])
    identf_d = din("identf", [128, 128])
    identb_d = din("identb", [128, 128], BF)
    tri_d = din("tri", [128, 128], BF)
    smask_d = din("smask", [128, SPC * 64], BF)

    y_p = dout("y_p", [SEQ, D])
    y_s = dout("y_s", [128, D])
    kv_p = dout("kv_p", [SEQ, DC + DR])
    conv_p = dout("conv_p", [30, DCONV])
    ffn_p = dout("ffn_p", [2, 2 * DFF])
    kv_s = dout("kv_s", [128, DC + DR])
    conv_s = dout("conv_s", [SPC, 30, DCONV])
    ffn_s = dout("ffn_s", [SPC * 2, 2 * DFF])

    class Arena:
        def __init__(self):
            self.off = 16512
            self.n = 0

        def alloc(self, shape, dt, name=None):
            esz = 4 if dt in (F32, I32) else 2
            nbytes = int(np.prod(shape[1:])) * esz
            self.off = (self.off + 63) // 64 * 64
            self.n += 1
            t = nc.alloc_sbuf_tensor_at("sb%d_%s" % (self.n, name or ""), list(shape), dt, offset=self.off)
            self.last = self.off
            self.off += nbytes
            assert self.off <= 229376, ("SBUF overflow", self.off, name)
            return t

        def at(self, shape, dt, name, offset):
            self.n += 1
            return nc.alloc_sbuf_tensor_at("sb%d_%s" % (self.n, name), list(shape), dt, offset=offset)

        def mark(self):
            return self.off

        def reset(self, m):
            self.off = m

    A = Arena()
    identf = A.alloc([128, 128], F32, "identf")
    identb = A.alloc([128, 128], BF, "identb")
    tri = A.alloc([128, 128], BF, "tri")
    vec = A.alloc([128, 64], F32, "vec")
    epsT = A.alloc([128, 1], F32, "eps")
    rstd_c = A.alloc([128, NTL], F32, "rstdc")
    bar_t = A.alloc([128, 2], F32, "bar")
    GA, GQ, GCO, GAO, GF, BDW = 0, 8, 11, 15, 19, 27
    convT = A.alloc([128, 4, NT], BF, "convT")
    attn_tok = A.alloc([128, NTL, 512], BF, "attn_tok")
    m_persist = A.mark()
    cqnT = A.alloc([128, 3, NT], BF, "cqnT")
    kvT = A.alloc([128, 3, NT], BF, "kvT")
    kvbf_s = A.alloc([128, 289], BF, "kvbf_s")
    m_ab = A.mark()

    ps = [nc.alloc_psum_tensor("ps%d" % i, [128, 512], F32) for i in range(8)]

    def psb(i):
        return ps[i][:, :].bitcast(BF)

    def dma(q, out, in_, reads=(), writes=()):
        def f(e):
            return e.dma_start(out=out, in_=in_)
        return P.op(q, f, reads=reads, writes=writes, dma=True)

    def mm(out, lhsT, rhs, start, stop, reads=(), writes=(), sig=None):
        def f(e):
            return e.matmul(out, lhsT, rhs, start=start, stop=stop)
        return P.op("pe", f, reads=reads, writes=writes, sig=(stop if sig is None else sig))

    def tr(out, in_, ident, reads=(), writes=(), sig=True):
        def f(e):
            return e.transpose(out, in_, ident)
        return P.op("pe", f, reads=reads, writes=writes, sig=sig)

    def act(out, in_, func, reads=(), writes=(), **kw):
        def f(e):
            return e.activation(out=out, in_=in_, func=func, **kw)
        return P.op("act", f, reads=reads, writes=writes)

    def tt(eng, out, in0, in1, op, reads=(), writes=()):
        def f(e):
            return e.tensor_tensor(out=out, in0=in0, in1=in1, op=op)
        return P.op(eng, f, reads=reads, writes=writes)

    def ts(eng, out, in0, s1, s2, op0, op1=None, reads=(), writes=()):
        def f(e):
            if op1 is None:
                return e.tensor_scalar(out=out, in0=in0, scalar1=s1, scalar2=None, op0=op0)
            return e.tensor_scalar(out=out, in0=in0, scalar1=s1, scalar2=s2, op0=op0, op1=op1)
        return P.op(eng, f, reads=reads, writes=writes)

    def stt(eng, out, in0, scalar, in1, op0, op1, reads=(), writes=()):
        eng = "dve"

        def f(e):
            return e.scalar_tensor_tensor(out=out, in0=in0, scalar=scalar, in1=in1, op0=op0, op1=op1)
        return P.op(eng, f, reads=reads, writes=writes)

    def cp(eng, out, in_, reads=(), writes=()):
        if eng == "act":
            return act(out, in_, AF.Copy, reads=reads, writes=writes)

        def f(e):
            return e.tensor_copy(out=out, in_=in_)
        return P.op(eng, f, reads=reads, writes=writes)

    def memset(eng, ap, val, writes=()):
        def f(e):
            return e.memset(ap, val)
        return P.op(eng, f, writes=writes)

    def recip(out, in_, reads=(), writes=()):
        def f(e):
            return e.reciprocal(out=out, in_=in_)
        return P.op("dve", f, reads=reads, writes=writes)

    def rstd_from_ss(ss, n, tmp, out, key):
        act(tmp, ss, AF.Sqrt, reads=[key + "ss", "eps"], writes=[key + "tmp"], scale=1.0 / n, bias=epsT[:, 0:1])
        recip(out, tmp, reads=[key + "tmp"], writes=[key])

    def bcast(v, shape):
        return v.unsqueeze(2).to_broadcast(list(shape))

    dma("sp", identf[:, :], identf_d, writes=["identf"])
    dma("sp", identb[:, :], identb_d, writes=["identb"])
    dma("sp", tri[:, :], tri_d, writes=["tri"])
    dma("sp", vec[:, :], vecs, writes=["vec"])
    memset("dve", epsT[:, :], EPS, writes=["eps"])

    win = A.alloc([128, 8, DIN], BF, "win")
    xt = [A.alloc([128, D], F32, "xt%d" % i) for i in range(2)]
    junk = A.alloc([128, D], BF, "junk")
    xnb = A.alloc([128, D], BF, "xnb")
    hT = A.alloc([128, 8, 128], BF, "hT")
    small = A.alloc([128, 32], F32, "small")
    cqnb = A.alloc([128, DCQ], BF, "cqnb")
    kvrow = [A.alloc([128, 288], F32, "kvrow%d" % i) for i in range(2)]
    kvbf = A.alloc([128, 288], BF, "kvbf")
    krs = A.alloc([128, 32], F32, "krs")
    rtmp = A.alloc([128, 64], F32, "rtmp")
    cs = A.alloc([128, NTL, 32], F32, "cs")
    gkvb = A.alloc([128, 256], F32, "gkvb")
    glnb = A.alloc([128, 512], F32, "glnb")
    blnb = A.alloc([128, 512], F32, "blnb")
    sig_t = A.alloc([128, 4, 128], F32, "sig")
    gluT = A.alloc([128, 4, 30 + 128], F32, "gluT")
    gluS = A.alloc([128, 4, SPC, 38], F32, "gluS")
    yT = A.alloc([128, 4, 128], F32, "yT")
    zt = A.alloc([128, 512], F32, "zt")
    s32 = A.alloc([128, 512], F32, "s32")
    sbf = A.alloc([128, 512], BF, "sbf")
    wdw_t = A.alloc([128, 4, CW], F32, "wdw")
    stt_t = A.alloc([120, 512], F32, "stt")
    gtok = A.alloc([128, 512], F32, "gtok")
    bnst = A.alloc([128, 8], F32, "bnst")
    assert A.mark() < 229000

    dma("pool", win[:, :, :], w_in, writes=["win"])
    dma("sp", cs[:, :, :], cstok.rearrange("(t p) c -> p t c", p=128), writes=["cs"])
    dma("sp", gkvb[:, :], rowv[:, 0:256].partition_broadcast(128), writes=["gkvb"])
    dma("sp", glnb[:, :], rowv[:, 256:768].partition_broadcast(128), writes=["glnb"])
    dma("sp", blnb[:, :], rowv[:, 768:1280].partition_broadcast(128), writes=["blnb"])
    dma("sp", wdw_t[:, :, :], wdw, writes=["wdw"])
    memset("pool", gluT[:, :, :], 0.0, writes=["gluT"])

    for r in range(4):
        dma("sp", stt_t[:, :], stc[120 * r:120 * r + 120, :], writes=["stt"])
        for c in range(4):
            tr(ps[0][:, 128 * c:128 * c + 120], stt_t[:, 128 * c:128 * c + 128], identf[:120, :120],
               reads=["stt", "identf"], writes=["ps0"], sig=(c == 3))
        for c in range(4):
            cp("dve", gluS[:, c, 4 * r:4 * r + 4, 0:30],
               ps[0][:, 128 * c:128 * c + 120].rearrange("p (s j) -> p s j", j=30),
               reads=["ps0"], writes=["gluS"])
    dma("sp", conv_s[:, 0:22, :], stc.rearrange("(s j) c -> s j c", j=30)[:, 8:30, :])

    sm = lambda i: small[:, i:i + 1]
    for t in range(NTL):
        samp = (t == NPT)
        xsrc = xs if samp else xp[128 * t:128 * t + 128, :]
        x_t = xt[t % 2]
        xk = "xt%d" % (t % 2)
        dma("sp", x_t[:, :], xsrc, writes=[xk])
        act(junk[:, :], x_t[:, :], AF.Square, reads=[xk], writes=["junk", "ss0ss"], accum_out=sm(0))
        rstd_from_ss(sm(0), D, sm(1), sm(2), "ss0")
        act(xnb[:, :], x_t[:, :], AF.Copy, reads=[xk, "ss0"], writes=["xnb"], scale=sm(2))
        for k in range(8):
            tr(psb(0)[:, 128 * k:128 * k + 128], xnb[:, 128 * k:128 * k + 128], identb[:, :],
               reads=["xnb", "identb"], writes=["ps0"], sig=(k == 7))
        tt("dve", hT[:, :, :], psb(0)[:, :].rearrange("p (k n) -> p k n", n=128),
           bcast(vec[:, GA:GA + 8], [128, 8, 128]), ALU.mult, reads=["ps0", "vec"], writes=["hT"])
        for k in range(8):
            mm(ps[1][:, 0:384], hT[:, k, :], win[:, k, 0:384], k == 0, k == 7, reads=["hT", "win"], writes=["ps1"])
        for k in range(8):
            mm(ps[2][:, 0:288], hT[:, k, :], win[:, k, 384:672], k == 0, k == 7, reads=["hT", "win"], writes=["ps2"])
        for c in range(8):
            bank = 3 if c < 4 else 4
            cc = c % 4
            for k in range(8):
                mm(ps[bank][:, 128 * cc:128 * cc + 128], win[:, k, 672 + 128 * c:672 + 128 * c + 128], hT[:, k, :],
                   k == 0, k == 7, reads=["hT", "win"], writes=["ps%d" % bank], sig=(k == 7 and cc == 3))
        act(junk[:, 0:384], ps[1][:, 0:384], AF.Square, reads=["ps1"], writes=["junk", "ss1ss"], accum_out=sm(3))
        rstd_from_ss(sm(3), DCQ, sm(4), sm(5), "ss1")
        act(cqnb[:, :], ps[1][:, 0:384], AF.Copy, reads=["ps1", "ss1"], writes=["cqnb"], scale=sm(5))
        for k in range(3):
            tr(psb(5)[:, 128 * k:128 * k + 128], cqnb[:, 128 * k:128 * k + 128], identb[:, :],
               reads=["cqnb", "identb"], writes=["ps5"], sig=(k == 2))
        tt("dve", cqnT[:, :, 128 * t:128 * t + 128], psb(5)[:, 0:384].rearrange("p (k n) -> p k n", n=128),
           bcast(vec[:, GQ:GQ + 3], [128, 3, 128]), ALU.mult, reads=["ps5", "vec"], writes=["cqnT"])
        kr_ = kvrow[t % 2]
        kk = "kvrow%d" % (t % 2)
        act(junk[:, 0:256], ps[2][:, 0:256], AF.Square, reads=["ps2"], writes=["junk", "ss2ss"], accum_out=sm(6))
        rstd_from_ss(sm(6), DC, sm(7), sm(8), "ss2")
        stt("dve", kr_[:, 0:256], ps[2][:, 0:256], sm(8), gkvb[:, :], ALU.mult, ALU.mult,
            reads=["ps2", "ss2", "gkvb"], writes=[kk])
        cp("act", krs[:, :], ps[2][:, 256:288], reads=["ps2"], writes=["krs"])
        cst = cs[:, t, :]
        tt("pool", rtmp[:, 0:16], krs[:, 0:16], cst[:, 0:16], ALU.mult, reads=["krs", "cs"], writes=["rt0"])
        tt("pool", rtmp[:, 16:32], krs[:, 16:32], cst[:, 16:32], ALU.mult, reads=["krs", "cs"], writes=["rt1"])
        tt("pool", kr_[:, 256:272], rtmp[:, 0:16], rtmp[:, 16:32], ALU.subtract, reads=["rt0", "rt1"], writes=[kk])
        tt("pool", rtmp[:, 32:48], krs[:, 16:32], cst[:, 0:16], ALU.mult, reads=["krs", "cs"], writes=["rt2"])
        tt("pool", rtmp[:, 48:64], krs[:, 0:16], cst[:, 16:32], ALU.mult, reads=["krs", "cs"], writes=["rt3"])
        tt("pool", kr_[:, 272:288], rtmp[:, 32:48], rtmp[:, 48:64], ALU.add, reads=["rt2", "rt3"], writes=[kk])
        dma("sp", (kv_s if samp else kv_p[128 * t:128 * t + 128, :]), kr_[:, :], reads=[kk])
        cp("act", kvbf[:, :], kr_[:, :], reads=[kk], writes=["kvbf"])
        if samp:
            cp("act", kvbf_s[:, 0:288], kr_[:, :], reads=[kk], writes=["kvbf_s"])
            memset("pool", kvbf_s[:, 288:289], 1.0, writes=["kvbf_s1"])
        tr(psb(5)[:, 0:128], kvbf[:, 0:128], identb[:, :], reads=["kvbf", "identb"], writes=["ps5"], sig=False)
        tr(psb(5)[:, 128:256], kvbf[:, 128:256], identb[:, :], reads=["kvbf", "identb"], writes=["ps5"], sig=False)
        tr(psb(5)[:96, 256:384], kvbf[:, 192:288], identb[:, :], reads=["kvbf", "identb"], writes=["ps5"])
        cp("dve", kvT[:, 0:2, 128 * t:128 * t + 128], psb(5)[:, 0:256].rearrange("p (k n) -> p k n", n=128),
           reads=["ps5"], writes=["kvT"])
        cp("dve", kvT[:96, 2, 128 * t:128 * t + 128], psb(5)[:96, 256:384], reads=["ps5"], writes=["kvT"])
        act(sig_t[:, :, :], ps[4][:, :].rearrange("p (c n) -> p c n", n=128), AF.Sigmoid, reads=["ps4"], writes=["sig"])
        if not samp:
            tt("dve", gluT[:, :, 30:158], ps[3][:, :].rearrange("p (c n) -> p c n", n=128), sig_t[:, :, :], ALU.mult,
               reads=["ps3", "sig"], writes=["gluT"])
        else:
            for c in range(4):
                tt("dve", gluS[:, c, :, 30:38], ps[3][:, 128 * c:128 * c + 128].rearrange("p (s j) -> p s j", j=8),
                   sig_t[:, c, :].rearrange("p (s j) -> p s j", j=8), ALU.mult, reads=["ps3", "sig"], writes=["gluS"])
        gk = "gluS" if samp else "gluT"

        def csrc(c, k):
            if samp:
                return gluS[:, c, :, k:k + 8]
            return gluT[:, c, k:k + 128]

        def cdst(c):
            return yT[:, c, :].rearrange("p (s j) -> p s j", j=8) if samp else yT[:, c, :]
        for c in range(4):
            ts("dve", cdst(c), csrc(c, 0), wdw_t[:, c, 0:1], vec[:, BDW + c:BDW + c + 1], ALU.mult, ALU.add,
               reads=[gk, "wdw", "vec"], writes=["yT%d" % c])
        for k in range(1, CW):
            for c in range(4):
                stt("dve", cdst(c), csrc(c, k), wdw_t[:, c, k:k + 1], cdst(c), ALU.mult, ALU.add,
                    reads=[gk, "wdw", "yT%d" % c], writes=["yT%d" % c])
        if t == NPT - 1:
            for c in range(4):
                tr(ps[6][:30, 128 * c:128 * c + 128], gluT[:, c, 128:158], identf[:, :],
                   reads=["gluT", "identf"], writes=["ps6"], sig=(c == 3))
            cp("act", gtok[:30, :], ps[6][:30, :], reads=["ps6"], writes=["gtok"])
            dma("sp", conv_p, gtok[:30, :], reads=["gtok"])
        if samp:
            for c in range(4):
                cp("pool", sig_t[:, c, :].rearrange("p (s j) -> p s j", j=8), gluS[:, c, :, 30:38],
                   reads=["gluS", "sig"], writes=["sig"])
            for c in range(4):
                tr(ps[6][:, 128 * c:128 * c + 128], sig_t[:, c, :], identf[:, :],
                   reads=["sig", "identf"], writes=["ps6"], sig=(c == 3))
            cp("act", gtok[:, :], ps[6][:, :], reads=["ps6"], writes=["gtok"])
            for s in range(SPC):
                dma("sp", conv_s[s, 22:30, :], gtok[8 * s:8 * s + 8, :], reads=["gtok"])
        if not samp:
            cp("pool", gluT[:, :, 0:30], gluT[:, :, 128:158], reads=["gluT", "yT2", "yT3", "yT0", "yT1"], writes=["gluT"])
        for c in range(4):
            tr(ps[7][:, 128 * c:128 * c + 128], yT[:, c, :], identf[:, :],
               reads=["yT%d" % c, "identf"], writes=["ps7"], sig=(c == 3))
        P.op("dve", lambda e: e.bn_stats(out=bnst[:, 0:6], in_=ps[7][:, :]), reads=["ps7"], writes=["bnst"])
        P.op("dve", lambda e: e.bn_aggr(out=bnst[:, 6:8], in_=bnst[:, 0:6]), reads=["bnst"], writes=["mv"])
        act(sm(9), bnst[:, 7:8], AF.Sqrt, reads=["mv", "eps"], writes=["lntmp"], scale=1.0, bias=epsT[:, 0:1])
        recip(sm(10), sm(9), reads=["lntmp"], writes=["lnr"])
        ts("dve", sm(11), bnst[:, 6:7], -1.0, sm(10), ALU.mult, ALU.mult, reads=["mv", "lnr"], writes=["lnb"])
        act(zt[:, :], ps[7][:, :], AF.Identity, reads=["ps7", "lnr", "lnb"], writes=["zt"], scale=sm(10), bias=sm(11))
        tt("dve", zt[:, :], zt[:, :], glnb[:, :], ALU.mult, reads=["zt", "glnb"], writes=["zt"])
        tt("pool", zt[:, :], zt[:, :], blnb[:, :], ALU.add, reads=["zt", "blnb"], writes=["zt"])
        act(s32[:, :], zt[:, :], AF.Silu, reads=["zt"], writes=["s32"])
        act(junk[:, 0:512], s32[:, :], AF.Square, reads=["s32"], writes=["junk", "ss3ss"], accum_out=sm(12))
        rstd_from_ss(sm(12), DCONV, sm(13), rstd_c[:, t:t + 1], "ss3")
        cp("dve", sbf[:, :], s32[:, :], reads=["s32"], writes=["sbf"])
        for c in range(4):
            tr(psb(5)[:, 512 + 128 * c:512 + 128 * c + 128], sbf[:, 128 * c:128 * c + 128], identb[:, :],
               reads=["sbf", "identb"], writes=["ps5"], sig=(c == 3))
        tt("dve", convT[:, :, 128 * t:128 * t + 128], psb(5)[:, 512:1024].rearrange("p (k n) -> p k n", n=128),
           bcast(vec[:, GCO:GCO + 4], [128, 4, 128]), ALU.mult, reads=["ps5", "vec"], writes=["convT"])

    A.reset(m_ab)
    P.barrier(lambda e: e.memset(bar_t[:, :], 0.0))
    wuv = A.alloc([128, 2, 512], BF, "wuv")
    QL = A.alloc([128, 2, NH, 128], BF, "QL")
    QR = A.alloc([128, NH, 128], BF, "QR")
    m_b = A.mark()
    wuq = A.alloc([128, 3, 768], BF, "wuq")
    wuqs = A.alloc([128, 3, 768], BF, "wuqs")
    wk = A.alloc([128, 3, 768], BF, "wk")
    wukT = A.alloc([64, NH, DC], BF, "wukT")
    Khl = [A.alloc([128, SEQ], BF, "Kh%d" % i) for i in range(2)]
    Vhl = [A.alloc([128, NPT, 65], BF, "Vh%d" % i) for i in range(2)]
    Qh = [A.alloc([128, 512], BF, "Qh%d" % i) for i in range(2)]
    ctl = [A.alloc([128, 512], F32, "ct%d" % i) for i in range(2)]
    snl = [A.alloc([128, 512], F32, "sn%d" % i) for i in range(2)]
    t1l = [A.alloc([128, 512], F32, "t1%d" % i) for i in range(2)]
    t2l = [A.alloc([128, 512], F32, "t2%d" % i) for i in range(2)]
    PT = [A.alloc([128, 512], BF, "PT%d" % i) for i in range(4)]
    Osbl = [A.alloc([128, 512], F32, "Osb%d" % i) for i in range(2)]
    rl4l = [A.alloc([128, 4], F32, "rl4%d" % i) for i in range(2)]

    dma("pool", wuq[:, :, :], w_uq, writes=["wuq"])
    dma("pool", wuqs[:, :, :], w_uqs, writes=["wuqs"])
    dma("pool", wk[:, :, :], w_k, writes=["wk"])
    dma("pool", wuv[:, :, :], w_uv, writes=["wuv"])
    dma("pool", wukT[:, :, :], w_ukT, writes=["wukT"])
    for i in range(2):
        memset("pool", Vhl[i][:, :, 64:65], 1.0, writes=["Vh1_%d" % i])

    def emitKV(h):
        Kh, Vh = Khl[h % 2], Vhl[h % 2]
        kk, vk = "Kh%d" % (h % 2), "Vh%d" % (h % 2)
        for g in range(0, NPT, 8):
            n = min(8, NPT - g)
            for i in range(n):
                kt = g + i
                for k in range(2):
                    mm(ps[1][:, 64 * i:64 * i + 64], kvT[:, k, 128 * kt:128 * kt + 128], wuv[:, k, 64 * h:64 * h + 64],
                       k == 0, k == 1, reads=["kvT", "wuv"], writes=["ps1"], sig=(k == 1 and i == n - 1))
            cp("act", Vh[:, g:g + n, 0:64], ps[1][:, 0:64 * n].rearrange("p (i v) -> p i v", v=64),
               reads=["ps1"], writes=[vk])
        for st in range(NST):
            for k in range(3):
                lo = 64 if k == 2 else 0
                hi = 96 if k == 2 else 128
                mm(ps[1][:96, :], wk[lo:hi, k, 96 * h:96 * h + 96], kvT[lo:hi, k, 512 * st:512 * st + 512],
                   k == 0, k == 2, reads=["kvT", "wk"], writes=["ps1"])
            cp("dve", Kh[:96, 512 * st:512 * st + 512], ps[1][:96, :], reads=["ps1"], writes=[kk])

    units = [(h, qs) for h in range(NH) for qs in range(NST + 1)]

    def emitQ(ui):
        h, qs = units[ui]
        samp = (qs == NST)
        T = 128 if samp else 512
        q0g = 512 * qs
        u = ui % 2
        ct, sn, Q, t1, t2 = ctl[u], snl[u], Qh[u], t1l[u], t2l[u]
        dma("sp", ct[64:96, 0:T], cosT[:, q0g:q0g + T], writes=["ct%d" % u])
        dma("sp", sn[64:96, 0:T], sinT[:, q0g:q0g + T], writes=["sn%d" % u])
        for k in range(3):
            mm(ps[2][:96, 0:T], wuq[:, k, 96 * h:96 * h + 96], cqnT[:, k, q0g:q0g + T], k == 0, k == 2,
               reads=["cqnT", "wuq"], writes=["ps2"])
        for k in range(3):
            mm(ps[3][:96, 0:T], wuqs[:, k, 96 * h:96 * h + 96], cqnT[:, k, q0g:q0g + T], k == 0, k == 2,
               reads=["cqnT", "wuqs"], writes=["ps3"])
        qk = "Q%d" % u
        cp("act", Q[0:64, 0:T], ps[2][0:64, 0:T], reads=["ps2"], writes=[qk])
        tt("dve", t1[64:96, 0:T], ps[2][64:96, 0:T], ct[64:96, 0:T], ALU.mult, reads=["ps2", "ct%d" % u], writes=["t1%d" % u])
        tt("dve", t2[64:96, 0:T], ps[3][64:96, 0:T], sn[64:96, 0:T], ALU.mult, reads=["ps3", "sn%d" % u], writes=["t2%d" % u])
        tt("pool", Q[64:96, 0:T], t1[64:96, 0:T], t2[64:96, 0:T], ALU.add, reads=["t1%d" % u, "t2%d" % u], writes=[qk])

    emitKV(0)
    emitQ(0)
    gkt = 0
    for ui, (h, qs) in enumerate(units):
        samp = (qs == NST)
        u = ui % 2
        Q = Qh[u]
        qk = "Q%d" % u
        Kh, Vh = Khl[h % 2], Vhl[h % 2]
        kk, vk, v1k = "Kh%d" % (h % 2), "Vh%d" % (h % 2), "Vh1_%d" % (h % 2)
        if samp:
            for c in range(2):
                mm(ps[1][:, 128 * c:128 * c + 128], wukT[:, h, 128 * c:128 * c + 128], Q[0:64, 0:128], True, True,
                   reads=["wukT", qk], writes=["ps1"], sig=(c == 1))
            cp("act", QL[:, :, h, :], ps[1][:, 0:256].rearrange("p (c n) -> p c n", n=128), reads=["ps1"], writes=["QL"])
            cp("pool", QR[64:96, h, :], Q[64:96, 0:128], reads=[qk], writes=["QR"])
            if ui + 1 < len(units):
                emitQ(ui + 1)
            continue
        nkt = 4 * qs + 4
        ob = 6 if (ui % 2 == 0) else 0
        obk = "ps%d" % ob

        def stS(kt):
            j = kt - 4 * qs
            q0 = 128 * j if j > 0 else 0
            sb = 4 + ((gkt + kt) % 2)
            sl = (gkt + kt) % 4
            mm(ps[sb][:, q0:512], Kh[:96, 128 * kt:128 * kt + 128], Q[:96, q0:512], True, True,
               reads=[kk, qk], writes=["ps%d" % sb])
            act(PT[sl][:, q0:512], ps[sb][:, q0:512], AF.Exp, reads=["ps%d" % sb], writes=["PT%d" % sl], scale=SCALE)
            if j >= 0:
                tt("pool", PT[sl][:, q0:q0 + 128], PT[sl][:, q0:q0 + 128], tri[:, :], ALU.mult,
                   reads=["PT%d" % sl, "tri"], writes=["PT%d" % sl])

        def stPV(kt):
            j = kt - 4 * qs
            q0 = 128 * j if j > 0 else 0
            sl = (gkt + kt) % 4
            mm(ps[ob][:65, q0:512], Vh[:, kt, 0:65], PT[sl][:, q0:512], kt == 0, kt == nkt - 1,
               reads=[vk, v1k, "PT%d" % sl], writes=[obk], sig=True)

        stS(0)
        for kt in range(nkt):
            if kt + 1 < nkt:
                stS(kt + 1)
            if kt == 1:
                if ui + 1 < len(units):
                    emitQ(ui + 1)
                if qs == 0 and h + 1 < NH:
                    emitKV(h + 1)
            stPV(kt)
        gkt += nkt
        Osb, rl4 = Osbl[ui % 2], rl4l[ui % 2]
        ok_, rk_ = "Osb%d" % (ui % 2), "rl4%d" % (ui % 2)
        cp("act", Osb[:65, :], ps[ob][:65, :], reads=[obk], writes=[ok_])
        for jj in range(4):
            tr(ps[7][:, 65 * jj:65 * jj + 65], Osb[:65, 128 * jj:128 * jj + 128], identf[:65, :65],
               reads=[ok_, "identf"], writes=["ps7"], sig=(jj == 3))
        ov = ps[7][:, 0:260].rearrange("p (j v) -> p j v", v=65)
        recip(rl4[:, :], ov[:, :, 64], reads=["ps7"], writes=[rk_])
        tt("dve", attn_tok[:, 4 * qs:4 * qs + 4, 64 * h:64 * h + 64], ov[:, :, 0:64],
           bcast(rl4[:, :], [128, 4, 64]), ALU.mult, reads=["ps7", rk_], writes=["attn_tok"])

    if cfg.get("STOP") == "B":
        P.emit(nc)
        return nc
    A.reset(m_b)
    P.barrier(lambda e: e.memset(bar_t[:, :], 0.0))
    NSL = 8
    NKS = 4
    pg = [A.alloc([128, 289], BF, "pg%d" % i) for i in range(NSL)]
    KT = [A.alloc([128, 384], BF, "KT%d" % i) for i in range(NKS)]
    PTs = [A.alloc([128, 64], BF, "PTs%d" % i) for i in range(NKS)]
    ptb = A.alloc([128, NPAGES], I32, "ptb")
    idx = A.alloc([128, NPAGES], I32, "idx")
    iot = A.alloc([128, 1], I32, "iota")
    smask = A.alloc([128, SPC * 64], BF, "smask")
    olatl = [A.alloc([64, 256], BF, "olat%d" % i) for i in range(2)]
    rlsl = [A.alloc([64, 1], F32, "rls%d" % i) for i in range(2)]
    OLT = A.alloc([128, 2, NH, 128], BF, "OLT")
    dma("sp", ptb[:, :], ptab.partition_broadcast(128), writes=["ptb"])
    dma("sp", smask[:, :], smask_d, writes=["smask"])
    P.op("pool", lambda e: e.iota(iot[:, :], pattern=[[0, 1]], base=0, channel_multiplier=1), writes=["iota"])
    ts("pool", idx[:, :], ptb[:, :], PAGE, iot[:, 0:1], ALU.mult, ALU.add, reads=["ptb", "iota"], writes=["idx"])
    for i in range(NSL):
        memset("pool", pg[i][:, 288:289], 1.0, writes=["pg1_%d" % i])
    pages = []
    gi = 0
    for s in range(SPC):
        for p in range(NPG + 1):
            new = (p == NPG)
            pages.append(dict(s=s, p=p, new=new, gi=(None if new else gi), fi=(None if new else s * NPG + p)))
            if not new:
                gi += 1
    NPGS = len(pages)
    real = [pgd for pgd in pages if not pgd["new"]]

    def stG(g):
        if g >= len(real):
            return
        pgd = real[g]
        gs = g % NSL
        pgt = pg[gs]
        fi = pgd["fi"]

        def gat(e, pgt=pgt, fi=fi):
            return e.indirect_dma_start(out=pgt[:, 0:288], out_offset=None, in_=cache,
                                        in_offset=bass.IndirectOffsetOnAxis(ap=idx[:, fi:fi + 1], axis=0))
        P.op("pool", gat, reads=["idx"], writes=["pg%d" % gs], dma=True)

    def stT(i):
        if i >= NPGS or pages[i]["new"]:
            return
        g = pages[i]["gi"]
        gs = g % NSL
        pgk = "pg%d" % gs
        pgt = pg[gs]
        kb = g % 3
        ksl = g % NKS
        tr(psb(kb)[:, 0:128], pgt[:, 0:128], identb[:, :], reads=[pgk, "identb"], writes=["ps%d" % kb], sig=False)
        tr(psb(kb)[:, 128:256], pgt[:, 128:256], identb[:, :], reads=[pgk, "identb"], writes=["ps%d" % kb], sig=False)
        tr(psb(kb)[:96, 256:384], pgt[:, 192:288], identb[:, :], reads=[pgk, "identb"], writes=["ps%d" % kb])
        cp("dve", KT[ksl][:, 0:384], psb(kb)[:, 0:384], reads=["ps%d" % kb], writes=["KT%d" % ksl])

    def stQK(i):
        if i >= NPGS:
            return
        pgd = pages[i]
        s = pgd["s"]
        if not pgd["new"]:
            ksl = pgd["gi"] % NKS
            k0, k1, k2 = KT[ksl][:, 0:128], KT[ksl][:, 128:256], KT[ksl][64:96, 256:384]
            kreads = ["KT%d" % ksl]
        else:
            c0 = NPT * 128
            k0, k1, k2 = kvT[:, 0, c0:c0 + 128], kvT[:, 1, c0:c0 + 128], kvT[64:96, 2, c0:c0 + 128]
            kreads = ["kvT"]
        sbk = 3 + (i % 2)
        so = 64 * ((i // 2) % 2)
        pso = ps[sbk][:, so:so + 64]
        psk = "ps%d_%d" % (sbk, so)
        psl = i % NKS
        mm(pso, k0, QL[:, 0, :, 8 * s:8 * s + 8], True, False, reads=kreads + ["QL"], writes=[psk])
        mm(pso, k1, QL[:, 1, :, 8 * s:8 * s + 8], False, False, reads=kreads + ["QL"], writes=[psk])
        mm(pso, k2, QR[64:96, :, 8 * s:8 * s + 8], False, True, reads=kreads + ["QR"], writes=[psk])
        act(PTs[psl][:, :], pso, AF.Exp, reads=[psk], writes=["PTs%d" % psl], scale=SCALE)
        if pgd["new"]:
            tt("pool", PTs[psl][:, :], PTs[psl][:, :], smask[:, 64 * s:64 * s + 64], ALU.mult,
               reads=["PTs%d" % psl, "smask"], writes=["PTs%d" % psl])

    def stPV(i):
        pgd = pages[i]
        s = pgd["s"]
        psl = i % NKS
        ob = 5 if s % 2 == 0 else 7
        if not pgd["new"]:
            gs = pgd["gi"] % NSL
            vrhs = pg[gs][:, 0:289]
            vreads = ["pg%d" % gs, "pg1_%d" % gs]
        else:
            vrhs = kvbf_s[:, 0:289]
            vreads = ["kvbf_s", "kvbf_s1"]
        mm(ps[ob][:64, 0:289], PTs[psl][:, :], vrhs, pgd["p"] == 0, pgd["new"], reads=["PTs%d" % psl] + vreads,
           writes=["ps%d" % ob], sig=True)
        if pgd["new"]:
            olat, rls = olatl[s % 2], rlsl[s % 2]
            olk, rlk = "olat%d" % (s % 2), "rls%d" % (s % 2)
            recip(rls[:, :], ps[ob][:64, 288:289], reads=["ps%d" % ob], writes=[rlk])
            ts("dve", olat[:, :], ps[ob][:64, 0:256], rls[:, 0:1], None, ALU.mult, reads=["ps%d" % ob, rlk], writes=[olk])
            for c in range(2):
                tr(psb(6)[:, 64 * c:64 * c + 64], olat[:, 128 * c:128 * c + 128], identb[:64, :64],
                   reads=[olk, "identb"], writes=["ps6"], sig=(c == 1))
            cp("act", OLT[:, :, :, 8 * s:8 * s + 8], psb(6)[:, 0:128].rearrange("p (c h j) -> p c h j", c=2, h=NH),
               reads=["ps6"], writes=["OLT"])

    GD = 5
    for g in range(GD):
        stG(g)
    stT(0)
    stT(1)
    stQK(0)
    for i in range(NPGS):
        if not pages[i]["new"]:
            stG(pages[i]["gi"] + GD)
        stT(i + 2)
        stQK(i + 1)
        stPV(i)
    for h in range(NH):
        for c in range(2):
            mm(ps[1][:, 64 * h:64 * h + 64], OLT[:, c, h, :], wuv[:, c, 64 * h:64 * h + 64], c == 0, c == 1,
               reads=["OLT", "wuv"], writes=["ps1"], sig=(c == 1 and h == NH - 1))
    cp("act", attn_tok[:, NPT, :], ps[1][:, :], reads=["ps1"], writes=["attn_tok"])

    if cfg.get("STOP") == "S":
        P.emit(nc)
        return nc
    def phaseC_new():
        A.reset(m_persist)
        P.barrier(lambda e: e.memset(bar_t[:, :], 0.0))
        guardC = []
        wdn = A.alloc([128, NFC, D], BF, "wdn")
        wst = [A.alloc([128, 8, 256], BF, "wst%d" % i) for i in range(2)]
        x1 = A.alloc([128, 4, D], F32, "x1")
        x1_off = A.last
        h2T = A.alloc([128, 8, 512], BF, "h2T")
        actT = A.alloc([128, NFC, 512], BF, "actT")
        ugl = [A.alloc([128, 640], F32, "ug%d" % i) for i in range(2)]
        wos = A.at([128, 8, D], BF, "wos", A.last - 2560)
        uvl = [A.alloc([128, 640], F32, "uv%d" % i) for i in range(2)]
        cgl = [A.alloc([128, 512], F32, "cg%d" % i) for i in range(2)]
        cvl = [A.alloc([128, 512], F32, "cv%d" % i) for i in range(2)]
        xt2l = [A.alloc([128, D], F32, "xt2%d" % i) for i in range(1)]
        x1n = A.alloc([128, D], BF, "x1n")
        aT = A.alloc([128, 4, 128], BF, "aT")
        carry = A.alloc([128, 44, 2], F32, "carry")
        carry_s = A.at([128, 44, SPC, 2], F32, "carry_s", x1_off + 4096)
        wft = A.alloc([128, 44, 4], F32, "wft")
        gfb = A.alloc([128, D], F32, "gfb")
        sm2 = A.alloc([128, 16], F32, "sm2")
        stfl = [A.alloc([32, 512], F32, "stf%d" % i) for i in range(2)]
        x2l = [A.alloc([128, D], F32, "x2%d" % i) for i in range(2)]

        WOSK = ["ug0", "ug1", "uv0", "uv1", "cg0", "cg1", "cv0", "cv1"]
        NCH = (NST + 1) * NFC

        def emit_wst(g):
            if g < NCH:
                dma("pool", wst[g % 2][:, :, :], w_up[g % NFC], writes=["wst%d" % (g % 2)])
        dma("pool", wos[:, :, :], w_o, writes=WOSK)
        dma("pool", wdn[:, :, :], w_dn, writes=["wdn"])
        emit_wst(0)
        dma("sp", wft[:, :, :], wf, writes=["wft"])
        dma("sp", gfb[:, :], rowv[:, 1280:2304].partition_broadcast(128), writes=["gfb"])
        memset("pool", carry[:, :, :], 0.0, writes=["carry"])
        s2 = lambda i: sm2[:, i:i + 1]
        gch = 0
        gtl = 0

        for st in range(NST + 1):
            samp = (st == NST)
            ntile = 1 if samp else 4
            T = 128 * ntile
            if st > 0:
                dma("pool", wos[:, :, :], w_o, writes=WOSK)
            if samp:
                for g in range(11):
                    sb_ = stfl[g % 2]
                    sk_ = "stf%d" % (g % 2)
                    dma("sp", sb_[:, :], stf[:, 512 * g:512 * g + 512], writes=[sk_])
                    for i in range(4):
                        tr(ps[0][:, 32 * i:32 * i + 32], sb_[:, 128 * i:128 * i + 128], identf[:32, :32],
                           reads=[sk_, "identf"], writes=["ps0"], sig=(i == 3))
                    cp("dve", carry_s[:, 4 * g:4 * g + 4, :, :].rearrange("p c s j -> p c (s j)"),
                       ps[0][:, 0:128].rearrange("p (c n) -> p c n", n=32), reads=["ps0"], writes=["carry_s", "x1_1", "x1_2"])
            for j in range(ntile):
                t = 4 * st + j
                xt2 = xt2l[0]
                xk2 = "xt20"
                gtl += 1
                dma("sp", xt2[:, :], (xs if samp else xp[128 * t:128 * t + 128, :]), writes=[xk2])
                act(x1n[:, 0:512], attn_tok[:, t, :], AF.Square, reads=["attn_tok"], writes=["x1n", "sa_ss"], accum_out=s2(0))
                rstd_from_ss(s2(0), 512, s2(1), s2(2), "sa_")
                for c in range(4):
                    tr(psb(0)[:, 128 * c:128 * c + 128], attn_tok[:, t, 128 * c:128 * c + 128], identb[:, :],
                       reads=["attn_tok", "identb"], writes=["ps0"], sig=(c == 3))
                tt("dve", aT[:, :, :], psb(0)[:, 0:512].rearrange("p (k n) -> p k n", n=128),
                   bcast(vec[:, GAO:GAO + 4], [128, 4, 128]), ALU.mult, reads=["ps0", "vec"], writes=["aT"])
                for n in range(2):
                    for c in range(4):
                        mm(ps[1 + n][:, :], aT[:, c, :], wos[:, c, 512 * n:512 * n + 512], c == 0, c == 3,
                           reads=["aT"] + WOSK, writes=["ps%d" % (1 + n)])
                for n in range(2):
                    for c in range(4):
                        mm(ps[3 + n][:, :], convT[:, c, 128 * t:128 * t + 128], wos[:, 4 + c, 512 * n:512 * n + 512], c == 0, c == 3,
                           reads=["convT"] + WOSK, writes=["ps%d" % (3 + n)])
                for n in range(2):
                    stt("dve", x1[:, j, 512 * n:512 * n + 512], ps[1 + n][:, :], s2(2), xt2[:, 512 * n:512 * n + 512], ALU.mult, ALU.add,
                        reads=["ps%d" % (1 + n), "sa_", xk2], writes=["x1_%d" % j])
                    stt("dve", x1[:, j, 512 * n:512 * n + 512], ps[3 + n][:, :], rstd_c[:, t:t + 1], x1[:, j, 512 * n:512 * n + 512],
                        ALU.mult, ALU.add, reads=["ps%d" % (3 + n), "ss3", "x1_%d" % j], writes=["x1_%d" % j])
                act(x1n[:, :], x1[:, j, :], AF.Square, reads=["x1_%d" % j], writes=["x1n", "sf_ss"], accum_out=s2(3))
                rstd_from_ss(s2(3), D, s2(4), s2(5), "sf_")
                act(x1n[:, :], x1[:, j, :], AF.Copy, reads=["x1_%d" % j, "sf_"], writes=["x1n"], scale=s2(5))
                for k in range(8):
                    tr(psb(5)[:, 128 * k:128 * k + 128], x1n[:, 128 * k:128 * k + 128], identb[:, :],
                       reads=["x1n", "identb"], writes=["ps5"], sig=(k == 7))
                tt("dve", h2T[:, :, 128 * j:128 * j + 128], psb(5)[:, :].rearrange("p (k n) -> p k n", n=128),
                   bcast(vec[:, GF:GF + 8], [128, 8, 128]), ALU.mult, reads=["ps5", "vec"], writes=["h2T"])
            for i in range(NFC):
                par = gch % 2
                gch += 1
                wkk = "wst%d" % par
                emit_wst(gch)
                banks = (6, 7) if (par == 0 or cfg.get("C_BANKS") == "same") else (0, 5)
                taps = []
                for half in range(2):
                    bank = banks[half]
                    ub = (ugl, uvl)[half][par]
                    cb = (cgl, cvl)[half][par]
                    ubk = ("ug%d", "uv%d")[half] % par
                    cbk = ("cg%d", "cv%d")[half] % par
                    ch = i + NFC * half
                    for k in range(8):
                        mm(ps[bank][:, 0:T], wst[par][:, k, 128 * half:128 * half + 128], h2T[:, k, 0:T], k == 0, k == 7,
                           reads=[wkk, "h2T"], writes=["ps%d" % bank])
                    if not samp:
                        cp("pool", ub[:, 0:2], carry[:, ch, :], reads=["carry"], writes=[ubk])
                        act(ub[:, 2:2 + T], ps[bank][:, 0:T], AF.Copy, reads=["ps%d" % bank], writes=[ubk])
                        cp("pool", carry[:, ch, :], ub[:, T:T + 2], reads=[ubk], writes=["carry"])
                        v0, v1 = ub[:, 0:T], ub[:, 1:1 + T]
                        co = cb[:, 0:T]
                        act(co, ps[bank][:, 0:T], AF.Identity, reads=["ps%d" % bank, "wft"], writes=[cbk],
                            scale=wft[:, ch, 2:3], bias=wft[:, ch, 3:4])
                    else:
                        u3 = ub[:, 0:160].rearrange("p (s j) -> p s j", j=10)
                        cp("pool", u3[:, :, 0:2], carry_s[:, ch, :, :], reads=["carry_s"], writes=[ubk])
                        act(u3[:, :, 2:10], ps[bank][:, 0:128].rearrange("p (s j) -> p s j", j=8), AF.Copy,
                            reads=["ps%d" % bank], writes=[ubk])
                        cp("pool", carry_s[:, ch, :, :], u3[:, :, 8:10], reads=[ubk], writes=["carry_s"])
                        v0, v1 = u3[:, :, 0:8], u3[:, :, 1:9]
                        co = cb[:, 0:128].rearrange("p (s j) -> p s j", j=8)
                        act(co, ps[bank][:, 0:128].rearrange("p (s j) -> p s j", j=8), AF.Identity,
                            reads=["ps%d" % bank, "wft"], writes=[cbk], scale=wft[:, ch, 2:3], bias=wft[:, ch, 3:4])
                    taps.append((co, v0, v1, ch, ubk, cbk))
                for kk_ in (1, 0):
                    for (co, v0, v1, ch, ubk, cbk) in taps:
                        stt("dve", co, (v1 if kk_ == 1 else v0), wft[:, ch, kk_:kk_ + 1], co, ALU.mult, ALU.add,
                            reads=[ubk, "wft", cbk], writes=[cbk])
                cgb, cvb = cgl[par], cvl[par]
                act(cgb[:, 0:T], cgb[:, 0:T], AF.Silu, reads=["cg%d" % par], writes=["cg%d" % par])
                tt(cfg.get("C_TTENG", "dve"), actT[:, i, 0:T], cgb[:, 0:T], cvb[:, 0:T], ALU.mult, reads=["cg%d" % par, "cv%d" % par], writes=["actT"])
            if st == NST - 1:
                for g in range(11):
                    for i in range(4):
                        ch = 4 * g + i
                        tr(ps[0][:2, 128 * i:128 * i + 128], carry[:, ch, :], identf[:, :],
                           reads=["carry", "identf"], writes=["ps0"], sig=(i == 3))
                    cp("act", stfl[g % 2][:2, :], ps[0][:2, :], reads=["ps0"], writes=["stf%d" % (g % 2)])
                    dma("sp", ffn_p[:, 512 * g:512 * g + 512], stfl[g % 2][:2, :], reads=["stf%d" % (g % 2)])
            if samp:
                for g in range(11):
                    for i in range(4):
                        ch = 4 * g + i
                        tr(ps[0][:32, 128 * i:128 * i + 128], carry_s[:, ch, :, :].rearrange("p s j -> p (s j)"), identf[:, :],
                           reads=["carry_s", "identf"], writes=["ps0"], sig=(i == 3))
                    cp("act", stfl[g % 2][:32, :], ps[0][:32, :], reads=["ps0"], writes=["stf%d" % (g % 2)])
                    dma("sp", ffn_s[:, 512 * g:512 * g + 512], stfl[g % 2][:32, :], reads=["stf%d" % (g % 2)])
            for j in range(ntile):
                t = 4 * st + j
                x2 = x2l[j % 2]
                x2k = "x2%d" % (j % 2)
                pb = (1, 2) if j % 2 == 0 else (3, 4)
                for n in range(2):
                    for i in range(NFC):
                        mm(ps[pb[n]][:, :], actT[:, i, 128 * j:128 * j + 128], wdn[:, i, 512 * n:512 * n + 512], i == 0, i == NFC - 1,
                           reads=["actT", "wdn"], writes=["ps%d" % pb[n]])
                for n in range(2):
                    tt("dve", x2[:, 512 * n:512 * n + 512], ps[pb[n]][:, :], x1[:, j, 512 * n:512 * n + 512], ALU.add,
                       reads=["ps%d" % pb[n], "x1_%d" % j], writes=[x2k])
                act(x1n[:, :], x2[:, :], AF.Square, reads=[x2k], writes=["x1n", "sy_ss"], accum_out=s2(6))
                rstd_from_ss(s2(6), D, s2(7), s2(8), "sy_")
                stt("dve", x2[:, :], x2[:, :], s2(8), gfb[:, :], ALU.mult, ALU.mult, reads=[x2k, "sy_", "gfb"], writes=[x2k])
                dma("sp", (y_s if samp else y_p[128 * t:128 * t + 128, :]), x2[:, :], reads=[x2k])


    def phaseC_old():
        A.reset(m_persist)
        P.barrier(lambda e: e.memset(bar_t[:, :], 0.0))
        guardC = []
        wdn = A.alloc([128, NFC, D], BF, "wdn")
        wst = [A.alloc([128, 8, 256], BF, "wst%d" % i) for i in range(2)]
        x1 = A.alloc([128, 4, D], F32, "x1")
        x1_off = A.last
        h2T = A.alloc([128, 8, 512], BF, "h2T")
        actT = A.alloc([128, NFC, 512], BF, "actT")
        wos = A.at([128, 8, D], BF, "wos", A.last)
        ug = A.alloc([128, 640], F32, "ug")
        uv = A.alloc([128, 640], F32, "uv")
        cg = A.alloc([128, 512], F32, "cg")
        cv = A.alloc([128, 512], F32, "cv")
        sg = A.alloc([128, 512], F32, "sg")
        xt2 = A.alloc([128, D], F32, "xt2")
        x1n = A.alloc([128, D], BF, "x1n")
        aT = A.alloc([128, 4, 128], BF, "aT")
        junk2 = A.alloc([128, D], BF, "junk2")
        carry = A.alloc([128, 44, 2], F32, "carry")
        carry_s = A.at([128, 44, SPC, 2], F32, "carry_s", x1_off + 4096)
        wft = A.alloc([128, 44, 4], F32, "wft")
        gfb = A.alloc([128, D], F32, "gfb")
        sm2 = A.alloc([128, 16], F32, "sm2")
        stfl = [A.alloc([32, 512], F32, "stf%d" % i) for i in range(2)]
        x2 = A.alloc([128, D], F32, "x2")
        assert A.mark() < 229000, A.mark()

        dma("pool", wdn[:, :, :], w_dn, reads=guardC, writes=["wdn"])
        dma("sp", wft[:, :, :], wf, reads=guardC, writes=["wft"])
        dma("sp", gfb[:, :], rowv[:, 1280:2304].partition_broadcast(128), reads=guardC, writes=["gfb"])
        memset("pool", carry[:, :, :], 0.0, writes=["carry"])
        s2 = lambda i: sm2[:, i:i + 1]
        GC = guardC

        for st in range(NST + 1):
            samp = (st == NST)
            ntile = 1 if samp else 4
            T = 128 * ntile
            dma("pool", wos[:, :, :], w_o, reads=GC, writes=["actT"])
            if samp:
                for g in range(11):
                    sb_ = stfl[g % 2]
                    sk_ = "stf%d" % (g % 2)
                    dma("sp", sb_[:, :], stf[:, 512 * g:512 * g + 512], reads=GC, writes=[sk_])
                    for i in range(4):
                        tr(ps[0][:, 32 * i:32 * i + 32], sb_[:, 128 * i:128 * i + 128], identf[:32, :32],
                           reads=[sk_, "identf"], writes=["ps0"], sig=(i == 3))
                    cp("dve", carry_s[:, 4 * g:4 * g + 4, :, :].rearrange("p c s j -> p c (s j)"),
                       ps[0][:, 0:128].rearrange("p (c n) -> p c n", n=32), reads=["ps0"] + GC, writes=["carry_s", "x1_1", "x1_2"])
            for j in range(ntile):
                t = 4 * st + j
                dma("sp", xt2[:, :], (xs if samp else xp[128 * t:128 * t + 128, :]), reads=GC, writes=["xt2"])
                act(junk2[:, 0:512], attn_tok[:, t, :], AF.Square, reads=["attn_tok"] + GC, writes=["junk2", "sa_ss"], accum_out=s2(0))
                rstd_from_ss(s2(0), 512, s2(1), s2(2), "sa_")
                for c in range(4):
                    tr(psb(0)[:, 128 * c:128 * c + 128], attn_tok[:, t, 128 * c:128 * c + 128], identb[:, :],
                       reads=["attn_tok", "identb"], writes=["ps0"], sig=(c == 3))
                tt("dve", aT[:, :, :], psb(0)[:, 0:512].rearrange("p (k n) -> p k n", n=128),
                   bcast(vec[:, GAO:GAO + 4], [128, 4, 128]), ALU.mult, reads=["ps0", "vec"] + GC, writes=["aT"])
                for n in range(2):
                    for c in range(4):
                        mm(ps[1 + n][:, :], aT[:, c, :], wos[:, c, 512 * n:512 * n + 512], c == 0, c == 3,
                           reads=["aT", "actT"], writes=["ps%d" % (1 + n)])
                for n in range(2):
                    for c in range(4):
                        mm(ps[3 + n][:, :], convT[:, c, 128 * t:128 * t + 128], wos[:, 4 + c, 512 * n:512 * n + 512], c == 0, c == 3,
                           reads=["convT", "actT"], writes=["ps%d" % (3 + n)])
                for n in range(2):
                    stt("dve", x1[:, j, 512 * n:512 * n + 512], ps[1 + n][:, :], s2(2), xt2[:, 512 * n:512 * n + 512], ALU.mult, ALU.add,
                        reads=["ps%d" % (1 + n), "sa_", "xt2"] + GC, writes=["x1_%d" % j])
                    stt("dve", x1[:, j, 512 * n:512 * n + 512], ps[3 + n][:, :], rstd_c[:, t:t + 1], x1[:, j, 512 * n:512 * n + 512],
                        ALU.mult, ALU.add, reads=["ps%d" % (3 + n), "ss3", "x1_%d" % j], writes=["x1_%d" % j])
                act(junk2[:, :], x1[:, j, :], AF.Square, reads=["x1_%d" % j], writes=["junk2", "sf_ss"], accum_out=s2(3))
                rstd_from_ss(s2(3), D, s2(4), s2(5), "sf_")
                act(x1n[:, :], x1[:, j, :], AF.Copy, reads=["x1_%d" % j, "sf_"], writes=["x1n"], scale=s2(5))
                for k in range(8):
                    tr(psb(5)[:, 128 * k:128 * k + 128], x1n[:, 128 * k:128 * k + 128], identb[:, :],
                       reads=["x1n", "identb"], writes=["ps5"], sig=(k == 7))
                tt("dve", h2T[:, :, 128 * j:128 * j + 128], psb(5)[:, :].rearrange("p (k n) -> p k n", n=128),
                   bcast(vec[:, GF:GF + 8], [128, 8, 128]), ALU.mult, reads=["ps5", "vec"] + GC, writes=["h2T"])
            for i in range(NFC):
                wsl = (st * NFC + i) % 2
                wkk = "wst%d" % wsl
                dma("pool", wst[wsl][:, :, :], w_up[i], reads=GC, writes=[wkk])
                for half, (bank, ub, ubk, cb, cbk, eng) in enumerate(((6, ug, "ug", cg, "cg", "dve"), (7, uv, "uv", cv, "cv", "pool"))):
                    ch = i + NFC * half
                    for k in range(8):
                        mm(ps[bank][:, 0:T], wst[wsl][:, k, 128 * half:128 * half + 128], h2T[:, k, 0:T], k == 0, k == 7,
                           reads=[wkk, "h2T"], writes=["ps%d" % bank])
                    if not samp:
                        cp("pool", ub[:, 0:2], carry[:, ch, :], reads=["carry", cbk] + GC, writes=[ubk])
                        act(ub[:, 2:2 + T], ps[bank][:, 0:T], AF.Copy, reads=["ps%d" % bank, cbk] + GC, writes=[ubk])
                        cp("pool", carry[:, ch, :], ub[:, T:T + 2], reads=[ubk], writes=["carry"])
                        v0, v1, v2 = ub[:, 0:T], ub[:, 1:1 + T], ub[:, 2:2 + T]
                        co = cb[:, 0:T]
                    else:
                        u3 = ub[:, 0:160].rearrange("p (s j) -> p s j", j=10)
                        cp("pool", u3[:, :, 0:2], carry_s[:, ch, :, :], reads=["carry_s", cbk] + GC, writes=[ubk])
                        act(u3[:, :, 2:10], ps[bank][:, 0:128].rearrange("p (s j) -> p s j", j=8), AF.Copy,
                            reads=["ps%d" % bank, cbk] + GC, writes=[ubk])
                        cp("pool", carry_s[:, ch, :, :], u3[:, :, 8:10], reads=[ubk], writes=["carry_s"])
                        v0, v1, v2 = u3[:, :, 0:8], u3[:, :, 1:9], u3[:, :, 2:10]
                        co = cb[:, 0:128].rearrange("p (s j) -> p s j", j=8)
                    ts(eng, co, v2, wft[:, ch, 2:3], wft[:, ch, 3:4], ALU.mult, ALU.add, reads=[ubk, "wft"], writes=[cbk])
                    stt(eng, co, v1, wft[:, ch, 1:2], co, ALU.mult, ALU.add, reads=[ubk, "wft", cbk], writes=[cbk])
                    stt(eng, co, v0, wft[:, ch, 0:1], co, ALU.mult, ALU.add, reads=[ubk, "wft", cbk], writes=[cbk])
                act(sg[:, 0:T], cg[:, 0:T], AF.Silu, reads=["cg"] + GC, writes=["sg"])
                tt("dve", actT[:, i, 0:T], sg[:, 0:T], cv[:, 0:T], ALU.mult, reads=["sg", "cv"] + GC, writes=["actT"])
            if st == NST - 1:
                for g in range(11):
                    for i in range(4):
                        ch = 4 * g + i
                        tr(ps[0][:2, 128 * i:128 * i + 128], carry[:, ch, :], identf[:, :],
                           reads=["carry", "identf"], writes=["ps0"], sig=(i == 3))
                    cp("act", stfl[g % 2][:2, :], ps[0][:2, :], reads=["ps0"], writes=["stf%d" % (g % 2)])
                    dma("sp", ffn_p[:, 512 * g:512 * g + 512], stfl[g % 2][:2, :], reads=["stf%d" % (g % 2)])
            if samp:
                for g in range(11):
                    for i in range(4):
                        ch = 4 * g + i
                        tr(ps[0][:32, 128 * i:128 * i + 128], carry_s[:, ch, :, :].rearrange("p s j -> p (s j)"), identf[:, :],
                           reads=["carry_s", "identf"], writes=["ps0"], sig=(i == 3))
                    cp("act", stfl[g % 2][:32, :], ps[0][:32, :], reads=["ps0"], writes=["stf%d" % (g % 2)])
                    dma("sp", ffn_s[:, 512 * g:512 * g + 512], stfl[g % 2][:32, :], reads=["stf%d" % (g % 2)])
            for j in range(ntile):
                t = 4 * st + j
                for n in range(2):
                    for i in range(NFC):
                        mm(ps[1 + n][:, :], actT[:, i, 128 * j:128 * j + 128], wdn[:, i, 512 * n:512 * n + 512], i == 0, i == NFC - 1,
                           reads=["actT", "wdn"], writes=["ps%d" % (1 + n)])
                for n in range(2):
                    tt("dve", x2[:, 512 * n:512 * n + 512], ps[1 + n][:, :], x1[:, j, 512 * n:512 * n + 512], ALU.add,
                       reads=["ps%d" % (1 + n), "x1_%d" % j] + GC, writes=["x2"])
                act(junk2[:, :], x2[:, :], AF.Square, reads=["x2"], writes=["junk2", "sy_ss"], accum_out=s2(6))
                rstd_from_ss(s2(6), D, s2(7), s2(8), "sy_")
                stt("dve", x2[:, :], x2[:, :], s2(8), gfb[:, :], ALU.mult, ALU.mult, reads=["x2", "sy_", "gfb"], writes=["x2"])
                dma("sp", (y_s if samp else y_p[128 * t:128 * t + 128, :]), x2[:, :], reads=["x2"])


    if cfg.get("CMODE", "new") == "new":
        phaseC_new()
    else:
        phaseC_old()
    P.emit(nc)
    return nc


def _bf(a):
    return np.ascontiguousarray(a).astype(ml_dtypes.bfloat16)


def host_consts(cfg):
    SEQ, PAST = cfg["SEQ"], cfg["PAST"]
    NT = SEQ + 128
    pos = np.concatenate([np.arange(SEQ), PAST + (np.arange(128) % DSEQ)]).astype(np.float32)
    inv = (np.float32(10000.0) ** (-(np.arange(16, dtype=np.float32) * np.float32(2.0) / np.float32(DR)))).astype(np.float32)
    ang = (pos[:, None] * inv[None, :]).astype(np.float32)
    cos = np.cos(ang).astype(np.float32)
    sin = np.sin(ang).astype(np.float32)
    cstok = np.concatenate([cos, sin], axis=1)
    cosT = np.concatenate([cos, cos], axis=1).T.copy()
    sinT = np.concatenate([-sin, sin], axis=1).T.copy()
    tri = (np.arange(128)[:, None] <= np.arange(128)[None, :]).astype(np.float32)
    k = np.arange(128)
    sm = np.zeros((128, SPC, NH, DSEQ), np.float32)
    for s in range(SPC):
        for t in range(DSEQ):
            sm[(k // DSEQ == s) & (k % DSEQ <= t), s, :, t] = 1.0
    return dict(cstok=cstok, cosT=cosT, sinT=sinT, identf=np.eye(128, dtype=np.float32), identb=_bf(np.eye(128)),
                tri=_bf(tri), smask=_bf(sm.reshape(128, SPC * 64)))


def host_weights(inp):
    f = np.float32
    w_in = inp["w_in"][0].reshape(8, 128, DIN).transpose(1, 0, 2)
    wuq = inp["w_uq"][0].reshape(DCQ, NH * 96)
    wuqs = inp["w_uq"][0].copy()
    wuqs[:, :, 64:80] = inp["w_uq"][0][:, :, 80:96]
    wuqs[:, :, 80:96] = inp["w_uq"][0][:, :, 64:80]
    wuqs = wuqs.reshape(DCQ, NH * 96)
    wk = np.zeros((3, 128, NH, 96), f)
    wk[0, :, :, 0:64] = inp["w_uk"][0][0:128]
    wk[1, :, :, 0:64] = inp["w_uk"][0][128:256]
    for h in range(NH):
        wk[2, 64:96, h, 64:96] = np.eye(32, dtype=f)
    vec = np.zeros((128, 64), f)
    vec[:, 0:8] = inp["g_attn_norm"][0].reshape(8, 128).T
    vec[:, 8:11] = inp["g_q_norm"][0].reshape(3, 128).T
    vec[:, 11:15] = inp["g_conv_out"][0].reshape(4, 128).T
    vec[:, 15:19] = inp["g_attn_out"][0].reshape(4, 128).T
    vec[:, 19:27] = inp["g_ffn_norm"][0].reshape(8, 128).T
    vec[:, 27:31] = inp["b_dw"][0].reshape(4, 128).T
    wfv = np.zeros((128, 44, 4), f)
    wfv[:, :, 0:3] = inp["w_ffn_dw"][0].reshape(3, 44, 128).transpose(2, 1, 0)
    wfv[:, :, 3] = inp["b_ffn_dw"][0].reshape(44, 128).T
    rowv = np.zeros((1, 3072), f)
    rowv[0, 0:256] = inp["g_kv_norm"][0]
    rowv[0, 256:768] = inp["g_conv_ln"][0]
    rowv[0, 768:1280] = inp["b_conv_ln"][0]
    rowv[0, 1280:2304] = inp["g_final"]
    wup = inp["w_up"][0].reshape(8, 128, 2, NFC, 128).transpose(3, 1, 0, 2, 4).reshape(NFC, 128, 8, 256)
    c = np.ascontiguousarray
    return dict(
        w_in=c(w_in), w_uq=c(wuq.reshape(3, 128, 768).transpose(1, 0, 2)), w_uqs=c(wuqs.reshape(3, 128, 768).transpose(1, 0, 2)),
        w_k=c(wk.reshape(3, 128, 768).transpose(1, 0, 2)), w_ukT=c(inp["w_uk"][0].transpose(2, 1, 0)),
        w_uv=c(inp["w_uv"][0].reshape(2, 128, 512).transpose(1, 0, 2)),
        w_o=c(inp["w_o"][0].reshape(8, 128, D).transpose(1, 0, 2)), w_up=c(wup),
        w_dn=c(inp["w_down"][0].reshape(NFC, 128, D).transpose(1, 0, 2)),
        wdw=c(inp["w_dw"][0].reshape(CW, 4, 128).transpose(2, 1, 0)), vecs=vec, wf=wfv, rowv=rowv)


_NC_CACHE = {}


def run(inp, cfg):
    SEQ, NPG, NPOOL = cfg["SEQ"], cfg["NPG"], cfg["NPOOL"]
    key = tuple(sorted(cfg.items()))
    if key not in _NC_CACHE:
        _NC_CACHE[key] = build(cfg)
    nc = _NC_CACHE[key]
    consts = host_consts(cfg)
    wts = host_weights(inp)
    cache2d = np.ascontiguousarray(inp["cache_kv_latent"][0]).reshape(NPOOL * PAGE, DC + DR)
    in_maps = []
    for c in range(8):
        m = dict(consts)
        m.update(wts)
        m["xp"] = np.ascontiguousarray(inp["x_prompt"][c])
        m["xs"] = np.ascontiguousarray(inp["x_sample"][SPC * c:SPC * c + SPC]).reshape(128, D)
        m["cache"] = cache2d
        m["stc"] = np.ascontiguousarray(inp["state_conv"][0, SPC * c:SPC * c + SPC]).reshape(SPC * 30, DCONV)
        m["stf"] = np.ascontiguousarray(inp["state_ffn_conv"][0, SPC * c:SPC * c + SPC]).reshape(SPC * 2, 2 * DFF)
        m["ptab"] = np.ascontiguousarray(inp["page_table"][SPC * c:SPC * c + SPC]).reshape(1, SPC * NPG).astype(np.int32)
        in_maps.append(m)
    res = run_bass_kernel_spmd(nc, in_maps, core_ids=list(range(8)))
    R = res.results
    y_p = np.stack([R[c]["y_p"] for c in range(8)])
    y_s = np.concatenate([R[c]["y_s"].reshape(SPC, DSEQ, D) for c in range(8)])
    kv_p = np.stack([R[c]["kv_p"] for c in range(8)])[None]
    conv_p = np.stack([R[c]["conv_p"] for c in range(8)])[None]
    ffn_p = np.stack([R[c]["ffn_p"] for c in range(8)])[None]
    kv_s = np.concatenate([R[c]["kv_s"].reshape(SPC, DSEQ, DC + DR) for c in range(8)])[None]
    conv_s = np.concatenate([R[c]["conv_s"] for c in range(8)])[None]
    ffn_s = np.concatenate([R[c]["ffn_s"].reshape(SPC, 2, 2 * DFF) for c in range(8)])[None]
    return tuple(np.asarray(a, np.float32) for a in (y_p, y_s, kv_p, conv_p, ffn_p, kv_s, conv_s, ffn_s))


def kernel(**inputs):
    inp = {k: np.asarray(v) for k, v in inputs.items()}
    return run(inp, CFG_FULL)
```

```python
import numpy as np
import ml_dtypes
import concourse.bass as bass
import concourse.mybir as mybir
from concourse.bass_utils import run_bass_kernel_spmd

F32 = mybir.dt.float32
BF = mybir.dt.bfloat16
I32 = mybir.dt.int32
AF = mybir.ActivationFunctionType
ALU = mybir.AluOpType

D = 1024
NH = 8
DC = 256
DCQ = 384
DR = 32
DCONV = 512
CW = 31
DIN = 1696
DFF = 2816
NFC = 22
EPS = 1e-6
SCALE = 96.0 ** -0.5
SPC = 16
DSEQ = 8
PAGE = 128

CFG_FULL = dict(SEQ=4096, NPG=128, NPOOL=20480, PAST=16384)


class Prog:
    ENGS = ("pe", "act", "dve", "pool", "sp")
    RING = 8

    def __init__(self):
        self.ops = []
        self.lastw = {}
        self.readers = {}
        self.cnt = {e: 0 for e in self.ENGS}
        self.dcnt = {e: 0 for e in self.ENGS}
        self.pending_pe = []
        self.cur_barrier = None

    def barrier(self, fn):
        keys = list(set(self.lastw) | set(self.readers))
        o = self.op("pool", fn, writes=keys)
        self.cur_barrier = o["idx"]

    def op(self, eng, fn, reads=(), writes=(), dma=False, sig=True):
        o = dict(eng=eng, fn=fn, dma=dma, sig=sig, deps=set(), idx=len(self.ops))
        deps = set()
        if self.cur_barrier is not None:
            deps.add(self.cur_barrier)
        for r in reads:
            if r in self.lastw:
                deps.add(self.lastw[r])
        for w in writes:
            if w in self.lastw:
                deps.add(self.lastw[w])
            for rd in self.readers.get(w, ()):
                deps.add(rd)
        deps.discard(o["idx"])
        o["deps"] = deps
        if dma:
            i = self.dcnt[eng]
            self.dcnt[eng] += 1
            o["dsem"] = (eng, i % self.RING)
            o["dval"] = 16 * (i // self.RING + 1)
            o["dprev"] = 16 * (i // self.RING)
        else:
            if sig:
                self.cnt[eng] += 1
                o["sval"] = self.cnt[eng]
                if eng == "pe":
                    for p in self.pending_pe:
                        p["sval"] = self.cnt[eng]
                    self.pending_pe = []
            else:
                assert eng == "pe"
                self.pending_pe.append(o)
        self.ops.append(o)
        for w in writes:
            self.lastw[w] = o["idx"]
            self.readers[w] = []
        for r in reads:
            if r not in writes:
                self.readers.setdefault(r, []).append(o["idx"])
        return o

    def check(self):
        assert not self.pending_pe
        ops = self.ops
        queues = {e: [o for o in ops if o["eng"] == e] for e in self.ENGS}
        pos = {e: 0 for e in self.ENGS}
        done = [False] * len(ops)
        sigdone = {e: 0 for e in self.ENGS}
        ddone = {}
        dq = {e: [o for o in queues[e] if o["dma"]] for e in self.ENGS}
        progress = True
        while progress:
            progress = False
            for e in self.ENGS:
                while pos[e] < len(queues[e]):
                    o = queues[e][pos[e]]
                    ok = True
                    for d in o["deps"]:
                        do = ops[d]
                        if do["dma"]:
                            if not done[d]:
                                ok = False
                                break
                        elif do["eng"] == "pe" and e == "pe":
                            continue
                        elif sigdone[do["eng"]] < do["sval"]:
                            ok = False
                            break
                    if not ok:
                        break
                    done[o["idx"]] = True
                    if not o["dma"] and o["sig"]:
                        sigdone[e] = o["sval"]
                    pos[e] += 1
                    progress = True
        stuck = {e: (pos[e], len(queues[e])) for e in self.ENGS if pos[e] < len(queues[e])}
        if stuck:
            for e in stuck:
                o = queues[e][pos[e]]
                print("STUCK", e, o["idx"], [(d, ops[d]["eng"], ops[d].get("sval"), done[d]) for d in o["deps"] if not (done[d] and (ops[d]["dma"] or sigdone[ops[d]["eng"]] >= ops[d].get("sval", 0)))][:6])
            raise RuntimeError("deadlock in program: %s" % stuck)

    def emit(self, nc):
        assert not self.pending_pe
        self.check()
        import contextlib
        with contextlib.ExitStack() as es:
            csem = {e: es.enter_context(nc.semaphore("c_" + e)) for e in ("pe", "act", "dve", "pool")}
            dsem = {}
            for e in ("sp", "pool", "act"):
                if self.dcnt[e]:
                    for r in range(self.RING):
                        dsem[(e, r)] = es.enter_context(nc.semaphore("d_%s%d" % (e, r)))
            block = es.enter_context(nc.Block())
            ops = self.ops

            def run(engname, eng):
                waited = {}

                def wait(sem, key, val):
                    if waited.get(key, 0) >= val:
                        return
                    waited[key] = val
                    eng.wait_ge(sem, val)

                for o in ops:
                    if o["eng"] != engname:
                        continue
                    for d in sorted(o["deps"]):
                        do = ops[d]
                        if do["dma"]:
                            wait(dsem[do["dsem"]], do["dsem"], do["dval"])
                        else:
                            if do["eng"] == "pe" and engname == "pe":
                                continue
                            wait(csem[do["eng"]], do["eng"], do["sval"])
                    if o["dma"]:
                        if o["dprev"] > 0:
                            wait(dsem[o["dsem"]], o["dsem"], o["dprev"])
                        ins = o["fn"](eng)
                        ins.then_inc(dsem[o["dsem"]], 16)
                    else:
                        ins = o["fn"](eng)
                        if o["sig"]:
                            ins.then_inc(csem[engname], 1)
                if engname == "sp":
                    for (e, r), s in dsem.items():
                        n = self.dcnt[e]
                        k = (n - 1 - r) // self.RING + 1 if n > r else 0
                        if k > 0:
                            eng.wait_ge(s, 16 * k)
                    for e in ("pe", "act", "dve", "pool"):
                        if self.cnt[e]:
                            eng.wait_ge(csem[e], self.cnt[e])

            @block.sync
            def _(e):
                run("sp", e)

            @block.tensor
            def _(e):
                run("pe", e)

            @block.scalar
            def _(e):
                run("act", e)

            @block.vector
            def _(e):
                run("dve", e)

            @block.gpsimd
            def _(e):
                run("pool", e)


def build(cfg):
    SEQ, NPG, NPOOL, PAST = cfg["SEQ"], cfg["NPG"], cfg["NPOOL"], cfg["PAST"]
    NPT = SEQ // 128
    NTL = NPT + 1
    NT = NTL * 128
    NST = SEQ // 512
    NPAGES = SPC * NPG

    nc = bass.Bass("TRN2", target_bir_lowering=False)
    P = Prog()

    def din(name, shape, dt=F32):
        return nc.dram_tensor(name, list(shape), dt, kind="ExternalInput").ap()

    def dout(name, shape, dt=F32):
        return nc.dram_tensor(name, list(shape), dt, kind="ExternalOutput").ap()

    xp = din("xp", [SEQ, D])
    xs = din("xs", [128, D])
    cache = din("cache", [NPOOL * PAGE, DC + DR])
    stc = din("stc", [SPC * 30, DCONV])
    stf = din("stf", [SPC * 2, 2 * DFF])
    ptab = din("ptab", [1, NPAGES], I32)
    w_in = din("w_in", [128, 8, DIN])
    w_uq = din("w_uq", [128, 3, 768])
    w_uqs = din("w_uqs", [128, 3, 768])
    w_k = din("w_k", [128, 3, 768])
    w_ukT = din("w_ukT", [64, NH, DC])
    w_uv = din("w_uv", [128, 2, 512])
    w_o = din("w_o", [128, 8, D])
    w_up = din("w_up", [NFC, 128, 8, 256])
    w_dn = din("w_dn", [128, NFC, D])
    wdw = din("wdw", [128, 4, CW])
    vecs = din("vecs", [128, 64])
    wf = din("wf", [128, 44, 4])
    rowv = din("rowv", [1, 3072])
    cstok = din("cstok", [NT, 32])
    cosT = din("cosT", [32, NT])
    sinT = din("sinT", [32, NT])
    identf_d = din("identf", [128, 128])
    identb_d = din("identb", [128, 128], BF)
    tri_d = din("tri", [128, 128], BF)
    smask_d = din("smask", [128, SPC * 64], BF)

    y_p = dout("y_p", [SEQ, D])
    y_s = dout("y_s", [128, D])
    kv_p = dout("kv_p", [SEQ, DC + DR])
    conv_p = dout("conv_p", [30, DCONV])
    ffn_p = dout("ffn_p", [2, 2 * DFF])
    kv_s = dout("kv_s", [128, DC + DR])
    conv_s = dout("conv_s", [SPC, 30, DCONV])
    ffn_s = dout("ffn_s", [SPC * 2, 2 * DFF])

    class Arena:
        def __init__(self):
            self.off = 16512
            self.n = 0

        def alloc(self, shape, dt, name=None):
            esz = 4 if dt in (F32, I32) else 2
            nbytes = int(np.prod(shape[1:])) * esz
            self.off = (self.off + 63) // 64 * 64
            self.n += 1
            t = nc.alloc_sbuf_tensor_at("sb%d_%s" % (self.n, name or ""), list(shape), dt, offset=self.off)
            self.last = self.off
            self.off += nbytes
            assert self.off <= 229376, ("SBUF overflow", self.off, name)
            return t

        def at(self, shape, dt, name, offset):
            self.n += 1
            return nc.alloc_sbuf_tensor_at("sb%d_%s" % (self.n, name), list(shape), dt, offset=offset)

        def mark(self):
            return self.off

        def reset(self, m):
            self.off = m

    A = Arena()
    identf = A.alloc([128, 128], F32, "identf")
    identb = A.alloc([128, 128], BF, "identb")
    tri = A.alloc([128, 128], BF, "tri")
    vec = A.alloc([128, 64], F32, "vec")
    epsT = A.alloc([128, 1], F32, "eps")
    rstd_c = A.alloc([128, NTL], F32, "rstdc")
    bar_t = A.alloc([128, 2], F32, "bar")
    GA, GQ, GCO, GAO, GF, BDW = 0, 8, 11, 15, 19, 27
    convT = A.alloc([128, 4, NT], BF, "convT")
    attn_tok = A.alloc([128, max(NTL, CW), 512], BF, "attn_tok")
    attn_off = A.last
    m_persist = A.mark()
    cqnT = A.alloc([128, 3, NT], BF, "cqnT")
    kvT = A.alloc([128, 3, NT], BF, "kvT")
    kvbf_s = A.alloc([128, 289], BF, "kvbf_s")
    m_ab = A.mark()

    ps = [nc.alloc_psum_tensor("ps%d" % i, [128, 512], F32) for i in range(8)]

    def psb(i):
        return ps[i][:, :].bitcast(BF)

    def dma(q, out, in_, reads=(), writes=()):
        def f(e):
            return e.dma_start(out=out, in_=in_)
        return P.op(q, f, reads=reads, writes=writes, dma=True)

    def mm(out, lhsT, rhs, start, stop, reads=(), writes=(), sig=None):
        def f(e):
            return e.matmul(out, lhsT, rhs, start=start, stop=stop)
        return P.op("pe", f, reads=reads, writes=writes, sig=(stop if sig is None else sig))

    def tr(out, in_, ident, reads=(), writes=(), sig=True):
        def f(e):
            return e.transpose(out, in_, ident)
        return P.op("pe", f, reads=reads, writes=writes, sig=sig)

    def act(out, in_, func, reads=(), writes=(), **kw):
        def f(e):
            return e.activation(out=out, in_=in_, func=func, **kw)
        return P.op("act", f, reads=reads, writes=writes)

    def tt(eng, out, in0, in1, op, reads=(), writes=()):
        def f(e):
            return e.tensor_tensor(out=out, in0=in0, in1=in1, op=op)
        return P.op(eng, f, reads=reads, writes=writes)

    def ts(eng, out, in0, s1, s2, op0, op1=None, reads=(), writes=()):
        def f(e):
            if op1 is None:
                return e.tensor_scalar(out=out, in0=in0, scalar1=s1, scalar2=None, op0=op0)
            return e.tensor_scalar(out=out, in0=in0, scalar1=s1, scalar2=s2, op0=op0, op1=op1)
        return P.op(eng, f, reads=reads, writes=writes)

    def stt(eng, out, in0, scalar, in1, op0, op1, reads=(), writes=()):
        eng = "dve"

        def f(e):
            return e.scalar_tensor_tensor(out=out, in0=in0, scalar=scalar, in1=in1, op0=op0, op1=op1)
        return P.op(eng, f, reads=reads, writes=writes)

    def cp(eng, out, in_, reads=(), writes=()):
        if eng == "act":
            return act(out, in_, AF.Copy, reads=reads, writes=writes)

        def f(e):
            return e.tensor_copy(out=out, in_=in_)
        return P.op(eng, f, reads=reads, writes=writes)

    def memset(eng, ap, val, writes=()):
        def f(e):
            return e.memset(ap, val)
        return P.op(eng, f, writes=writes)

    def recip(out, in_, reads=(), writes=()):
        def f(e):
            return e.reciprocal(out=out, in_=in_)
        return P.op("dve", f, reads=reads, writes=writes)

    def rstd_from_ss(ss, n, tmp, out, key):
        act(tmp, ss, AF.Sqrt, reads=[key + "ss", "eps"], writes=[key + "tmp"], scale=1.0 / n, bias=epsT[:, 0:1])
        recip(out, tmp, reads=[key + "tmp"], writes=[key])

    def bcast(v, shape):
        return v.unsqueeze(2).to_broadcast(list(shape))

    dma("sp", identf[:, :], identf_d, writes=["identf"])
    dma("sp", identb[:, :], identb_d, writes=["identb"])
    dma("sp", tri[:, :], tri_d, writes=["tri"])
    dma("sp", vec[:, :], vecs, writes=["vec"])
    memset("dve", epsT[:, :], EPS, writes=["eps"])

    win = A.alloc([128, 8, DIN], BF, "win")
    xt = [A.alloc([128, D], F32, "xt%d" % i) for i in range(2)]
    junk = A.alloc([128, D], BF, "junk")
    xnb = A.alloc([128, D], BF, "xnb")
    hT = A.alloc([128, 8, 128], BF, "hT")
    small = A.alloc([128, 32], F32, "small")
    cqnb = A.alloc([128, DCQ], BF, "cqnb")
    kvrow = [A.alloc([128, 288], F32, "kvrow%d" % i) for i in range(2)]
    kvbf = A.alloc([128, 288], BF, "kvbf")
    krs = A.alloc([128, 32], F32, "krs")
    rtmp = A.alloc([128, 64], F32, "rtmp")
    cs = A.alloc([128, NTL, 32], F32, "cs")
    gkvb = A.alloc([128, 256], F32, "gkvb")
    glnb = A.alloc([128, 512], F32, "glnb")
    blnb = A.alloc([128, 512], F32, "blnb")
    sig_t = A.alloc([128, 4, 128], F32, "sig")
    gluT = A.alloc([128, 4, 30 + 128], F32, "gluT")
    gluS = A.alloc([128, 4, SPC, 38], F32, "gluS")
    yT = A.alloc([128, 4, 128], F32, "yT")
    zt = A.alloc([128, 512], F32, "zt")
    s32 = A.alloc([128, 512], F32, "s32")
    sbf = A.alloc([128, 512], BF, "sbf")
    wdw_t = A.alloc([128, 4, CW], F32, "wdw")
    stt_t = A.alloc([120, 512], F32, "stt")
    gtok = A.alloc([128, 512], F32, "gtok")
    bnst = A.alloc([128, 8], F32, "bnst")
    diag = A.at([128, 4, CW, 128], BF, "diag", attn_off)
    gluTb = A.alloc([128, 4, 30 + 128], BF, "gluTb")
    ones1 = A.alloc([1, 128], BF, "ones1")
    bdwf = A.alloc([1, 512], F32, "bdwf")
    bdwt = A.alloc([1, 512], F32, "bdwt")
    bhi = A.alloc([1, 512], BF, "bhi")
    blo = A.alloc([1, 512], BF, "blo")
    assert A.mark() <= 229376, A.mark()

    dma("pool", win[:, :, :], w_in, writes=["win"])
    dma("sp", cs[:, :, :], cstok.rearrange("(t p) c -> p t c", p=128), writes=["cs"])
    dma("sp", gkvb[:, :], rowv[:, 0:256].partition_broadcast(128), writes=["gkvb"])
    dma("sp", glnb[:, :], rowv[:, 256:768].partition_broadcast(128), writes=["glnb"])
    dma("sp", blnb[:, :], rowv[:, 768:1280].partition_broadcast(128), writes=["blnb"])
    dma("sp", wdw_t[:, :, :], wdw, writes=["wdw"])
    memset("pool", gluT[:, :, :], 0.0, writes=["gluT"])
    memset("pool", gluTb[:, :, :], 0.0, writes=["gluTb"])
    memset("pool", ones1[:, :], 1.0, writes=["ones1"])
    dma("sp", bdwf[:, :], rowv[:, 2304:2816], writes=["bdwf"])
    cp("dve", bhi[:, :], bdwf[:, :], reads=["bdwf"], writes=["bhi"])
    tt("dve", bdwt[:, :], bdwf[:, :], bhi[:, :], ALU.subtract, reads=["bdwf", "bhi"], writes=["bdwt"])
    cp("dve", blo[:, :], bdwt[:, :], reads=["bdwt"], writes=["blo"])
    for c in range(4):
        for k in range(CW):
            ts("pool" if (k % 2) else "dve", diag[:, c, k, :], identb[:, :], wdw_t[:, c, k:k + 1], None, ALU.mult,
               reads=["identb", "wdw"], writes=["diag"])

    for r in range(4):
        dma("sp", stt_t[:, :], stc[120 * r:120 * r + 120, :], writes=["stt"])
        for c in range(4):
            tr(ps[0][:, 128 * c:128 * c + 120], stt_t[:, 128 * c:128 * c + 128], identf[:120, :120],
               reads=["stt", "identf"], writes=["ps0"], sig=(c == 3))
        for c in range(4):
            cp("dve", gluS[:, c, 4 * r:4 * r + 4, 0:30],
               ps[0][:, 128 * c:128 * c + 120].rearrange("p (s j) -> p s j", j=30),
               reads=["ps0"], writes=["gluS"])
    dma("sp", conv_s[:, 0:22, :], stc.rearrange("(s j) c -> s j c", j=30)[:, 8:30, :])

    sm = lambda i: small[:, i:i + 1]
    for t in range(NTL):
        samp = (t == NPT)
        xsrc = xs if samp else xp[128 * t:128 * t + 128, :]
        x_t = xt[t % 2]
        xk = "xt%d" % (t % 2)
        dma("sp", x_t[:, :], xsrc, writes=[xk])
        act(junk[:, :], x_t[:, :], AF.Square, reads=[xk], writes=["junk", "ss0ss"], accum_out=sm(0))
        rstd_from_ss(sm(0), D, sm(1), sm(2), "ss0")
        act(xnb[:, :], x_t[:, :], AF.Copy, reads=[xk, "ss0"], writes=["xnb"], scale=sm(2))
        for k in range(8):
            tr(psb(0)[:, 128 * k:128 * k + 128], xnb[:, 128 * k:128 * k + 128], identb[:, :],
               reads=["xnb", "identb"], writes=["ps0"], sig=(k == 7))
        tt("dve", hT[:, :, :], psb(0)[:, :].rearrange("p (k n) -> p k n", n=128),
           bcast(vec[:, GA:GA + 8], [128, 8, 128]), ALU.mult, reads=["ps0", "vec"], writes=["hT"])
        for k in range(8):
            mm(ps[1][:, 0:384], hT[:, k, :], win[:, k, 0:384], k == 0, k == 7, reads=["hT", "win"], writes=["ps1"])
        for k in range(8):
            mm(ps[2][:, 0:288], hT[:, k, :], win[:, k, 384:672], k == 0, k == 7, reads=["hT", "win"], writes=["ps2"])
        for c in range(8):
            bank = 3 if c < 4 else 4
            cc = c % 4
            for k in range(8):
                mm(ps[bank][:, 128 * cc:128 * cc + 128], win[:, k, 672 + 128 * c:672 + 128 * c + 128], hT[:, k, :],
                   k == 0, k == 7, reads=["hT", "win"], writes=["ps%d" % bank], sig=(k == 7 and cc == 3))
        act(junk[:, 0:384], ps[1][:, 0:384], AF.Square, reads=["ps1"], writes=["junk", "ss1ss"], accum_out=sm(3))
        rstd_from_ss(sm(3), DCQ, sm(4), sm(5), "ss1")
        act(cqnb[:, :], ps[1][:, 0:384], AF.Copy, reads=["ps1", "ss1"], writes=["cqnb"], scale=sm(5))
        for k in range(3):
            tr(psb(5)[:, 128 * k:128 * k + 128], cqnb[:, 128 * k:128 * k + 128], identb[:, :],
               reads=["cqnb", "identb"], writes=["ps5"], sig=(k == 2))
        tt("dve", cqnT[:, :, 128 * t:128 * t + 128], psb(5)[:, 0:384].rearrange("p (k n) -> p k n", n=128),
           bcast(vec[:, GQ:GQ + 3], [128, 3, 128]), ALU.mult, reads=["ps5", "vec"], writes=["cqnT"])
        kr_ = kvrow[t % 2]
        kk = "kvrow%d" % (t % 2)
        act(junk[:, 0:256], ps[2][:, 0:256], AF.Square, reads=["ps2"], writes=["junk", "ss2ss"], accum_out=sm(6))
        rstd_from_ss(sm(6), DC, sm(7), sm(8), "ss2")
        stt("dve", kr_[:, 0:256], ps[2][:, 0:256], sm(8), gkvb[:, :], ALU.mult, ALU.mult,
            reads=["ps2", "ss2", "gkvb"], writes=[kk])
        cp("act", krs[:, :], ps[2][:, 256:288], reads=["ps2"], writes=["krs"])
        cst = cs[:, t, :]
        tt("pool", rtmp[:, 0:16], krs[:, 0:16], cst[:, 0:16], ALU.mult, reads=["krs", "cs"], writes=["rt0"])
        tt("pool", rtmp[:, 16:32], krs[:, 16:32], cst[:, 16:32], ALU.mult, reads=["krs", "cs"], writes=["rt1"])
        tt("pool", kr_[:, 256:272], rtmp[:, 0:16], rtmp[:, 16:32], ALU.subtract, reads=["rt0", "rt1"], writes=[kk])
        tt("pool", rtmp[:, 32:48], krs[:, 16:32], cst[:, 0:16], ALU.mult, reads=["krs", "cs"], writes=["rt2"])
        tt("pool", rtmp[:, 48:64], krs[:, 0:16], cst[:, 16:32], ALU.mult, reads=["krs", "cs"], writes=["rt3"])
        tt("pool", kr_[:, 272:288], rtmp[:, 32:48], rtmp[:, 48:64], ALU.add, reads=["rt2", "rt3"], writes=[kk])
        dma("sp", (kv_s if samp else kv_p[128 * t:128 * t + 128, :]), kr_[:, :], reads=[kk])
        cp("act", kvbf[:, :], kr_[:, :], reads=[kk], writes=["kvbf"])
        if samp:
            cp("act", kvbf_s[:, 0:288], kr_[:, :], reads=[kk], writes=["kvbf_s"])
            memset("pool", kvbf_s[:, 288:289], 1.0, writes=["kvbf_s1"])
        tr(psb(5)[:, 0:128], kvbf[:, 0:128], identb[:, :], reads=["kvbf", "identb"], writes=["ps5"], sig=False)
        tr(psb(5)[:, 128:256], kvbf[:, 128:256], identb[:, :], reads=["kvbf", "identb"], writes=["ps5"], sig=False)
        tr(psb(5)[:96, 256:384], kvbf[:, 192:288], identb[:, :], reads=["kvbf", "identb"], writes=["ps5"])
        cp("dve", kvT[:, 0:2, 128 * t:128 * t + 128], psb(5)[:, 0:256].rearrange("p (k n) -> p k n", n=128),
           reads=["ps5"], writes=["kvT"])
        cp("dve", kvT[:96, 2, 128 * t:128 * t + 128], psb(5)[:96, 256:384], reads=["ps5"], writes=["kvT"])
        act(sig_t[:, :, :], ps[4][:, :].rearrange("p (c n) -> p c n", n=128), AF.Sigmoid, reads=["ps4"], writes=["sig"])
        if not samp:
            tt("dve", gluT[:, :, 30:158], ps[3][:, :].rearrange("p (c n) -> p c n", n=128), sig_t[:, :, :], ALU.mult,
               reads=["ps3", "sig"], writes=["gluT"])
            cp("pool", gluTb[:, :, 30:158], gluT[:, :, 30:158], reads=["gluT"], writes=["gluTb"])
        else:
            for c in range(4):
                tt("dve", gluS[:, c, :, 30:38], ps[3][:, 128 * c:128 * c + 128].rearrange("p (s j) -> p s j", j=8),
                   sig_t[:, c, :].rearrange("p (s j) -> p s j", j=8), ALU.mult, reads=["ps3", "sig"], writes=["gluS"])
        if not samp:
            mm(ps[7][:, 0:512], ones1[0:1, :], bhi[0:1, :], True, False, reads=["ones1", "bhi"], writes=["ps7"], sig=False)
            mm(ps[7][:, 0:512], ones1[0:1, :], blo[0:1, :], False, False, reads=["ones1", "blo"], writes=["ps7"], sig=False)
            for c in range(4):
                for k in range(CW):
                    last = (c == 3 and k == CW - 1)
                    mm(ps[7][:, 128 * c:128 * c + 128], gluTb[:, c, k:k + 128], diag[:, c, k, :], False, last,
                       reads=["gluTb", "diag"], writes=["ps7"], sig=last)
        else:
            gk = "gluS"

            def csrc(c, k):
                return gluS[:, c, :, k:k + 8]

            def cdst(c):
                return yT[:, c, :].rearrange("p (s j) -> p s j", j=8)
            for c in range(4):
                ts("dve", cdst(c), csrc(c, 0), wdw_t[:, c, 0:1], vec[:, BDW + c:BDW + c + 1], ALU.mult, ALU.add,
                   reads=[gk, "wdw", "vec"], writes=["yT%d" % c])
            for k in range(1, CW):
                for c in range(4):
                    stt("dve", cdst(c), csrc(c, k), wdw_t[:, c, k:k + 1], cdst(c), ALU.mult, ALU.add,
                        reads=[gk, "wdw", "yT%d" % c], writes=["yT%d" % c])
        if t == NPT - 1:
            for c in range(4):
                tr(ps[6][:30, 128 * c:128 * c + 128], gluT[:, c, 128:158], identf[:, :],
                   reads=["gluT", "identf"], writes=["ps6"], sig=(c == 3))
            cp("act", gtok[:30, :], ps[6][:30, :], reads=["ps6"], writes=["gtok"])
            dma("sp", conv_p, gtok[:30, :], reads=["gtok"])
        if samp:
            for c in range(4):
                cp("pool", sig_t[:, c, :].rearrange("p (s j) -> p s j", j=8), gluS[:, c, :, 30:38],
                   reads=["gluS", "sig"], writes=["sig"])
            for c in range(4):
                tr(ps[6][:, 128 * c:128 * c + 128], sig_t[:, c, :], identf[:, :],
                   reads=["sig", "identf"], writes=["ps6"], sig=(c == 3))
            cp("act", gtok[:, :], ps[6][:, :], reads=["ps6"], writes=["gtok"])
            for s in range(SPC):
                dma("sp", conv_s[s, 22:30, :], gtok[8 * s:8 * s + 8, :], reads=["gtok"])
        if not samp:
            cp("pool", gluTb[:, :, 0:30], gluTb[:, :, 128:158], reads=["gluTb"], writes=["gluTb"])
        if samp:
            for c in range(4):
                tr(ps[7][:, 128 * c:128 * c + 128], yT[:, c, :], identf[:, :],
                   reads=["yT%d" % c, "identf"], writes=["ps7"], sig=(c == 3))
        P.op("dve", lambda e: e.bn_stats(out=bnst[:, 0:6], in_=ps[7][:, :]), reads=["ps7"], writes=["bnst"])
        P.op("dve", lambda e: e.bn_aggr(out=bnst[:, 6:8], in_=bnst[:, 0:6]), reads=["bnst"], writes=["mv"])
        act(sm(9), bnst[:, 7:8], AF.Sqrt, reads=["mv", "eps"], writes=["lntmp"], scale=1.0, bias=epsT[:, 0:1])
        recip(sm(10), sm(9), reads=["lntmp"], writes=["lnr"])
        ts("dve", sm(11), bnst[:, 6:7], -1.0, sm(10), ALU.mult, ALU.mult, reads=["mv", "lnr"], writes=["lnb"])
        act(zt[:, :], ps[7][:, :], AF.Identity, reads=["ps7", "lnr", "lnb"], writes=["zt"], scale=sm(10), bias=sm(11))
        tt("dve", zt[:, :], zt[:, :], glnb[:, :], ALU.mult, reads=["zt", "glnb"], writes=["zt"])
        tt("pool", zt[:, :], zt[:, :], blnb[:, :], ALU.add, reads=["zt", "blnb"], writes=["zt"])
        act(s32[:, :], zt[:, :], AF.Silu, reads=["zt"], writes=["s32"])
        act(junk[:, 0:512], s32[:, :], AF.Square, reads=["s32"], writes=["junk", "ss3ss"], accum_out=sm(12))
        rstd_from_ss(sm(12), DCONV, sm(13), rstd_c[:, t:t + 1], "ss3")
        cp("dve", sbf[:, :], s32[:, :], reads=["s32"], writes=["sbf"])
        for c in range(4):
            tr(psb(5)[:, 512 + 128 * c:512 + 128 * c + 128], sbf[:, 128 * c:128 * c + 128], identb[:, :],
               reads=["sbf", "identb"], writes=["ps5"], sig=(c == 3))
        tt("dve", convT[:, :, 128 * t:128 * t + 128], psb(5)[:, 512:1024].rearrange("p (k n) -> p k n", n=128),
           bcast(vec[:, GCO:GCO + 4], [128, 4, 128]), ALU.mult, reads=["ps5", "vec"], writes=["convT"])

    A.reset(m_ab)
    P.barrier(lambda e: e.memset(bar_t[:, :], 0.0))
    wuv = A.alloc([128, 2, 512], BF, "wuv")
    QL = A.alloc([128, 2, NH, 128], BF, "QL")
    QR = A.alloc([128, NH, 128], BF, "QR")
    m_b = A.mark()
    wuq = A.alloc([128, 3, 768], BF, "wuq")
    wuqs = A.alloc([128, 3, 768], BF, "wuqs")
    wk = A.alloc([128, 3, 768], BF, "wk")
    wukT = A.alloc([64, NH, DC], BF, "wukT")
    Khl = [A.alloc([128, SEQ], BF, "Kh%d" % i) for i in range(2)]
    Vhl = [A.alloc([128, NPT, 65], BF, "Vh%d" % i) for i in range(2)]
    Qh = [A.alloc([128, 512], BF, "Qh%d" % i) for i in range(2)]
    ctl = [A.alloc([128, 512], F32, "ct%d" % i) for i in range(2)]
    snl = [A.alloc([128, 512], F32, "sn%d" % i) for i in range(2)]
    t1l = [A.alloc([128, 512], F32, "t1%d" % i) for i in range(2)]
    t2l = [A.alloc([128, 512], F32, "t2%d" % i) for i in range(2)]
    PT = [A.alloc([128, 512], BF, "PT%d" % i) for i in range(4)]
    Osbl = [A.alloc([128, 512], F32, "Osb%d" % i) for i in range(2)]
    rl4l = [A.alloc([128, 4], F32, "rl4%d" % i) for i in range(2)]

    dma("pool", wuq[:, :, :], w_uq, writes=["wuq"])
    dma("pool", wuqs[:, :, :], w_uqs, writes=["wuqs"])
    dma("pool", wk[:, :, :], w_k, writes=["wk"])
    dma("pool", wuv[:, :, :], w_uv, writes=["wuv"])
    dma("pool", wukT[:, :, :], w_ukT, writes=["wukT"])
    for i in range(2):
        memset("pool", Vhl[i][:, :, 64:65], 1.0, writes=["Vh1_%d" % i])

    def emitKV(h):
        Kh, Vh = Khl[h % 2], Vhl[h % 2]
        kk, vk = "Kh%d" % (h % 2), "Vh%d" % (h % 2)
        for g in range(0, NPT, 8):
            n = min(8, NPT - g)
            for i in range(n):
                kt = g + i
                for k in range(2):
                    mm(ps[1][:, 64 * i:64 * i + 64], kvT[:, k, 128 * kt:128 * kt + 128], wuv[:, k, 64 * h:64 * h + 64],
                       k == 0, k == 1, reads=["kvT", "wuv"], writes=["ps1"], sig=(k == 1 and i == n - 1))
            cp("act", Vh[:, g:g + n, 0:64], ps[1][:, 0:64 * n].rearrange("p (i v) -> p i v", v=64),
               reads=["ps1"], writes=[vk])
        for st in range(NST):
            for k in range(3):
                lo = 64 if k == 2 else 0
                hi = 96 if k == 2 else 128
                mm(ps[1][:96, :], wk[lo:hi, k, 96 * h:96 * h + 96], kvT[lo:hi, k, 512 * st:512 * st + 512],
                   k == 0, k == 2, reads=["kvT", "wk"], writes=["ps1"])
            cp("dve", Kh[:96, 512 * st:512 * st + 512], ps[1][:96, :], reads=["ps1"], writes=[kk])

    units = [(h, qs) for h in range(NH) for qs in range(NST + 1)]

    def emitQ(ui):
        h, qs = units[ui]
        samp = (qs == NST)
        T = 128 if samp else 512
        q0g = 512 * qs
        u = ui % 2
        ct, sn, Q, t1, t2 = ctl[u], snl[u], Qh[u], t1l[u], t2l[u]
        dma("sp", ct[64:96, 0:T], cosT[:, q0g:q0g + T], writes=["ct%d" % u])
        dma("sp", sn[64:96, 0:T], sinT[:, q0g:q0g + T], writes=["sn%d" % u])
        for k in range(3):
            mm(ps[2][:96, 0:T], wuq[:, k, 96 * h:96 * h + 96], cqnT[:, k, q0g:q0g + T], k == 0, k == 2,
               reads=["cqnT", "wuq"], writes=["ps2"])
        for k in range(3):
            mm(ps[3][:96, 0:T], wuqs[:, k, 96 * h:96 * h + 96], cqnT[:, k, q0g:q0g + T], k == 0, k == 2,
               reads=["cqnT", "wuqs"], writes=["ps3"])
        qk = "Q%d" % u
        cp("act", Q[0:64, 0:T], ps[2][0:64, 0:T], reads=["ps2"], writes=[qk])
        tt("dve", t1[64:96, 0:T], ps[2][64:96, 0:T], ct[64:96, 0:T], ALU.mult, reads=["ps2", "ct%d" % u], writes=["t1%d" % u])
        tt("dve", t2[64:96, 0:T], ps[3][64:96, 0:T], sn[64:96, 0:T], ALU.mult, reads=["ps3", "sn%d" % u], writes=["t2%d" % u])
        tt("pool", Q[64:96, 0:T], t1[64:96, 0:T], t2[64:96, 0:T], ALU.add, reads=["t1%d" % u, "t2%d" % u], writes=[qk])

    emitKV(0)
    emitQ(0)
    gkt = 0
    for ui, (h, qs) in enumerate(units):
        samp = (qs == NST)
        u = ui % 2
        Q = Qh[u]
        qk = "Q%d" % u
        Kh, Vh = Khl[h % 2], Vhl[h % 2]
        kk, vk, v1k = "Kh%d" % (h % 2), "Vh%d" % (h % 2), "Vh1_%d" % (h % 2)
        if samp:
            for c in range(2):
                mm(ps[1][:, 128 * c:128 * c + 128], wukT[:, h, 128 * c:128 * c + 128], Q[0:64, 0:128], True, True,
                   reads=["wukT", qk], writes=["ps1"], sig=(c == 1))
            cp("act", QL[:, :, h, :], ps[1][:, 0:256].rearrange("p (c n) -> p c n", n=128), reads=["ps1"], writes=["QL"])
            cp("pool", QR[64:96, h, :], Q[64:96, 0:128], reads=[qk], writes=["QR"])
            if ui + 1 < len(units):
                emitQ(ui + 1)
            continue
        nkt = 4 * qs + 4
        ob = 6 if (ui % 2 == 0) else 0
        obk = "ps%d" % ob

        def stS(kt):
            j = kt - 4 * qs
            q0 = 128 * j if j > 0 else 0
            sb = 4 + ((gkt + kt) % 2)
            sl = (gkt + kt) % 4
            mm(ps[sb][:, q0:512], Kh[:96, 128 * kt:128 * kt + 128], Q[:96, q0:512], True, True,
               reads=[kk, qk], writes=["ps%d" % sb])
            act(PT[sl][:, q0:512], ps[sb][:, q0:512], AF.Exp, reads=["ps%d" % sb], writes=["PT%d" % sl], scale=SCALE)
            if j >= 0:
                tt("pool", PT[sl][:, q0:q0 + 128], PT[sl][:, q0:q0 + 128], tri[:, :], ALU.mult,
                   reads=["PT%d" % sl, "tri"], writes=["PT%d" % sl])

        def stPV(kt):
            j = kt - 4 * qs
            q0 = 128 * j if j > 0 else 0
            sl = (gkt + kt) % 4
            mm(ps[ob][:65, q0:512], Vh[:, kt, 0:65], PT[sl][:, q0:512], kt == 0, kt == nkt - 1,
               reads=[vk, v1k, "PT%d" % sl], writes=[obk], sig=True)

        stS(0)
        for kt in range(nkt):
            if kt + 1 < nkt:
                stS(kt + 1)
            if kt == 1:
                if ui + 1 < len(units):
                    emitQ(ui + 1)
                if qs == 0 and h + 1 < NH:
                    emitKV(h + 1)
            stPV(kt)
        gkt += nkt
        Osb, rl4 = Osbl[ui % 2], rl4l[ui % 2]
        ok_, rk_ = "Osb%d" % (ui % 2), "rl4%d" % (ui % 2)
        cp("act", Osb[:65, :], ps[ob][:65, :], reads=[obk], writes=[ok_])
        for jj in range(4):
            tr(ps[7][:, 65 * jj:65 * jj + 65], Osb[:65, 128 * jj:128 * jj + 128], identf[:65, :65],
               reads=[ok_, "identf"], writes=["ps7"], sig=(jj == 3))
        ov = ps[7][:, 0:260].rearrange("p (j v) -> p j v", v=65)
        recip(rl4[:, :], ov[:, :, 64], reads=["ps7"], writes=[rk_])
        tt("dve", attn_tok[:, 4 * qs:4 * qs + 4, 64 * h:64 * h + 64], ov[:, :, 0:64],
           bcast(rl4[:, :], [128, 4, 64]), ALU.mult, reads=["ps7", rk_], writes=["attn_tok"])

    if cfg.get("STOP") == "B":
        P.emit(nc)
        return nc
    A.reset(m_b)
    P.barrier(lambda e: e.memset(bar_t[:, :], 0.0))
    NSL = 8
    NKS = 4
    pg = [A.alloc([128, 289], BF, "pg%d" % i) for i in range(NSL)]
    KT = [A.alloc([128, 384], BF, "KT%d" % i) for i in range(NKS)]
    PTs = [A.alloc([128, 64], BF, "PTs%d" % i) for i in range(NKS)]
    ptb = A.alloc([128, NPAGES], I32, "ptb")
    idx = A.alloc([128, NPAGES], I32, "idx")
    iot = A.alloc([128, 1], I32, "iota")
    smask = A.alloc([128, SPC * 64], BF, "smask")
    olatl = [A.alloc([64, 256], BF, "olat%d" % i) for i in range(2)]
    rlsl = [A.alloc([64, 1], F32, "rls%d" % i) for i in range(2)]
    OLT = A.alloc([128, 2, NH, 128], BF, "OLT")
    dma("sp", ptb[:, :], ptab.partition_broadcast(128), writes=["ptb"])
    dma("sp", smask[:, :], smask_d, writes=["smask"])
    P.op("pool", lambda e: e.iota(iot[:, :], pattern=[[0, 1]], base=0, channel_multiplier=1), writes=["iota"])
    ts("pool", idx[:, :], ptb[:, :], PAGE, iot[:, 0:1], ALU.mult, ALU.add, reads=["ptb", "iota"], writes=["idx"])
    for i in range(NSL):
        memset("pool", pg[i][:, 288:289], 1.0, writes=["pg1_%d" % i])
    pages = []
    gi = 0
    for s in range(SPC):
        for p in range(NPG + 1):
            new = (p == NPG)
            pages.append(dict(s=s, p=p, new=new, gi=(None if new else gi), fi=(None if new else s * NPG + p)))
            if not new:
                gi += 1
    NPGS = len(pages)
    real = [pgd for pgd in pages if not pgd["new"]]

    def stG(g):
        if g >= len(real):
            return
        pgd = real[g]
        gs = g % NSL
        pgt = pg[gs]
        fi = pgd["fi"]

        def gat(e, pgt=pgt, fi=fi):
            return e.indirect_dma_start(out=pgt[:, 0:288], out_offset=None, in_=cache,
                                        in_offset=bass.IndirectOffsetOnAxis(ap=idx[:, fi:fi + 1], axis=0))
        P.op("pool", gat, reads=["idx"], writes=["pg%d" % gs], dma=True)

    def stT(i):
        if i >= NPGS or pages[i]["new"]:
            return
        g = pages[i]["gi"]
        gs = g % NSL
        pgk = "pg%d" % gs
        pgt = pg[gs]
        kb = g % 3
        ksl = g % NKS
        tr(psb(kb)[:, 0:128], pgt[:, 0:128], identb[:, :], reads=[pgk, "identb"], writes=["ps%d" % kb], sig=False)
        tr(psb(kb)[:, 128:256], pgt[:, 128:256], identb[:, :], reads=[pgk, "identb"], writes=["ps%d" % kb], sig=False)
        tr(psb(kb)[:96, 256:384], pgt[:, 192:288], identb[:, :], reads=[pgk, "identb"], writes=["ps%d" % kb])
        cp("dve", KT[ksl][:, 0:384], psb(kb)[:, 0:384], reads=["ps%d" % kb], writes=["KT%d" % ksl])

    def stQK(i):
        if i >= NPGS:
            return
        pgd = pages[i]
        s = pgd["s"]
        if not pgd["new"]:
            ksl = pgd["gi"] % NKS
            k0, k1, k2 = KT[ksl][:, 0:128], KT[ksl][:, 128:256], KT[ksl][64:96, 256:384]
            kreads = ["KT%d" % ksl]
        else:
            c0 = NPT * 128
            k0, k1, k2 = kvT[:, 0, c0:c0 + 128], kvT[:, 1, c0:c0 + 128], kvT[64:96, 2, c0:c0 + 128]
            kreads = ["kvT"]
        sbk = 3 + (i % 2)
        so = 64 * ((i // 2) % 2)
        pso = ps[sbk][:, so:so + 64]
        psk = "ps%d_%d" % (sbk, so)
        psl = i % NKS
        mm(pso, k0, QL[:, 0, :, 8 * s:8 * s + 8], True, False, reads=kreads + ["QL"], writes=[psk])
        mm(pso, k1, QL[:, 1, :, 8 * s:8 * s + 8], False, False, reads=kreads + ["QL"], writes=[psk])
        mm(pso, k2, QR[64:96, :, 8 * s:8 * s + 8], False, True, reads=kreads + ["QR"], writes=[psk])
        act(PTs[psl][:, :], pso, AF.Exp, reads=[psk], writes=["PTs%d" % psl], scale=SCALE)
        if pgd["new"]:
            tt("pool", PTs[psl][:, :], PTs[psl][:, :], smask[:, 64 * s:64 * s + 64], ALU.mult,
               reads=["PTs%d" % psl, "smask"], writes=["PTs%d" % psl])

    def stPV(i):
        pgd = pages[i]
        s = pgd["s"]
        psl = i % NKS
        ob = 5 if s % 2 == 0 else 7
        if not pgd["new"]:
            gs = pgd["gi"] % NSL
            vrhs = pg[gs][:, 0:289]
            vreads = ["pg%d" % gs, "pg1_%d" % gs]
        else:
            vrhs = kvbf_s[:, 0:289]
            vreads = ["kvbf_s", "kvbf_s1"]
        mm(ps[ob][:64, 0:289], PTs[psl][:, :], vrhs, pgd["p"] == 0, pgd["new"], reads=["PTs%d" % psl] + vreads,
           writes=["ps%d" % ob], sig=True)
        if pgd["new"]:
            olat, rls = olatl[s % 2], rlsl[s % 2]
            olk, rlk = "olat%d" % (s % 2), "rls%d" % (s % 2)
            recip(rls[:, :], ps[ob][:64, 288:289], reads=["ps%d" % ob], writes=[rlk])
            ts("dve", olat[:, :], ps[ob][:64, 0:256], rls[:, 0:1], None, ALU.mult, reads=["ps%d" % ob, rlk], writes=[olk])
            for c in range(2):
                tr(psb(6)[:, 64 * c:64 * c + 64], olat[:, 128 * c:128 * c + 128], identb[:64, :64],
                   reads=[olk, "identb"], writes=["ps6"], sig=(c == 1))
            cp("act", OLT[:, :, :, 8 * s:8 * s + 8], psb(6)[:, 0:128].rearrange("p (c h j) -> p c h j", c=2, h=NH),
               reads=["ps6"], writes=["OLT"])

    GD = 5
    for g in range(GD):
        stG(g)
    stT(0)
    stT(1)
    stQK(0)
    for i in range(NPGS):
        if not pages[i]["new"]:
            stG(pages[i]["gi"] + GD)
        stT(i + 2)
        stQK(i + 1)
        stPV(i)
    for h in range(NH):
        for c in range(2):
            mm(ps[1][:, 64 * h:64 * h + 64], OLT[:, c, h, :], wuv[:, c, 64 * h:64 * h + 64], c == 0, c == 1,
               reads=["OLT", "wuv"], writes=["ps1"], sig=(c == 1 and h == NH - 1))
    cp("act", attn_tok[:, NPT, :], ps[1][:, :], reads=["ps1"], writes=["attn_tok"])

    if cfg.get("STOP") == "S":
        P.emit(nc)
        return nc
    def phaseC_new():
        A.reset(m_persist)
        P.barrier(lambda e: e.memset(bar_t[:, :], 0.0))
        guardC = []
        wdn = A.alloc([128, NFC, D], BF, "wdn")
        wst = [A.alloc([128, 8, 256], BF, "wst%d" % i) for i in range(2)]
        x1 = A.alloc([128, 4, D], F32, "x1")
        x1_off = A.last
        h2T = A.alloc([128, 8, 512], BF, "h2T")
        actT = A.alloc([128, NFC, 512], BF, "actT")
        ugl = [A.alloc([128, 640], F32, "ug%d" % i) for i in range(2)]
        wos = A.at([128, 8, D], BF, "wos", A.last - 2560)
        uvl = [A.alloc([128, 640], F32, "uv%d" % i) for i in range(2)]
        cgl = [A.alloc([128, 512], F32, "cg%d" % i) for i in range(2)]
        cvl = [A.alloc([128, 512], F32, "cv%d" % i) for i in range(2)]
        xt2l = [A.alloc([128, D], F32, "xt2%d" % i) for i in range(1)]
        x1n = A.alloc([128, D], BF, "x1n")
        aT = A.alloc([128, 4, 128], BF, "aT")
        carry = A.alloc([128, 44, 2], F32, "carry")
        carry_s = A.at([128, 44, SPC, 2], F32, "carry_s", x1_off + 4096)
        wft = A.alloc([128, 44, 4], F32, "wft")
        gfb = A.alloc([128, D], F32, "gfb")
        sm2 = A.alloc([128, 16], F32, "sm2")
        stfl = [A.alloc([32, 512], F32, "stf%d" % i) for i in range(2)]
        x2l = [A.alloc([128, D], F32, "x2%d" % i) for i in range(2)]

        WOSK = ["ug0", "ug1", "uv0", "uv1", "cg0", "cg1", "cv0", "cv1"]
        NCH = (NST + 1) * NFC

        def emit_wst(g):
            if g < NCH:
                dma("pool", wst[g % 2][:, :, :], w_up[g % NFC], writes=["wst%d" % (g % 2)])
        dma("pool", wos[:, :, :], w_o, writes=WOSK)
        dma("pool", wdn[:, :, :], w_dn, writes=["wdn"])
        emit_wst(0)
        dma("sp", wft[:, :, :], wf, writes=["wft"])
        dma("sp", gfb[:, :], rowv[:, 1280:2304].partition_broadcast(128), writes=["gfb"])
        memset("pool", carry[:, :, :], 0.0, writes=["carry"])
        s2 = lambda i: sm2[:, i:i + 1]
        gch = 0
        gtl = 0

        for st in range(NST + 1):
            samp = (st == NST)
            ntile = 1 if samp else 4
            T = 128 * ntile
            if st > 0:
                dma("pool", wos[:, :, :], w_o, writes=WOSK)
            if samp:
                for g in range(11):
                    sb_ = stfl[g % 2]
                    sk_ = "stf%d" % (g % 2)
                    dma("sp", sb_[:, :], stf[:, 512 * g:512 * g + 512], writes=[sk_])
                    for i in range(4):
                        tr(ps[0][:, 32 * i:32 * i + 32], sb_[:, 128 * i:128 * i + 128], identf[:32, :32],
                           reads=[sk_, "identf"], writes=["ps0"], sig=(i == 3))
                    cp("dve", carry_s[:, 4 * g:4 * g + 4, :, :].rearrange("p c s j -> p c (s j)"),
                       ps[0][:, 0:128].rearrange("p (c n) -> p c n", n=32), reads=["ps0"], writes=["carry_s", "x1_1", "x1_2"])
            for j in range(ntile):
                t = 4 * st + j
                xt2 = xt2l[0]
                xk2 = "xt20"
                gtl += 1
                dma("sp", xt2[:, :], (xs if samp else xp[128 * t:128 * t + 128, :]), writes=[xk2])
                act(x1n[:, 0:512], attn_tok[:, t, :], AF.Square, reads=["attn_tok"], writes=["x1n", "sa_ss"], accum_out=s2(0))
                rstd_from_ss(s2(0), 512, s2(1), s2(2), "sa_")
                for c in range(4):
                    tr(psb(0)[:, 128 * c:128 * c + 128], attn_tok[:, t, 128 * c:128 * c + 128], identb[:, :],
                       reads=["attn_tok", "identb"], writes=["ps0"], sig=(c == 3))
                tt("dve", aT[:, :, :], psb(0)[:, 0:512].rearrange("p (k n) -> p k n", n=128),
                   bcast(vec[:, GAO:GAO + 4], [128, 4, 128]), ALU.mult, reads=["ps0", "vec"], writes=["aT"])
                for n in range(2):
                    for c in range(4):
                        mm(ps[1 + n][:, :], aT[:, c, :], wos[:, c, 512 * n:512 * n + 512], c == 0, c == 3,
                           reads=["aT"] + WOSK, writes=["ps%d" % (1 + n)])
                for n in range(2):
                    for c in range(4):
                        mm(ps[3 + n][:, :], convT[:, c, 128 * t:128 * t + 128], wos[:, 4 + c, 512 * n:512 * n + 512], c == 0, c == 3,
                           reads=["convT"] + WOSK, writes=["ps%d" % (3 + n)])
                for n in range(2):
                    stt("dve", x1[:, j, 512 * n:512 * n + 512], ps[1 + n][:, :], s2(2), xt2[:, 512 * n:512 * n + 512], ALU.mult, ALU.add,
                        reads=["ps%d" % (1 + n), "sa_", xk2], writes=["x1_%d" % j])
                    stt("dve", x1[:, j, 512 * n:512 * n + 512], ps[3 + n][:, :], rstd_c[:, t:t + 1], x1[:, j, 512 * n:512 * n + 512],
                        ALU.mult, ALU.add, reads=["ps%d" % (3 + n), "ss3", "x1_%d" % j], writes=["x1_%d" % j])
                act(x1n[:, :], x1[:, j, :], AF.Square, reads=["x1_%d" % j], writes=["x1n", "sf_ss"], accum_out=s2(3))
                rstd_from_ss(s2(3), D, s2(4), s2(5), "sf_")
                act(x1n[:, :], x1[:, j, :], AF.Copy, reads=["x1_%d" % j, "sf_"], writes=["x1n"], scale=s2(5))
                for k in range(8):
                    tr(psb(5)[:, 128 * k:128 * k + 128], x1n[:, 128 * k:128 * k + 128], identb[:, :],
                       reads=["x1n", "identb"], writes=["ps5"], sig=(k == 7))
                tt("dve", h2T[:, :, 128 * j:128 * j + 128], psb(5)[:, :].rearrange("p (k n) -> p k n", n=128),
                   bcast(vec[:, GF:GF + 8], [128, 8, 128]), ALU.mult, reads=["ps5", "vec"], writes=["h2T"])
            for i in range(NFC):
                par = gch % 2
                gch += 1
                wkk = "wst%d" % par
                emit_wst(gch)
                banks = (6, 7) if (par == 0 or cfg.get("C_BANKS") == "same") else (0, 5)
                taps = []
                for half in range(2):
                    bank = banks[half]
                    ub = (ugl, uvl)[half][par]
                    cb = (cgl, cvl)[half][par]
                    ubk = ("ug%d", "uv%d")[half] % par
                    cbk = ("cg%d", "cv%d")[half] % par
                    ch = i + NFC * half
                    for k in range(8):
                        mm(ps[bank][:, 0:T], wst[par][:, k, 128 * half:128 * half + 128], h2T[:, k, 0:T], k == 0, k == 7,
                           reads=[wkk, "h2T"], writes=["ps%d" % bank])
                    if not samp:
                        cp("pool", ub[:, 0:2], carry[:, ch, :], reads=["carry"], writes=[ubk])
                        act(ub[:, 2:2 + T], ps[bank][:, 0:T], AF.Copy, reads=["ps%d" % bank], writes=[ubk])
                        cp("pool", carry[:, ch, :], ub[:, T:T + 2], reads=[ubk], writes=["carry"])
                        v0, v1 = ub[:, 0:T], ub[:, 1:1 + T]
                        co = cb[:, 0:T]
                        act(co, ps[bank][:, 0:T], AF.Identity, reads=["ps%d" % bank, "wft"], writes=[cbk],
                            scale=wft[:, ch, 2:3], bias=wft[:, ch, 3:4])
                    else:
                        u3 = ub[:, 0:160].rearrange("p (s j) -> p s j", j=10)
                        cp("pool", u3[:, :, 0:2], carry_s[:, ch, :, :], reads=["carry_s"], writes=[ubk])
                        act(u3[:, :, 2:10], ps[bank][:, 0:128].rearrange("p (s j) -> p s j", j=8), AF.Copy,
                            reads=["ps%d" % bank], writes=[ubk])
                        cp("pool", carry_s[:, ch, :, :], u3[:, :, 8:10], reads=[ubk], writes=["carry_s"])
                        v0, v1 = u3[:, :, 0:8], u3[:, :, 1:9]
                        co = cb[:, 0:128].rearrange("p (s j) -> p s j", j=8)
                        act(co, ps[bank][:, 0:128].rearrange("p (s j) -> p s j", j=8), AF.Identity,
                            reads=["ps%d" % bank, "wft"], writes=[cbk], scale=wft[:, ch, 2:3], bias=wft[:, ch, 3:4])
                    taps.append((co, v0, v1, ch, ubk, cbk))
                for kk_ in (1, 0):
                    for (co, v0, v1, ch, ubk, cbk) in taps:
                        stt("dve", co, (v1 if kk_ == 1 else v0), wft[:, ch, kk_:kk_ + 1], co, ALU.mult, ALU.add,
                            reads=[ubk, "wft", cbk], writes=[cbk])
                cgb, cvb = cgl[par], cvl[par]
                act(cgb[:, 0:T], cgb[:, 0:T], AF.Silu, reads=["cg%d" % par], writes=["cg%d" % par])
                tt(cfg.get("C_TTENG", "dve"), actT[:, i, 0:T], cgb[:, 0:T], cvb[:, 0:T], ALU.mult, reads=["cg%d" % par, "cv%d" % par], writes=["actT"])
            if st == NST - 1:
                for g in range(11):
                    for i in range(4):
                        ch = 4 * g + i
                        tr(ps[0][:2, 128 * i:128 * i + 128], carry[:, ch, :], identf[:, :],
                           reads=["carry", "identf"], writes=["ps0"], sig=(i == 3))
                    cp("act", stfl[g % 2][:2, :], ps[0][:2, :], reads=["ps0"], writes=["stf%d" % (g % 2)])
                    dma("sp", ffn_p[:, 512 * g:512 * g + 512], stfl[g % 2][:2, :], reads=["stf%d" % (g % 2)])
            if samp:
                for g in range(11):
                    for i in range(4):
                        ch = 4 * g + i
                        tr(ps[0][:32, 128 * i:128 * i + 128], carry_s[:, ch, :, :].rearrange("p s j -> p (s j)"), identf[:, :],
                           reads=["carry_s", "identf"], writes=["ps0"], sig=(i == 3))
                    cp("act", stfl[g % 2][:32, :], ps[0][:32, :], reads=["ps0"], writes=["stf%d" % (g % 2)])
                    dma("sp", ffn_s[:, 512 * g:512 * g + 512], stfl[g % 2][:32, :], reads=["stf%d" % (g % 2)])
            for j in range(ntile):
                t = 4 * st + j
                x2 = x2l[j % 2]
                x2k = "x2%d" % (j % 2)
                pb = (1, 2) if j % 2 == 0 else (3, 4)
                for n in range(2):
                    for i in range(NFC):
                        mm(ps[pb[n]][:, :], actT[:, i, 128 * j:128 * j + 128], wdn[:, i, 512 * n:512 * n + 512], i == 0, i == NFC - 1,
                           reads=["actT", "wdn"], writes=["ps%d" % pb[n]])
                for n in range(2):
                    tt("dve", x2[:, 512 * n:512 * n + 512], ps[pb[n]][:, :], x1[:, j, 512 * n:512 * n + 512], ALU.add,
                       reads=["ps%d" % pb[n], "x1_%d" % j], writes=[x2k])
                act(x1n[:, :], x2[:, :], AF.Square, reads=[x2k], writes=["x1n", "sy_ss"], accum_out=s2(6))
                rstd_from_ss(s2(6), D, s2(7), s2(8), "sy_")
                stt("dve", x2[:, :], x2[:, :], s2(8), gfb[:, :], ALU.mult, ALU.mult, reads=[x2k, "sy_", "gfb"], writes=[x2k])
                dma("sp", (y_s if samp else y_p[128 * t:128 * t + 128, :]), x2[:, :], reads=[x2k])


    def phaseC_old():
        A.reset(m_persist)
        P.barrier(lambda e: e.memset(bar_t[:, :], 0.0))
        guardC = []
        wdn = A.alloc([128, NFC, D], BF, "wdn")
        wst = [A.alloc([128, 8, 256], BF, "wst%d" % i) for i in range(2)]
        x1 = A.alloc([128, 4, D], F32, "x1")
        x1_off = A.last
        h2T = A.alloc([128, 8, 512], BF, "h2T")
        actT = A.alloc([128, NFC, 512], BF, "actT")
        wos = A.at([128, 8, D], BF, "wos", A.last)
        ug = A.alloc([128, 640], F32, "ug")
        uv = A.alloc([128, 640], F32, "uv")
        cg = A.alloc([128, 512], F32, "cg")
        cv = A.alloc([128, 512], F32, "cv")
        sg = A.alloc([128, 512], F32, "sg")
        xt2 = A.alloc([128, D], F32, "xt2")
        x1n = A.alloc([128, D], BF, "x1n")
        aT = A.alloc([128, 4, 128], BF, "aT")
        junk2 = A.alloc([128, D], BF, "junk2")
        carry = A.alloc([128, 44, 2], F32, "carry")
        carry_s = A.at([128, 44, SPC, 2], F32, "carry_s", x1_off + 4096)
        wft = A.alloc([128, 44, 4], F32, "wft")
        gfb = A.alloc([128, D], F32, "gfb")
        sm2 = A.alloc([128, 16], F32, "sm2")
        stfl = [A.alloc([32, 512], F32, "stf%d" % i) for i in range(2)]
        x2 = A.alloc([128, D], F32, "x2")
        assert A.mark() < 229000, A.mark()

        dma("pool", wdn[:, :, :], w_dn, reads=guardC, writes=["wdn"])
        dma("sp", wft[:, :, :], wf, reads=guardC, writes=["wft"])
        dma("sp", gfb[:, :], rowv[:, 1280:2304].partition_broadcast(128), reads=guardC, writes=["gfb"])
        memset("pool", carry[:, :, :], 0.0, writes=["carry"])
        s2 = lambda i: sm2[:, i:i + 1]
        GC = guardC

        for st in range(NST + 1):
            samp = (st == NST)
            ntile = 1 if samp else 4
            T = 128 * ntile
            dma("pool", wos[:, :, :], w_o, reads=GC, writes=["actT"])
            if samp:
                for g in range(11):
                    sb_ = stfl[g % 2]
                    sk_ = "stf%d" % (g % 2)
                    dma("sp", sb_[:, :], stf[:, 512 * g:512 * g + 512], reads=GC, writes=[sk_])
                    for i in range(4):
                        tr(ps[0][:, 32 * i:32 * i + 32], sb_[:, 128 * i:128 * i + 128], identf[:32, :32],
                           reads=[sk_, "identf"], writes=["ps0"], sig=(i == 3))
                    cp("dve", carry_s[:, 4 * g:4 * g + 4, :, :].rearrange("p c s j -> p c (s j)"),
                       ps[0][:, 0:128].rearrange("p (c n) -> p c n", n=32), reads=["ps0"] + GC, writes=["carry_s", "x1_1", "x1_2"])
            for j in range(ntile):
                t = 4 * st + j
                dma("sp", xt2[:, :], (xs if samp else xp[128 * t:128 * t + 128, :]), reads=GC, writes=["xt2"])
                act(junk2[:, 0:512], attn_tok[:, t, :], AF.Square, reads=["attn_tok"] + GC, writes=["junk2", "sa_ss"], accum_out=s2(0))
                rstd_from_ss(s2(0), 512, s2(1), s2(2), "sa_")
                for c in range(4):
                    tr(psb(0)[:, 128 * c:128 * c + 128], attn_tok[:, t, 128 * c:128 * c + 128], identb[:, :],
                       reads=["attn_tok", "identb"], writes=["ps0"], sig=(c == 3))
                tt("dve", aT[:, :, :], psb(0)[:, 0:512].rearrange("p (k n) -> p k n", n=128),
                   bcast(vec[:, GAO:GAO + 4], [128, 4, 128]), ALU.mult, reads=["ps0", "vec"] + GC, writes=["aT"])
                for n in range(2):
                    for c in range(4):
                        mm(ps[1 + n][:, :], aT[:, c, :], wos[:, c, 512 * n:512 * n + 512], c == 0, c == 3,
                           reads=["aT", "actT"], writes=["ps%d" % (1 + n)])
                for n in range(2):
                    for c in range(4):
                        mm(ps[3 + n][:, :], convT[:, c, 128 * t:128 * t + 128], wos[:, 4 + c, 512 * n:512 * n + 512], c == 0, c == 3,
                           reads=["convT", "actT"], writes=["ps%d" % (3 + n)])
                for n in range(2):
                    stt("dve", x1[:, j, 512 * n:512 * n + 512], ps[1 + n][:, :], s2(2), xt2[:, 512 * n:512 * n + 512], ALU.mult, ALU.add,
                        reads=["ps%d" % (1 + n), "sa_", "xt2"] + GC, writes=["x1_%d" % j])
                    stt("dve", x1[:, j, 512 * n:512 * n + 512], ps[3 + n][:, :], rstd_c[:, t:t + 1], x1[:, j, 512 * n:512 * n + 512],
                        ALU.mult, ALU.add, reads=["ps%d" % (3 + n), "ss3", "x1_%d" % j], writes=["x1_%d" % j])
                act(junk2[:, :], x1[:, j, :], AF.Square, reads=["x1_%d" % j], writes=["junk2", "sf_ss"], accum_out=s2(3))
                rstd_from_ss(s2(3), D, s2(4), s2(5), "sf_")
                act(x1n[:, :], x1[:, j, :], AF.Copy, reads=["x1_%d" % j, "sf_"], writes=["x1n"], scale=s2(5))
                for k in range(8):
                    tr(psb(5)[:, 128 * k:128 * k + 128], x1n[:, 128 * k:128 * k + 128], identb[:, :],
                       reads=["x1n", "identb"], writes=["ps5"], sig=(k == 7))
                tt("dve", h2T[:, :, 128 * j:128 * j + 128], psb(5)[:, :].rearrange("p (k n) -> p k n", n=128),
                   bcast(vec[:, GF:GF + 8], [128, 8, 128]), ALU.mult, reads=["ps5", "vec"] + GC, writes=["h2T"])
            for i in range(NFC):
                wsl = (st * NFC + i) % 2
                wkk = "wst%d" % wsl
                dma("pool", wst[wsl][:, :, :], w_up[i], reads=GC, writes=[wkk])
                for half, (bank, ub, ubk, cb, cbk, eng) in enumerate(((6, ug, "ug", cg, "cg", "dve"), (7, uv, "uv", cv, "cv", "pool"))):
                    ch = i + NFC * half
                    for k in range(8):
                        mm(ps[bank][:, 0:T], wst[wsl][:, k, 128 * half:128 * half + 128], h2T[:, k, 0:T], k == 0, k == 7,
                           reads=[wkk, "h2T"], writes=["ps%d" % bank])
                    if not samp:
                        cp("pool", ub[:, 0:2], carry[:, ch, :], reads=["carry", cbk] + GC, writes=[ubk])
                        act(ub[:, 2:2 + T], ps[bank][:, 0:T], AF.Copy, reads=["ps%d" % bank, cbk] + GC, writes=[ubk])
                        cp("pool", carry[:, ch, :], ub[:, T:T + 2], reads=[ubk], writes=["carry"])
                        v0, v1, v2 = ub[:, 0:T], ub[:, 1:1 + T], ub[:, 2:2 + T]
                        co = cb[:, 0:T]
                    else:
                        u3 = ub[:, 0:160].rearrange("p (s j) -> p s j", j=10)
                        cp("pool", u3[:, :, 0:2], carry_s[:, ch, :, :], reads=["carry_s", cbk] + GC, writes=[ubk])
                        act(u3[:, :, 2:10], ps[bank][:, 0:128].rearrange("p (s j) -> p s j", j=8), AF.Copy,
                            reads=["ps%d" % bank, cbk] + GC, writes=[ubk])
                        cp("pool", carry_s[:, ch, :, :], u3[:, :, 8:10], reads=[ubk], writes=["carry_s"])
                        v0, v1, v2 = u3[:, :, 0:8], u3[:, :, 1:9], u3[:, :, 2:10]
                        co = cb[:, 0:128].rearrange("p (s j) -> p s j", j=8)
                    ts(eng, co, v2, wft[:, ch, 2:3], wft[:, ch, 3:4], ALU.mult, ALU.add, reads=[ubk, "wft"], writes=[cbk])
                    stt(eng, co, v1, wft[:, ch, 1:2], co, ALU.mult, ALU.add, reads=[ubk, "wft", cbk], writes=[cbk])
                    stt(eng, co, v0, wft[:, ch, 0:1], co, ALU.mult, ALU.add, reads=[ubk, "wft", cbk], writes=[cbk])
                act(sg[:, 0:T], cg[:, 0:T], AF.Silu, reads=["cg"] + GC, writes=["sg"])
                tt("dve", actT[:, i, 0:T], sg[:, 0:T], cv[:, 0:T], ALU.mult, reads=["sg", "cv"] + GC, writes=["actT"])
            if st == NST - 1:
                for g in range(11):
                    for i in range(4):
                        ch = 4 * g + i
                        tr(ps[0][:2, 128 * i:128 * i + 128], carry[:, ch, :], identf[:, :],
                           reads=["carry", "identf"], writes=["ps0"], sig=(i == 3))
                    cp("act", stfl[g % 2][:2, :], ps[0][:2, :], reads=["ps0"], writes=["stf%d" % (g % 2)])
                    dma("sp", ffn_p[:, 512 * g:512 * g + 512], stfl[g % 2][:2, :], reads=["stf%d" % (g % 2)])
            if samp:
                for g in range(11):
                    for i in range(4):
                        ch = 4 * g + i
                        tr(ps[0][:32, 128 * i:128 * i + 128], carry_s[:, ch, :, :].rearrange("p s j -> p (s j)"), identf[:, :],
                           reads=["carry_s", "identf"], writes=["ps0"], sig=(i == 3))
                    cp("act", stfl[g % 2][:32, :], ps[0][:32, :], reads=["ps0"], writes=["stf%d" % (g % 2)])
                    dma("sp", ffn_s[:, 512 * g:512 * g + 512], stfl[g % 2][:32, :], reads=["stf%d" % (g % 2)])
            for j in range(ntile):
                t = 4 * st + j
                for n in range(2):
                    for i in range(NFC):
                        mm(ps[1 + n][:, :], actT[:, i, 128 * j:128 * j + 128], wdn[:, i, 512 * n:512 * n + 512], i == 0, i == NFC - 1,
                           reads=["actT", "wdn"], writes=["ps%d" % (1 + n)])
                for n in range(2):
                    tt("dve", x2[:, 512 * n:512 * n + 512], ps[1 + n][:, :], x1[:, j, 512 * n:512 * n + 512], ALU.add,
                       reads=["ps%d" % (1 + n), "x1_%d" % j] + GC, writes=["x2"])
                act(junk2[:, :], x2[:, :], AF.Square, reads=["x2"], writes=["junk2", "sy_ss"], accum_out=s2(6))
                rstd_from_ss(s2(6), D, s2(7), s2(8), "sy_")
                stt("dve", x2[:, :], x2[:, :], s2(8), gfb[:, :], ALU.mult, ALU.mult, reads=["x2", "sy_", "gfb"], writes=["x2"])
                dma("sp", (y_s if samp else y_p[128 * t:128 * t + 128, :]), x2[:, :], reads=["x2"])


    if cfg.get("CMODE", "new") == "new":
        phaseC_new()
    else:
        phaseC_old()
    P.emit(nc)
    return nc


def _bf(a):
    return np.ascontiguousarray(a).astype(ml_dtypes.bfloat16)


def host_consts(cfg):
    SEQ, PAST = cfg["SEQ"], cfg["PAST"]
    NT = SEQ + 128
    pos = np.concatenate([np.arange(SEQ), PAST + (np.arange(128) % DSEQ)]).astype(np.float32)
    inv = (np.float32(10000.0) ** (-(np.arange(16, dtype=np.float32) * np.float32(2.0) / np.float32(DR)))).astype(np.float32)
    ang = (pos[:, None] * inv[None, :]).astype(np.float32)
    cos = np.cos(ang).astype(np.float32)
    sin = np.sin(ang).astype(np.float32)
    cstok = np.concatenate([cos, sin], axis=1)
    cosT = np.concatenate([cos, cos], axis=1).T.copy()
    sinT = np.concatenate([-sin, sin], axis=1).T.copy()
    tri = (np.arange(128)[:, None] <= np.arange(128)[None, :]).astype(np.float32)
    k = np.arange(128)
    sm = np.zeros((128, SPC, NH, DSEQ), np.float32)
    for s in range(SPC):
        for t in range(DSEQ):
            sm[(k // DSEQ == s) & (k % DSEQ <= t), s, :, t] = 1.0
    return dict(cstok=cstok, cosT=cosT, sinT=sinT, identf=np.eye(128, dtype=np.float32), identb=_bf(np.eye(128)),
                tri=_bf(tri), smask=_bf(sm.reshape(128, SPC * 64)))


def host_weights(inp):
    f = np.float32
    w_in = inp["w_in"][0].reshape(8, 128, DIN).transpose(1, 0, 2)
    wuq = inp["w_uq"][0].reshape(DCQ, NH * 96)
    wuqs = inp["w_uq"][0].copy()
    wuqs[:, :, 64:80] = inp["w_uq"][0][:, :, 80:96]
    wuqs[:, :, 80:96] = inp["w_uq"][0][:, :, 64:80]
    wuqs = wuqs.reshape(DCQ, NH * 96)
    wk = np.zeros((3, 128, NH, 96), f)
    wk[0, :, :, 0:64] = inp["w_uk"][0][0:128]
    wk[1, :, :, 0:64] = inp["w_uk"][0][128:256]
    for h in range(NH):
        wk[2, 64:96, h, 64:96] = np.eye(32, dtype=f)
    vec = np.zeros((128, 64), f)
    vec[:, 0:8] = inp["g_attn_norm"][0].reshape(8, 128).T
    vec[:, 8:11] = inp["g_q_norm"][0].reshape(3, 128).T
    vec[:, 11:15] = inp["g_conv_out"][0].reshape(4, 128).T
    vec[:, 15:19] = inp["g_attn_out"][0].reshape(4, 128).T
    vec[:, 19:27] = inp["g_ffn_norm"][0].reshape(8, 128).T
    vec[:, 27:31] = inp["b_dw"][0].reshape(4, 128).T
    wfv = np.zeros((128, 44, 4), f)
    wfv[:, :, 0:3] = inp["w_ffn_dw"][0].reshape(3, 44, 128).transpose(2, 1, 0)
    wfv[:, :, 3] = inp["b_ffn_dw"][0].reshape(44, 128).T
    rowv = np.zeros((1, 3072), f)
    rowv[0, 0:256] = inp["g_kv_norm"][0]
    rowv[0, 256:768] = inp["g_conv_ln"][0]
    rowv[0, 768:1280] = inp["b_conv_ln"][0]
    rowv[0, 1280:2304] = inp["g_final"]
    rowv[0, 2304:2816] = inp["b_dw"][0]
    wup = inp["w_up"][0].reshape(8, 128, 2, NFC, 128).transpose(3, 1, 0, 2, 4).reshape(NFC, 128, 8, 256)
    c = np.ascontiguousarray
    return dict(
        w_in=c(w_in), w_uq=c(wuq.reshape(3, 128, 768).transpose(1, 0, 2)), w_uqs=c(wuqs.reshape(3, 128, 768).transpose(1, 0, 2)),
        w_k=c(wk.reshape(3, 128, 768).transpose(1, 0, 2)), w_ukT=c(inp["w_uk"][0].transpose(2, 1, 0)),
        w_uv=c(inp["w_uv"][0].reshape(2, 128, 512).transpose(1, 0, 2)),
        w_o=c(inp["w_o"][0].reshape(8, 128, D).transpose(1, 0, 2)), w_up=c(wup),
        w_dn=c(inp["w_down"][0].reshape(NFC, 128, D).transpose(1, 0, 2)),
        wdw=c(inp["w_dw"][0].reshape(CW, 4, 128).transpose(2, 1, 0)), vecs=vec, wf=wfv, rowv=rowv)


_NC_CACHE = {}


def run(inp, cfg):
    SEQ, NPG, NPOOL = cfg["SEQ"], cfg["NPG"], cfg["NPOOL"]
    key = tuple(sorted(cfg.items()))
    if key not in _NC_CACHE:
        _NC_CACHE[key] = build(cfg)
    nc = _NC_CACHE[key]
    consts = host_consts(cfg)
    wts = host_weights(inp)
    cache2d = np.ascontiguousarray(inp["cache_kv_latent"][0]).reshape(NPOOL * PAGE, DC + DR)
    in_maps = []
    for c in range(8):
        m = dict(consts)
        m.update(wts)
        m["xp"] = np.ascontiguousarray(inp["x_prompt"][c])
        m["xs"] = np.ascontiguousarray(inp["x_sample"][SPC * c:SPC * c + SPC]).reshape(128, D)
        m["cache"] = cache2d
        m["stc"] = np.ascontiguousarray(inp["state_conv"][0, SPC * c:SPC * c + SPC]).reshape(SPC * 30, DCONV)
        m["stf"] = np.ascontiguousarray(inp["state_ffn_conv"][0, SPC * c:SPC * c + SPC]).reshape(SPC * 2, 2 * DFF)
        m["ptab"] = np.ascontiguousarray(inp["page_table"][SPC * c:SPC * c + SPC]).reshape(1, SPC * NPG).astype(np.int32)
        in_maps.append(m)
    res = run_bass_kernel_spmd(nc, in_maps, core_ids=list(range(8)))
    R = res.results
    y_p = np.stack([R[c]["y_p"] for c in range(8)])
    y_s = np.concatenate([R[c]["y_s"].reshape(SPC, DSEQ, D) for c in range(8)])
    kv_p = np.stack([R[c]["kv_p"] for c in range(8)])[None]
    conv_p = np.stack([R[c]["conv_p"] for c in range(8)])[None]
    ffn_p = np.stack([R[c]["ffn_p"] for c in range(8)])[None]
    kv_s = np.concatenate([R[c]["kv_s"].reshape(SPC, DSEQ, DC + DR) for c in range(8)])[None]
    conv_s = np.concatenate([R[c]["conv_s"] for c in range(8)])[None]
    ffn_s = np.concatenate([R[c]["ffn_s"].reshape(SPC, 2, 2 * DFF) for c in range(8)])[None]
    return tuple(np.asarray(a, np.float32) for a in (y_p, y_s, kv_p, conv_p, ffn_p, kv_s, conv_s, ffn_s))


def kernel(**inputs):
    inp = {k: np.asarray(v) for k, v in inputs.items()}
    return run(inp, CFG_FULL)
```

```python
import numpy as np
import ml_dtypes
import concourse.bass as bass
import concourse.mybir as mybir
from concourse.bass_utils import run_bass_kernel_spmd

F32 = mybir.dt.float32
BF = mybir.dt.bfloat16
I32 = mybir.dt.int32
AF = mybir.ActivationFunctionType
ALU = mybir.AluOpType

D = 1024
NH = 8
DC = 256
DCQ = 384
DR = 32
DCONV = 512
CW = 31
DIN = 1696
DFF = 2816
NFC = 22
EPS = 1e-6
SCALE = 96.0 ** -0.5
SPC = 16
DSEQ = 8
PAGE = 128

CFG_FULL = dict(SEQ=4096, NPG=128, NPOOL=20480, PAST=16384)


class Prog:
    ENGS = ("pe", "act", "dve", "pool", "sp")
    RING = 8

    def __init__(self):
        self.ops = []
        self.lastw = {}
        self.readers = {}
        self.cnt = {e: 0 for e in self.ENGS}
        self.dcnt = {e: 0 for e in self.ENGS}
        self.pending_pe = []
        self.cur_barrier = None

    def barrier(self, fn):
        keys = list(set(self.lastw) | set(self.readers))
        o = self.op("pool", fn, writes=keys)
        self.cur_barrier = o["idx"]

    def op(self, eng, fn, reads=(), writes=(), dma=False, sig=True):
        o = dict(eng=eng, fn=fn, dma=dma, sig=sig, deps=set(), idx=len(self.ops))
        deps = set()
        if self.cur_barrier is not None:
            deps.add(self.cur_barrier)
        for r in reads:
            if r in self.lastw:
                deps.add(self.lastw[r])
        for w in writes:
            if w in self.lastw:
                deps.add(self.lastw[w])
            for rd in self.readers.get(w, ()):
                deps.add(rd)
        deps.discard(o["idx"])
        o["deps"] = deps
        if dma:
            i = self.dcnt[eng]
            self.dcnt[eng] += 1
            o["dsem"] = (eng, i % self.RING)
            o["dval"] = 16 * (i // self.RING + 1)
            o["dprev"] = 16 * (i // self.RING)
        else:
            if sig:
                self.cnt[eng] += 1
                o["sval"] = self.cnt[eng]
                if eng == "pe":
                    for p in self.pending_pe:
                        p["sval"] = self.cnt[eng]
                    self.pending_pe = []
            else:
                assert eng == "pe"
                self.pending_pe.append(o)
        self.ops.append(o)
        for w in writes:
            self.lastw[w] = o["idx"]
            self.readers[w] = []
        for r in reads:
            if r not in writes:
                self.readers.setdefault(r, []).append(o["idx"])
        return o

    def check(self):
        assert not self.pending_pe
        ops = self.ops
        queues = {e: [o for o in ops if o["eng"] == e] for e in self.ENGS}
        pos = {e: 0 for e in self.ENGS}
        done = [False] * len(ops)
        sigdone = {e: 0 for e in self.ENGS}
        ddone = {}
        dq = {e: [o for o in queues[e] if o["dma"]] for e in self.ENGS}
        progress = True
        while progress:
            progress = False
            for e in self.ENGS:
                while pos[e] < len(queues[e]):
                    o = queues[e][pos[e]]
                    ok = True
                    for d in o["deps"]:
                        do = ops[d]
                        if do["dma"]:
                            if not done[d]:
                                ok = False
                                break
                        elif do["eng"] == "pe" and e == "pe":
                            continue
                        elif sigdone[do["eng"]] < do["sval"]:
                            ok = False
                            break
                    if not ok:
                        break
                    done[o["idx"]] = True
                    if not o["dma"] and o["sig"]:
                        sigdone[e] = o["sval"]
                    pos[e] += 1
                    progress = True
        stuck = {e: (pos[e], len(queues[e])) for e in self.ENGS if pos[e] < len(queues[e])}
        if stuck:
            for e in stuck:
                o = queues[e][pos[e]]
                print("STUCK", e, o["idx"], [(d, ops[d]["eng"], ops[d].get("sval"), done[d]) for d in o["deps"] if not (done[d] and (ops[d]["dma"] or sigdone[ops[d]["eng"]] >= ops[d].get("sval", 0)))][:6])
            raise RuntimeError("deadlock in program: %s" % stuck)

    def emit(self, nc):
        assert not self.pending_pe
        self.check()
        import contextlib
        with contextlib.ExitStack() as es:
            csem = {e: es.enter_context(nc.semaphore("c_" + e)) for e in ("pe", "act", "dve", "pool")}
            dsem = {}
            for e in ("sp", "pool", "act"):
                if self.dcnt[e]:
                    for r in range(self.RING):
                        dsem[(e, r)] = es.enter_context(nc.semaphore("d_%s%d" % (e, r)))
            block = es.enter_context(nc.Block())
            ops = self.ops

            def run(engname, eng):
                waited = {}

                def wait(sem, key, val):
                    if waited.get(key, 0) >= val:
                        return
                    waited[key] = val
                    eng.wait_ge(sem, val)

                for o in ops:
                    if o["eng"] != engname:
                        continue
                    for d in sorted(o["deps"]):
                        do = ops[d]
                        if do["dma"]:
                            wait(dsem[do["dsem"]], do["dsem"], do["dval"])
                        else:
                            if do["eng"] == "pe" and engname == "pe":
                                continue
                            wait(csem[do["eng"]], do["eng"], do["sval"])
                    if o["dma"]:
                        if o["dprev"] > 0:
                            wait(dsem[o["dsem"]], o["dsem"], o["dprev"])
                        ins = o["fn"](eng)
                        ins.then_inc(dsem[o["dsem"]], 16)
                    else:
                        ins = o["fn"](eng)
                        if o["sig"]:
                            ins.then_inc(csem[engname], 1)
                if engname == "sp":
                    for (e, r), s in dsem.items():
                        n = self.dcnt[e]
                        k = (n - 1 - r) // self.RING + 1 if n > r else 0
                        if k > 0:
                            eng.wait_ge(s, 16 * k)
                    for e in ("pe", "act", "dve", "pool"):
                        if self.cnt[e]:
                            eng.wait_ge(csem[e], self.cnt[e])

            @block.sync
            def _(e):
                run("sp", e)

            @block.tensor
            def _(e):
                run("pe", e)

            @block.scalar
            def _(e):
                run("act", e)

            @block.vector
            def _(e):
                run("dve", e)

            @block.gpsimd
            def _(e):
                run("pool", e)


def build(cfg):
    SEQ, NPG, NPOOL, PAST = cfg["SEQ"], cfg["NPG"], cfg["NPOOL"], cfg["PAST"]
    NPT = SEQ // 128
    NTL = NPT + 1
    NT = NTL * 128
    NST = SEQ // 512
    NPAGES = SPC * NPG

    nc = bass.Bass("TRN2", target_bir_lowering=False)
    P = Prog()

    def din(name, shape, dt=F32):
        return nc.dram_tensor(name, list(shape), dt, kind="ExternalInput").ap()

    def dout(name, shape, dt=F32):
        return nc.dram_tensor(name, list(shape), dt, kind="ExternalOutput").ap()

    xp = din("xp", [SEQ, D])
    xs = din("xs", [128, D])
    cache = din("cache", [NPOOL * PAGE, DC + DR])
    stc = din("stc", [SPC * 30, DCONV])
    stf = din("stf", [SPC * 2, 2 * DFF])
    ptab = din("ptab", [1, NPAGES], I32)
    w_in = din("w_in", [128, 8, DIN])
    w_uq = din("w_uq", [128, 3, 768])
    w_uqs = din("w_uqs", [128, 3, 768])
    w_k = din("w_k", [128, 3, 768])
    w_ukT = din("w_ukT", [64, NH, DC])
    w_uv = din("w_uv", [128, 2, 512])
    w_o = din("w_o", [128, 8, D])
    w_up = din("w_up", [NFC, 128, 8, 256])
    w_dn = din("w_dn", [128, NFC, D])
    wdw = din("wdw", [128, 4, CW])
    vecs = din("vecs", [128, 64])
    wf = din("wf", [128, 44, 4])
    rowv = din("rowv", [1, 3072])
    cstok = din("cstok", [NT, 32])
    cosT = din("cosT", [32, NT])
    sinT = din("sinT", [32, NT])
    identf_d = din("identf", [128, 128])
    identb_d = din("identb", [128, 128], BF)
    tri_d = din("tri", [128, 128], BF)
    smask_d = din("smask", [128, SPC * 64], BF)

    y_p = dout("y_p", [SEQ, D])
    y_s = dout("y_s", [128, D])
    kv_p = dout("kv_p", [SEQ, DC + DR])
    conv_p = dout("conv_p", [30, DCONV])
    ffn_p = dout("ffn_p", [2, 2 * DFF])
    kv_s = dout("kv_s", [128, DC + DR])
    conv_s = dout("conv_s", [SPC, 30, DCONV])
    ffn_s = dout("ffn_s", [SPC * 2, 2 * DFF])

    class Arena:
        def __init__(self):
            self.off = 16512
            self.n = 0

        def alloc(self, shape, dt, name=None):
            esz = 4 if dt in (F32, I32) else 2
            nbytes = int(np.prod(shape[1:])) * esz
            self.off = (self.off + 63) // 64 * 64
            self.n += 1
            t = nc.alloc_sbuf_tensor_at("sb%d_%s" % (self.n, name or ""), list(shape), dt, offset=self.off)
            self.last = self.off
            self.off += nbytes
            assert self.off <= 229376, ("SBUF overflow", self.off, name)
            return t

        def at(self, shape, dt, name, offset):
            self.n += 1
            return nc.alloc_sbuf_tensor_at("sb%d_%s" % (self.n, name), list(shape), dt, offset=offset)

        def mark(self):
            return self.off

        def reset(self, m):
            self.off = m

    A = Arena()
    identf = A.alloc([128, 128], F32, "identf")
    identb = A.alloc([128, 128], BF, "identb")
    tri = A.alloc([128, 128], BF, "tri")
    vec = A.alloc([128, 64], F32, "vec")
    epsT = A.alloc([128, 1], F32, "eps")
    rstd_c = A.alloc([128, NTL], F32, "rstdc")
    bar_t = A.alloc([128, 2], F32, "bar")
    GA, GQ, GCO, GAO, GF, BDW = 0, 8, 11, 15, 19, 27
    convT = A.alloc([128, 4, NT], BF, "convT")
    attn_tok = A.alloc([128, max(NTL, CW), 512], BF, "attn_tok")
    attn_off = A.last
    m_persist = A.mark()
    cqnT = A.alloc([128, 3, NT], BF, "cqnT")
    kvT = A.alloc([128, 3, NT], BF, "kvT")
    kvbf_s = A.alloc([128, 289], BF, "kvbf_s")
    m_ab = A.mark()

    ps = [nc.alloc_psum_tensor("ps%d" % i, [128, 512], F32) for i in range(8)]

    def psb(i):
        return ps[i][:, :].bitcast(BF)

    def dma(q, out, in_, reads=(), writes=()):
        def f(e):
            return e.dma_start(out=out, in_=in_)
        return P.op(q, f, reads=reads, writes=writes, dma=True)

    def mm(out, lhsT, rhs, start, stop, reads=(), writes=(), sig=None):
        def f(e):
            return e.matmul(out, lhsT, rhs, start=start, stop=stop)
        return P.op("pe", f, reads=reads, writes=writes, sig=(stop if sig is None else sig))

    def tr(out, in_, ident, reads=(), writes=(), sig=True):
        def f(e):
            return e.transpose(out, in_, ident)
        return P.op("pe", f, reads=reads, writes=writes, sig=sig)

    def act(out, in_, func, reads=(), writes=(), **kw):
        def f(e):
            return e.activation(out=out, in_=in_, func=func, **kw)
        return P.op("act", f, reads=reads, writes=writes)

    def tt(eng, out, in0, in1, op, reads=(), writes=()):
        def f(e):
            return e.tensor_tensor(out=out, in0=in0, in1=in1, op=op)
        return P.op(eng, f, reads=reads, writes=writes)

    def ts(eng, out, in0, s1, s2, op0, op1=None, reads=(), writes=()):
        def f(e):
            if op1 is None:
                return e.tensor_scalar(out=out, in0=in0, scalar1=s1, scalar2=None, op0=op0)
            return e.tensor_scalar(out=out, in0=in0, scalar1=s1, scalar2=s2, op0=op0, op1=op1)
        return P.op(eng, f, reads=reads, writes=writes)

    def stt(eng, out, in0, scalar, in1, op0, op1, reads=(), writes=()):
        eng = "dve"

        def f(e):
            return e.scalar_tensor_tensor(out=out, in0=in0, scalar=scalar, in1=in1, op0=op0, op1=op1)
        return P.op(eng, f, reads=reads, writes=writes)

    def cp(eng, out, in_, reads=(), writes=()):
        if eng == "act":
            return act(out, in_, AF.Copy, reads=reads, writes=writes)

        def f(e):
            return e.tensor_copy(out=out, in_=in_)
        return P.op(eng, f, reads=reads, writes=writes)

    def memset(eng, ap, val, writes=()):
        def f(e):
            return e.memset(ap, val)
        return P.op(eng, f, writes=writes)

    def recip(out, in_, reads=(), writes=()):
        def f(e):
            return e.reciprocal(out=out, in_=in_)
        return P.op("dve", f, reads=reads, writes=writes)

    def rstd_from_ss(ss, n, tmp, out, key):
        act(tmp, ss, AF.Sqrt, reads=[key + "ss", "eps"], writes=[key + "tmp"], scale=1.0 / n, bias=epsT[:, 0:1])
        recip(out, tmp, reads=[key + "tmp"], writes=[key])

    def bcast(v, shape):
        return v.unsqueeze(2).to_broadcast(list(shape))

    dma("sp", identf[:, :], identf_d, writes=["identf"])
    dma("sp", identb[:, :], identb_d, writes=["identb"])
    dma("sp", tri[:, :], tri_d, writes=["tri"])
    dma("sp", vec[:, :], vecs, writes=["vec"])
    memset("dve", epsT[:, :], EPS, writes=["eps"])

    win = A.alloc([128, 8, DIN], BF, "win")
    xt = [A.alloc([128, D], F32, "xt%d" % i) for i in range(2)]
    junk = A.alloc([128, D], BF, "junk")
    xnb = A.alloc([128, D], BF, "xnb")
    hT = A.alloc([128, 8, 128], BF, "hT")
    small = A.alloc([128, 32], F32, "small")
    cqnb = A.alloc([128, DCQ], BF, "cqnb")
    kvrow = [A.alloc([128, 288], F32, "kvrow%d" % i) for i in range(2)]
    kvbf = A.alloc([128, 288], BF, "kvbf")
    krs = A.alloc([128, 32], F32, "krs")
    rtmp = A.alloc([128, 64], F32, "rtmp")
    cs = A.alloc([128, NTL, 32], F32, "cs")
    gkvb = A.alloc([128, 256], F32, "gkvb")
    glnb = A.alloc([128, 512], F32, "glnb")
    blnb = A.alloc([128, 512], F32, "blnb")
    sig_t = A.alloc([128, 4, 128], F32, "sig")
    gluT = A.alloc([128, 4, 30 + 128], F32, "gluT")
    gluS = A.alloc([128, 4, SPC, 38], F32, "gluS")
    yT = A.alloc([128, 4, 128], F32, "yT")
    zt = A.alloc([128, 512], F32, "zt")
    s32 = A.alloc([128, 512], F32, "s32")
    sbf = A.alloc([128, 512], BF, "sbf")
    wdw_t = A.alloc([128, 4, CW], F32, "wdw")
    stt_t = A.alloc([120, 512], F32, "stt")
    gtok = A.alloc([128, 512], F32, "gtok")
    bnst = A.alloc([128, 8], F32, "bnst")
    diag = A.at([128, 4, CW, 128], BF, "diag", attn_off)
    gluTbl = [A.alloc([128, 4, 30 + 128], BF, "gluTb%d" % i) for i in range(2)]
    ones1 = A.alloc([1, 128], BF, "ones1")
    bdwf = A.alloc([1, 512], F32, "bdwf")
    bdwt = A.alloc([1, 512], F32, "bdwt")
    bhi = A.alloc([1, 512], BF, "bhi")
    blo = A.alloc([1, 512], BF, "blo")
    assert A.mark() <= 229376, A.mark()

    dma("pool", win[:, :, :], w_in, writes=["win"])
    dma("sp", cs[:, :, :], cstok.rearrange("(t p) c -> p t c", p=128), writes=["cs"])
    dma("sp", gkvb[:, :], rowv[:, 0:256].partition_broadcast(128), writes=["gkvb"])
    dma("sp", glnb[:, :], rowv[:, 256:768].partition_broadcast(128), writes=["glnb"])
    dma("sp", blnb[:, :], rowv[:, 768:1280].partition_broadcast(128), writes=["blnb"])
    dma("sp", wdw_t[:, :, :], wdw, writes=["wdw"])
    memset("pool", gluT[:, :, :], 0.0, writes=["gluT"])
    for i in range(2):
        memset("pool", gluTbl[i][:, :, :], 0.0, writes=["gluTb%d" % i])
    memset("pool", ones1[:, :], 1.0, writes=["ones1"])
    dma("sp", bdwf[:, :], rowv[:, 2304:2816], writes=["bdwf"])
    cp("dve", bhi[:, :], bdwf[:, :], reads=["bdwf"], writes=["bhi"])
    tt("dve", bdwt[:, :], bdwf[:, :], bhi[:, :], ALU.subtract, reads=["bdwf", "bhi"], writes=["bdwt"])
    cp("dve", blo[:, :], bdwt[:, :], reads=["bdwt"], writes=["blo"])
    for c in range(4):
        for k in range(CW):
            if k % 2:
                act(diag[:, c, k, :], identb[:, :], AF.Copy, reads=["identb", "wdw"], writes=["diag"], scale=wdw_t[:, c, k:k + 1])
            else:
                ts("dve", diag[:, c, k, :], identb[:, :], wdw_t[:, c, k:k + 1], None, ALU.mult,
                   reads=["identb", "wdw"], writes=["diag"])

    for r in range(4):
        dma("sp", stt_t[:, :], stc[120 * r:120 * r + 120, :], writes=["stt"])
        for c in range(4):
            tr(ps[0][:, 128 * c:128 * c + 120], stt_t[:, 128 * c:128 * c + 128], identf[:120, :120],
               reads=["stt", "identf"], writes=["ps0"], sig=(c == 3))
        for c in range(4):
            cp("dve", gluS[:, c, 4 * r:4 * r + 4, 0:30],
               ps[0][:, 128 * c:128 * c + 120].rearrange("p (s j) -> p s j", j=30),
               reads=["ps0"], writes=["gluS"])
    dma("sp", conv_s[:, 0:22, :], stc.rearrange("(s j) c -> s j c", j=30)[:, 8:30, :])

    sm = lambda i: small[:, i:i + 1]

    def frontA(t):
        samp = (t == NPT)
        xsrc = xs if samp else xp[128 * t:128 * t + 128, :]
        x_t = xt[t % 2]
        xk = "xt%d" % (t % 2)
        dma("sp", x_t[:, :], xsrc, writes=[xk])
        act(junk[:, :], x_t[:, :], AF.Square, reads=[xk], writes=["junk", "ss0ss"], accum_out=sm(0))
        rstd_from_ss(sm(0), D, sm(1), sm(2), "ss0")
        act(xnb[:, :], x_t[:, :], AF.Copy, reads=[xk, "ss0"], writes=["xnb"], scale=sm(2))
        for k in range(8):
            tr(psb(0)[:, 128 * k:128 * k + 128], xnb[:, 128 * k:128 * k + 128], identb[:, :],
               reads=["xnb", "identb"], writes=["ps0"], sig=(k == 7))
        tt("dve", hT[:, :, :], psb(0)[:, :].rearrange("p (k n) -> p k n", n=128),
           bcast(vec[:, GA:GA + 8], [128, 8, 128]), ALU.mult, reads=["ps0", "vec"], writes=["hT"])
        for k in range(8):
            mm(ps[1][:, 0:384], hT[:, k, :], win[:, k, 0:384], k == 0, k == 7, reads=["hT", "win"], writes=["ps1"])
        for k in range(8):
            mm(ps[2][:, 0:288], hT[:, k, :], win[:, k, 384:672], k == 0, k == 7, reads=["hT", "win"], writes=["ps2"])
        for c in range(8):
            bank = 3 if c < 4 else 4
            cc = c % 4
            for k in range(8):
                mm(ps[bank][:, 128 * cc:128 * cc + 128], win[:, k, 672 + 128 * c:672 + 128 * c + 128], hT[:, k, :],
                   k == 0, k == 7, reads=["hT", "win"], writes=["ps%d" % bank], sig=(k == 7 and cc == 3))
        act(junk[:, 0:384], ps[1][:, 0:384], AF.Square, reads=["ps1"], writes=["junk", "ss1ss"], accum_out=sm(3))
        rstd_from_ss(sm(3), DCQ, sm(4), sm(5), "ss1")
        act(cqnb[:, :], ps[1][:, 0:384], AF.Copy, reads=["ps1", "ss1"], writes=["cqnb"], scale=sm(5))
        for k in range(3):
            tr(psb(5)[:, 128 * k:128 * k + 128], cqnb[:, 128 * k:128 * k + 128], identb[:, :],
               reads=["cqnb", "identb"], writes=["ps5"], sig=(k == 2))
        tt("dve", cqnT[:, :, 128 * t:128 * t + 128], psb(5)[:, 0:384].rearrange("p (k n) -> p k n", n=128),
           bcast(vec[:, GQ:GQ + 3], [128, 3, 128]), ALU.mult, reads=["ps5", "vec"], writes=["cqnT"])
        kr_ = kvrow[t % 2]
        kk = "kvrow%d" % (t % 2)
        act(junk[:, 0:256], ps[2][:, 0:256], AF.Square, reads=["ps2"], writes=["junk", "ss2ss"], accum_out=sm(6))
        rstd_from_ss(sm(6), DC, sm(7), sm(8), "ss2")
        stt("dve", kr_[:, 0:256], ps[2][:, 0:256], sm(8), gkvb[:, :], ALU.mult, ALU.mult,
            reads=["ps2", "ss2", "gkvb"], writes=[kk])
        cp("act", krs[:, :], ps[2][:, 256:288], reads=["ps2"], writes=["krs"])
        cst = cs[:, t, :]
        tt("pool", rtmp[:, 0:16], krs[:, 0:16], cst[:, 0:16], ALU.mult, reads=["krs", "cs"], writes=["rt0"])
        tt("pool", rtmp[:, 16:32], krs[:, 16:32], cst[:, 16:32], ALU.mult, reads=["krs", "cs"], writes=["rt1"])
        tt("pool", kr_[:, 256:272], rtmp[:, 0:16], rtmp[:, 16:32], ALU.subtract, reads=["rt0", "rt1"], writes=[kk])
        tt("pool", rtmp[:, 32:48], krs[:, 16:32], cst[:, 0:16], ALU.mult, reads=["krs", "cs"], writes=["rt2"])
        tt("pool", rtmp[:, 48:64], krs[:, 0:16], cst[:, 16:32], ALU.mult, reads=["krs", "cs"], writes=["rt3"])
        tt("pool", kr_[:, 272:288], rtmp[:, 32:48], rtmp[:, 48:64], ALU.add, reads=["rt2", "rt3"], writes=[kk])
        dma("sp", (kv_s if samp else kv_p[128 * t:128 * t + 128, :]), kr_[:, :], reads=[kk])
        cp("act", kvbf[:, :], kr_[:, :], reads=[kk], writes=["kvbf"])
        if samp:
            cp("act", kvbf_s[:, 0:288], kr_[:, :], reads=[kk], writes=["kvbf_s"])
            memset("pool", kvbf_s[:, 288:289], 1.0, writes=["kvbf_s1"])
        tr(psb(5)[:, 0:128], kvbf[:, 0:128], identb[:, :], reads=["kvbf", "identb"], writes=["ps5"], sig=False)
        tr(psb(5)[:, 128:256], kvbf[:, 128:256], identb[:, :], reads=["kvbf", "identb"], writes=["ps5"], sig=False)
        tr(psb(5)[:96, 256:384], kvbf[:, 192:288], identb[:, :], reads=["kvbf", "identb"], writes=["ps5"])
        cp("dve", kvT[:, 0:2, 128 * t:128 * t + 128], psb(5)[:, 0:256].rearrange("p (k n) -> p k n", n=128),
           reads=["ps5"], writes=["kvT"])
        cp("dve", kvT[:96, 2, 128 * t:128 * t + 128], psb(5)[:96, 256:384], reads=["ps5"], writes=["kvT"])
        act(sig_t[:, :, :], ps[4][:, :].rearrange("p (c n) -> p c n", n=128), AF.Sigmoid, reads=["ps4"], writes=["sig"])
        if not samp:
            tt("dve", gluT[:, :, 30:158], ps[3][:, :].rearrange("p (c n) -> p c n", n=128), sig_t[:, :, :], ALU.mult,
               reads=["ps3", "sig"], writes=["gluT"])
            gb, gbk = gluTbl[t % 2], "gluTb%d" % (t % 2)
            cp("pool", gb[:, :, 30:158], gluT[:, :, 30:158], reads=["gluT"], writes=[gbk])
            if t > 0:
                cp("pool", gb[:, :, 0:30], gluTbl[(t - 1) % 2][:, :, 128:158], reads=["gluTb%d" % ((t - 1) % 2)], writes=[gbk])
        else:
            for c in range(4):
                tt("dve", gluS[:, c, :, 30:38], ps[3][:, 128 * c:128 * c + 128].rearrange("p (s j) -> p s j", j=8),
                   sig_t[:, c, :].rearrange("p (s j) -> p s j", j=8), ALU.mult, reads=["ps3", "sig"], writes=["gluS"])
        if t == NPT - 1:
            for c in range(4):
                tr(ps[6][:30, 128 * c:128 * c + 128], gluT[:, c, 128:158], identf[:, :],
                   reads=["gluT", "identf"], writes=["ps6"], sig=(c == 3))
            cp("act", gtok[:30, :], ps[6][:30, :], reads=["ps6"], writes=["gtok"])
            dma("sp", conv_p, gtok[:30, :], reads=["gtok"])
        if samp:
            for c in range(4):
                cp("pool", sig_t[:, c, :].rearrange("p (s j) -> p s j", j=8), gluS[:, c, :, 30:38],
                   reads=["gluS", "sig"], writes=["sig"])
            for c in range(4):
                tr(ps[6][:, 128 * c:128 * c + 128], sig_t[:, c, :], identf[:, :],
                   reads=["sig", "identf"], writes=["ps6"], sig=(c == 3))
            cp("act", gtok[:, :], ps[6][:, :], reads=["ps6"], writes=["gtok"])
            for s in range(SPC):
                dma("sp", conv_s[s, 22:30, :], gtok[8 * s:8 * s + 8, :], reads=["gtok"])

    def backA(t):
        samp = (t == NPT)
        if not samp:
            mm(ps[7][:, 0:512], ones1[0:1, :], bhi[0:1, :], True, False, reads=["ones1", "bhi"], writes=["ps7"], sig=False)
            mm(ps[7][:, 0:512], ones1[0:1, :], blo[0:1, :], False, False, reads=["ones1", "blo"], writes=["ps7"], sig=False)
            for c in range(4):
                for k in range(CW):
                    last = (c == 3 and k == CW - 1)
                    mm(ps[7][:, 128 * c:128 * c + 128], gluTbl[t % 2][:, c, k:k + 128], diag[:, c, k, :], False, last,
                       reads=["gluTb%d" % (t % 2), "diag"], writes=["ps7"], sig=last)
        else:
            gk = "gluS"

            def csrc(c, k):
                return gluS[:, c, :, k:k + 8]

            def cdst(c):
                return yT[:, c, :].rearrange("p (s j) -> p s j", j=8)
            for c in range(4):
                ts("dve", cdst(c), csrc(c, 0), wdw_t[:, c, 0:1], vec[:, BDW + c:BDW + c + 1], ALU.mult, ALU.add,
                   reads=[gk, "wdw", "vec"], writes=["yT%d" % c])
            for k in range(1, CW):
                for c in range(4):
                    stt("dve", cdst(c), csrc(c, k), wdw_t[:, c, k:k + 1], cdst(c), ALU.mult, ALU.add,
                        reads=[gk, "wdw", "yT%d" % c], writes=["yT%d" % c])
        if samp:
            for c in range(4):
                tr(ps[7][:, 128 * c:128 * c + 128], yT[:, c, :], identf[:, :],
                   reads=["yT%d" % c, "identf"], writes=["ps7"], sig=(c == 3))
        P.op("dve", lambda e: e.bn_stats(out=bnst[:, 0:6], in_=ps[7][:, :]), reads=["ps7"], writes=["bnst"])
        P.op("dve", lambda e: e.bn_aggr(out=bnst[:, 6:8], in_=bnst[:, 0:6]), reads=["bnst"], writes=["mv"])
        act(sm(9), bnst[:, 7:8], AF.Sqrt, reads=["mv", "eps"], writes=["lntmp"], scale=1.0, bias=epsT[:, 0:1])
        recip(sm(10), sm(9), reads=["lntmp"], writes=["lnr"])
        ts("dve", sm(11), bnst[:, 6:7], -1.0, sm(10), ALU.mult, ALU.mult, reads=["mv", "lnr"], writes=["lnb"])
        act(zt[:, :], ps[7][:, :], AF.Identity, reads=["ps7", "lnr", "lnb"], writes=["zt"], scale=sm(10), bias=sm(11))
        tt("dve", zt[:, :], zt[:, :], glnb[:, :], ALU.mult, reads=["zt", "glnb"], writes=["zt"])
        tt("pool", zt[:, :], zt[:, :], blnb[:, :], ALU.add, reads=["zt", "blnb"], writes=["zt"])
        act(s32[:, :], zt[:, :], AF.Silu, reads=["zt"], writes=["s32"])
        act(junk[:, 0:512], s32[:, :], AF.Square, reads=["s32"], writes=["junk", "ss3ss"], accum_out=sm(12))
        rstd_from_ss(sm(12), DCONV, sm(13), rstd_c[:, t:t + 1], "ss3")
        cp("dve", sbf[:, :], s32[:, :], reads=["s32"], writes=["sbf"])
        for c in range(4):
            tr(psb(5)[:, 512 + 128 * c:512 + 128 * c + 128], sbf[:, 128 * c:128 * c + 128], identb[:, :],
               reads=["sbf", "identb"], writes=["ps5b"], sig=(c == 3))
        tt("dve", convT[:, :, 128 * t:128 * t + 128], psb(5)[:, 512:1024].rearrange("p (k n) -> p k n", n=128),
           bcast(vec[:, GCO:GCO + 4], [128, 4, 128]), ALU.mult, reads=["ps5b", "vec"], writes=["convT"])


    frontA(0)
    for t in range(NTL):
        if t + 1 < NTL:
            frontA(t + 1)
        backA(t)

    A.reset(m_ab)
    P.barrier(lambda e: e.memset(bar_t[:, :], 0.0))
    wuv = A.alloc([128, 2, 512], BF, "wuv")
    QL = A.alloc([128, 2, NH, 128], BF, "QL")
    QR = A.alloc([128, NH, 128], BF, "QR")
    m_b = A.mark()
    wuq = A.alloc([128, 3, 768], BF, "wuq")
    wuqs = A.alloc([128, 3, 768], BF, "wuqs")
    wk = A.alloc([128, 3, 768], BF, "wk")
    wukT = A.alloc([64, NH, DC], BF, "wukT")
    Khl = [A.alloc([128, SEQ], BF, "Kh%d" % i) for i in range(2)]
    Vhl = [A.alloc([128, NPT, 65], BF, "Vh%d" % i) for i in range(2)]
    Qh = [A.alloc([128, 512], BF, "Qh%d" % i) for i in range(2)]
    ctl = [A.alloc([128, 512], F32, "ct%d" % i) for i in range(2)]
    snl = [A.alloc([128, 512], F32, "sn%d" % i) for i in range(2)]
    t1l = [A.alloc([128, 512], F32, "t1%d" % i) for i in range(2)]
    t2l = [A.alloc([128, 512], F32, "t2%d" % i) for i in range(2)]
    PT = [A.alloc([128, 512], BF, "PT%d" % i) for i in range(4)]
    Osbl = [A.alloc([128, 512], F32, "Osb%d" % i) for i in range(2)]
    rl4l = [A.alloc([128, 4], F32, "rl4%d" % i) for i in range(2)]

    dma("pool", wuq[:, :, :], w_uq, writes=["wuq"])
    dma("pool", wuqs[:, :, :], w_uqs, writes=["wuqs"])
    dma("pool", wk[:, :, :], w_k, writes=["wk"])
    dma("pool", wuv[:, :, :], w_uv, writes=["wuv"])
    dma("pool", wukT[:, :, :], w_ukT, writes=["wukT"])
    for i in range(2):
        memset("pool", Vhl[i][:, :, 64:65], 1.0, writes=["Vh1_%d" % i])

    def emitKV(h):
        Kh, Vh = Khl[h % 2], Vhl[h % 2]
        kk, vk = "Kh%d" % (h % 2), "Vh%d" % (h % 2)
        for g in range(0, NPT, 8):
            n = min(8, NPT - g)
            for i in range(n):
                kt = g + i
                for k in range(2):
                    mm(ps[1][:, 64 * i:64 * i + 64], kvT[:, k, 128 * kt:128 * kt + 128], wuv[:, k, 64 * h:64 * h + 64],
                       k == 0, k == 1, reads=["kvT", "wuv"], writes=["ps1"], sig=(k == 1 and i == n - 1))
            cp("act", Vh[:, g:g + n, 0:64], ps[1][:, 0:64 * n].rearrange("p (i v) -> p i v", v=64),
               reads=["ps1"], writes=[vk])
        for st in range(NST):
            for k in range(3):
                lo = 64 if k == 2 else 0
                hi = 96 if k == 2 else 128
                mm(ps[1][:96, :], wk[lo:hi, k, 96 * h:96 * h + 96], kvT[lo:hi, k, 512 * st:512 * st + 512],
                   k == 0, k == 2, reads=["kvT", "wk"], writes=["ps1"])
            cp("dve", Kh[:96, 512 * st:512 * st + 512], ps[1][:96, :], reads=["ps1"], writes=[kk])

    units = [(h, qs) for h in range(NH) for qs in range(NST + 1)]

    def emitQ(ui):
        h, qs = units[ui]
        samp = (qs == NST)
        T = 128 if samp else 512
        q0g = 512 * qs
        u = ui % 2
        ct, sn, Q, t1, t2 = ctl[u], snl[u], Qh[u], t1l[u], t2l[u]
        dma("sp", ct[64:96, 0:T], cosT[:, q0g:q0g + T], writes=["ct%d" % u])
        dma("sp", sn[64:96, 0:T], sinT[:, q0g:q0g + T], writes=["sn%d" % u])
        for k in range(3):
            mm(ps[2][:96, 0:T], wuq[:, k, 96 * h:96 * h + 96], cqnT[:, k, q0g:q0g + T], k == 0, k == 2,
               reads=["cqnT", "wuq"], writes=["ps2"])
        for k in range(3):
            mm(ps[3][:96, 0:T], wuqs[:, k, 96 * h:96 * h + 96], cqnT[:, k, q0g:q0g + T], k == 0, k == 2,
               reads=["cqnT", "wuqs"], writes=["ps3"])
        qk = "Q%d" % u
        cp("act", Q[0:64, 0:T], ps[2][0:64, 0:T], reads=["ps2"], writes=[qk])
        tt("dve", t1[64:96, 0:T], ps[2][64:96, 0:T], ct[64:96, 0:T], ALU.mult, reads=["ps2", "ct%d" % u], writes=["t1%d" % u])
        tt("dve", t2[64:96, 0:T], ps[3][64:96, 0:T], sn[64:96, 0:T], ALU.mult, reads=["ps3", "sn%d" % u], writes=["t2%d" % u])
        tt("pool", Q[64:96, 0:T], t1[64:96, 0:T], t2[64:96, 0:T], ALU.add, reads=["t1%d" % u, "t2%d" % u], writes=[qk])

    emitKV(0)
    emitQ(0)
    gkt = 0
    for ui, (h, qs) in enumerate(units):
        samp = (qs == NST)
        u = ui % 2
        Q = Qh[u]
        qk = "Q%d" % u
        Kh, Vh = Khl[h % 2], Vhl[h % 2]
        kk, vk, v1k = "Kh%d" % (h % 2), "Vh%d" % (h % 2), "Vh1_%d" % (h % 2)
        if samp:
            for c in range(2):
                mm(ps[1][:, 128 * c:128 * c + 128], wukT[:, h, 128 * c:128 * c + 128], Q[0:64, 0:128], True, True,
                   reads=["wukT", qk], writes=["ps1"], sig=(c == 1))
            cp("act", QL[:, :, h, :], ps[1][:, 0:256].rearrange("p (c n) -> p c n", n=128), reads=["ps1"], writes=["QL"])
            cp("pool", QR[64:96, h, :], Q[64:96, 0:128], reads=[qk], writes=["QR"])
            if ui + 1 < len(units):
                emitQ(ui + 1)
            continue
        nkt = 4 * qs + 4
        ob = 6 if (ui % 2 == 0) else 0
        obk = "ps%d" % ob

        def stS(kt):
            j = kt - 4 * qs
            q0 = 128 * j if j > 0 else 0
            sb = 4 + ((gkt + kt) % 2)
            sl = (gkt + kt) % 4
            mm(ps[sb][:, q0:512], Kh[:96, 128 * kt:128 * kt + 128], Q[:96, q0:512], True, True,
               reads=[kk, qk], writes=["ps%d" % sb])
            act(PT[sl][:, q0:512], ps[sb][:, q0:512], AF.Exp, reads=["ps%d" % sb], writes=["PT%d" % sl], scale=SCALE)
            if j >= 0:
                tt("pool", PT[sl][:, q0:q0 + 128], PT[sl][:, q0:q0 + 128], tri[:, :], ALU.mult,
                   reads=["PT%d" % sl, "tri"], writes=["PT%d" % sl])

        def stPV(kt):
            j = kt - 4 * qs
            q0 = 128 * j if j > 0 else 0
            sl = (gkt + kt) % 4
            mm(ps[ob][:65, q0:512], Vh[:, kt, 0:65], PT[sl][:, q0:512], kt == 0, kt == nkt - 1,
               reads=[vk, v1k, "PT%d" % sl], writes=[obk], sig=True)

        stS(0)
        for kt in range(nkt):
            if kt + 1 < nkt:
                stS(kt + 1)
            if kt == 1:
                if ui + 1 < len(units):
                    emitQ(ui + 1)
                if qs == 0 and h + 1 < NH:
                    emitKV(h + 1)
            stPV(kt)
        gkt += nkt
        Osb, rl4 = Osbl[ui % 2], rl4l[ui % 2]
        ok_, rk_ = "Osb%d" % (ui % 2), "rl4%d" % (ui % 2)
        cp("act", Osb[:65, :], ps[ob][:65, :], reads=[obk], writes=[ok_])
        for jj in range(4):
            tr(ps[7][:, 65 * jj:65 * jj + 65], Osb[:65, 128 * jj:128 * jj + 128], identf[:65, :65],
               reads=[ok_, "identf"], writes=["ps7"], sig=(jj == 3))
        ov = ps[7][:, 0:260].rearrange("p (j v) -> p j v", v=65)
        recip(rl4[:, :], ov[:, :, 64], reads=["ps7"], writes=[rk_])
        tt("dve", attn_tok[:, 4 * qs:4 * qs + 4, 64 * h:64 * h + 64], ov[:, :, 0:64],
           bcast(rl4[:, :], [128, 4, 64]), ALU.mult, reads=["ps7", rk_], writes=["attn_tok"])

    if cfg.get("STOP") == "B":
        P.emit(nc)
        return nc
    A.reset(m_b)
    P.barrier(lambda e: e.memset(bar_t[:, :], 0.0))
    NSL = 8
    NKS = 4
    pg = [A.alloc([128, 289], BF, "pg%d" % i) for i in range(NSL)]
    KT = [A.alloc([128, 384], BF, "KT%d" % i) for i in range(NKS)]
    PTs = [A.alloc([128, 64], BF, "PTs%d" % i) for i in range(NKS)]
    ptb = A.alloc([128, NPAGES], I32, "ptb")
    idx = A.alloc([128, NPAGES], I32, "idx")
    iot = A.alloc([128, 1], I32, "iota")
    smask = A.alloc([128, SPC * 64], BF, "smask")
    olatl = [A.alloc([64, 256], BF, "olat%d" % i) for i in range(2)]
    rlsl = [A.alloc([64, 1], F32, "rls%d" % i) for i in range(2)]
    OLT = A.alloc([128, 2, NH, 128], BF, "OLT")
    dma("sp", ptb[:, :], ptab.partition_broadcast(128), writes=["ptb"])
    dma("sp", smask[:, :], smask_d, writes=["smask"])
    P.op("pool", lambda e: e.iota(iot[:, :], pattern=[[0, 1]], base=0, channel_multiplier=1), writes=["iota"])
    ts("pool", idx[:, :], ptb[:, :], PAGE, iot[:, 0:1], ALU.mult, ALU.add, reads=["ptb", "iota"], writes=["idx"])
    for i in range(NSL):
        memset("pool", pg[i][:, 288:289], 1.0, writes=["pg1_%d" % i])
    pages = []
    gi = 0
    for s in range(SPC):
        for p in range(NPG + 1):
            new = (p == NPG)
            pages.append(dict(s=s, p=p, new=new, gi=(None if new else gi), fi=(None if new else s * NPG + p)))
            if not new:
                gi += 1
    NPGS = len(pages)
    real = [pgd for pgd in pages if not pgd["new"]]

    def stG(g):
        if g >= len(real):
            return
        pgd = real[g]
        gs = g % NSL
        pgt = pg[gs]
        fi = pgd["fi"]

        def gat(e, pgt=pgt, fi=fi):
            return e.indirect_dma_start(out=pgt[:, 0:288], out_offset=None, in_=cache,
                                        in_offset=bass.IndirectOffsetOnAxis(ap=idx[:, fi:fi + 1], axis=0))
        P.op("pool", gat, reads=["idx"], writes=["pg%d" % gs], dma=True)

    def stT(i):
        if i >= NPGS or pages[i]["new"]:
            return
        g = pages[i]["gi"]
        gs = g % NSL
        pgk = "pg%d" % gs
        pgt = pg[gs]
        kb = g % 3
        ksl = g % NKS
        tr(psb(kb)[:, 0:128], pgt[:, 0:128], identb[:, :], reads=[pgk, "identb"], writes=["ps%d" % kb], sig=False)
        tr(psb(kb)[:, 128:256], pgt[:, 128:256], identb[:, :], reads=[pgk, "identb"], writes=["ps%d" % kb], sig=False)
        tr(psb(kb)[:96, 256:384], pgt[:, 192:288], identb[:, :], reads=[pgk, "identb"], writes=["ps%d" % kb])
        cp("dve", KT[ksl][:, 0:384], psb(kb)[:, 0:384], reads=["ps%d" % kb], writes=["KT%d" % ksl])

    def stQK(i):
        if i >= NPGS:
            return
        pgd = pages[i]
        s = pgd["s"]
        if not pgd["new"]:
            ksl = pgd["gi"] % NKS
            k0, k1, k2 = KT[ksl][:, 0:128], KT[ksl][:, 128:256], KT[ksl][64:96, 256:384]
            kreads = ["KT%d" % ksl]
        else:
            c0 = NPT * 128
            k0, k1, k2 = kvT[:, 0, c0:c0 + 128], kvT[:, 1, c0:c0 + 128], kvT[64:96, 2, c0:c0 + 128]
            kreads = ["kvT"]
        sbk = 3 + (i % 2)
        so = 64 * ((i // 2) % 2)
        pso = ps[sbk][:, so:so + 64]
        psk = "ps%d_%d" % (sbk, so)
        psl = i % NKS
        mm(pso, k0, QL[:, 0, :, 8 * s:8 * s + 8], True, False, reads=kreads + ["QL"], writes=[psk])
        mm(pso, k1, QL[:, 1, :, 8 * s:8 * s + 8], False, False, reads=kreads + ["QL"], writes=[psk])
        mm(pso, k2, QR[64:96, :, 8 * s:8 * s + 8], False, True, reads=kreads + ["QR"], writes=[psk])
        act(PTs[psl][:, :], pso, AF.Exp, reads=[psk], writes=["PTs%d" % psl], scale=SCALE)
        if pgd["new"]:
            tt("pool", PTs[psl][:, :], PTs[psl][:, :], smask[:, 64 * s:64 * s + 64], ALU.mult,
               reads=["PTs%d" % psl, "smask"], writes=["PTs%d" % psl])

    def stPV(i):
        pgd = pages[i]
        s = pgd["s"]
        psl = i % NKS
        ob = 5 if s % 2 == 0 else 7
        if not pgd["new"]:
            gs = pgd["gi"] % NSL
            vrhs = pg[gs][:, 0:289]
            vreads = ["pg%d" % gs, "pg1_%d" % gs]
        else:
            vrhs = kvbf_s[:, 0:289]
            vreads = ["kvbf_s", "kvbf_s1"]
        mm(ps[ob][:64, 0:289], PTs[psl][:, :], vrhs, pgd["p"] == 0, pgd["new"], reads=["PTs%d" % psl] + vreads,
           writes=["ps%d" % ob], sig=True)
        if pgd["new"]:
            olat, rls = olatl[s % 2], rlsl[s % 2]
            olk, rlk = "olat%d" % (s % 2), "rls%d" % (s % 2)
            recip(rls[:, :], ps[ob][:64, 288:289], reads=["ps%d" % ob], writes=[rlk])
            ts("dve", olat[:, :], ps[ob][:64, 0:256], rls[:, 0:1], None, ALU.mult, reads=["ps%d" % ob, rlk], writes=[olk])
            for c in range(2):
                tr(psb(6)[:, 64 * c:64 * c + 64], olat[:, 128 * c:128 * c + 128], identb[:64, :64],
                   reads=[olk, "identb"], writes=["ps6"], sig=(c == 1))
            cp("act", OLT[:, :, :, 8 * s:8 * s + 8], psb(6)[:, 0:128].rearrange("p (c h j) -> p c h j", c=2, h=NH),
               reads=["ps6"], writes=["OLT"])

    GD = 5
    for g in range(GD):
        stG(g)
    stT(0)
    stT(1)
    stQK(0)
    for i in range(NPGS):
        if not pages[i]["new"]:
            stG(pages[i]["gi"] + GD)
        stT(i + 2)
        stQK(i + 1)
        stPV(i)
    for h in range(NH):
        for c in range(2):
            mm(ps[1][:, 64 * h:64 * h + 64], OLT[:, c, h, :], wuv[:, c, 64 * h:64 * h + 64], c == 0, c == 1,
               reads=["OLT", "wuv"], writes=["ps1"], sig=(c == 1 and h == NH - 1))
    cp("act", attn_tok[:, NPT, :], ps[1][:, :], reads=["ps1"], writes=["attn_tok"])

    if cfg.get("STOP") == "S":
        P.emit(nc)
        return nc
    def phaseC_new():
        A.reset(m_persist)
        P.barrier(lambda e: e.memset(bar_t[:, :], 0.0))
        guardC = []
        wdn = A.alloc([128, NFC, D], BF, "wdn")
        wst = [A.alloc([128, 8, 256], BF, "wst%d" % i) for i in range(2)]
        x1 = A.alloc([128, 4, D], F32, "x1")
        x1_off = A.last
        h2T = A.alloc([128, 8, 512], BF, "h2T")
        actT = A.alloc([128, NFC, 512], BF, "actT")
        ugl = [A.alloc([128, 640], F32, "ug%d" % i) for i in range(2)]
        wos = A.at([128, 8, D], BF, "wos", A.last - 2560)
        uvl = [A.alloc([128, 640], F32, "uv%d" % i) for i in range(2)]
        cgl = [A.alloc([128, 512], F32, "cg%d" % i) for i in range(2)]
        cvl = [A.alloc([128, 512], F32, "cv%d" % i) for i in range(2)]
        xt2l = [A.alloc([128, D], F32, "xt2%d" % i) for i in range(1)]
        x1n = A.alloc([128, D], BF, "x1n")
        aT = A.alloc([128, 4, 128], BF, "aT")
        carry = A.alloc([128, 44, 2], F32, "carry")
        carry_s = A.at([128, 44, SPC, 2], F32, "carry_s", x1_off + 4096)
        wft = A.alloc([128, 44, 4], F32, "wft")
        gfb = A.alloc([128, D], F32, "gfb")
        sm2 = A.alloc([128, 16], F32, "sm2")
        stfl = [A.alloc([32, 512], F32, "stf%d" % i) for i in range(2)]
        x2l = [A.alloc([128, D], F32, "x2%d" % i) for i in range(2)]

        WOSK = ["ug0", "ug1", "uv0", "uv1", "cg0", "cg1", "cv0", "cv1"]
        NCH = (NST + 1) * NFC

        def emit_wst(g):
            if g < NCH:
                dma("pool", wst[g % 2][:, :, :], w_up[g % NFC], writes=["wst%d" % (g % 2)])
        dma("pool", wos[:, :, :], w_o, writes=WOSK)
        dma("pool", wdn[:, :, :], w_dn, writes=["wdn"])
        emit_wst(0)
        dma("sp", wft[:, :, :], wf, writes=["wft"])
        dma("sp", gfb[:, :], rowv[:, 1280:2304].partition_broadcast(128), writes=["gfb"])
        memset("pool", carry[:, :, :], 0.0, writes=["carry"])
        s2 = lambda i: sm2[:, i:i + 1]
        gch = 0
        gtl = 0

        for st in range(NST + 1):
            samp = (st == NST)
            ntile = 1 if samp else 4
            T = 128 * ntile
            if st > 0:
                dma("pool", wos[:, :, :], w_o, writes=WOSK)
            if samp:
                for g in range(11):
                    sb_ = stfl[g % 2]
                    sk_ = "stf%d" % (g % 2)
                    dma("sp", sb_[:, :], stf[:, 512 * g:512 * g + 512], writes=[sk_])
                    for i in range(4):
                        tr(ps[0][:, 32 * i:32 * i + 32], sb_[:, 128 * i:128 * i + 128], identf[:32, :32],
                           reads=[sk_, "identf"], writes=["ps0"], sig=(i == 3))
                    cp("dve", carry_s[:, 4 * g:4 * g + 4, :, :].rearrange("p c s j -> p c (s j)"),
                       ps[0][:, 0:128].rearrange("p (c n) -> p c n", n=32), reads=["ps0"], writes=["carry_s", "x1_1", "x1_2"])
            def P1(j):
                t = 4 * st + j
                xt2 = xt2l[0]
                xk2 = "xt20"
                sb = 0 if j % 2 == 0 else 9
                sak = "sa%d_" % (j % 2)
                dma("sp", xt2[:, :], (xs if samp else xp[128 * t:128 * t + 128, :]), writes=[xk2])
                act(aT[:, :, :].rearrange("p k n -> p (k n)"), attn_tok[:, t, :], AF.Square, reads=["attn_tok"],
                    writes=["aT", sak + "ss"], accum_out=s2(sb))
                rstd_from_ss(s2(sb), 512, s2(sb + 1), s2(sb + 2), sak)
                for c in range(4):
                    tr(psb(0)[:, 128 * c:128 * c + 128], attn_tok[:, t, 128 * c:128 * c + 128], identb[:, :],
                       reads=["attn_tok", "identb"], writes=["ps0"], sig=(c == 3))
                tt("dve", aT[:, :, :], psb(0)[:, 0:512].rearrange("p (k n) -> p k n", n=128),
                   bcast(vec[:, GAO:GAO + 4], [128, 4, 128]), ALU.mult, reads=["ps0", "vec"], writes=["aT"])
                for n in range(2):
                    for c in range(4):
                        mm(ps[1 + n][:, :], aT[:, c, :], wos[:, c, 512 * n:512 * n + 512], c == 0, c == 3,
                           reads=["aT"] + WOSK, writes=["ps%d" % (1 + n)])
                for n in range(2):
                    for c in range(4):
                        mm(ps[3 + n][:, :], convT[:, c, 128 * t:128 * t + 128], wos[:, 4 + c, 512 * n:512 * n + 512], c == 0, c == 3,
                           reads=["convT"] + WOSK, writes=["ps%d" % (3 + n)])
                for n in range(2):
                    stt("dve", x1[:, j, 512 * n:512 * n + 512], ps[1 + n][:, :], s2(sb + 2), xt2[:, 512 * n:512 * n + 512], ALU.mult, ALU.add,
                        reads=["ps%d" % (1 + n), sak, xk2], writes=["x1_%d" % j])
                for n in range(2):
                    stt("dve", x1[:, j, 512 * n:512 * n + 512], ps[3 + n][:, :], rstd_c[:, t:t + 1], x1[:, j, 512 * n:512 * n + 512],
                        ALU.mult, ALU.add, reads=["ps%d" % (3 + n), "ss3", "x1_%d" % j], writes=["x1_%d" % j])

            def P2(j):
                act(x1n[:, :], x1[:, j, :], AF.Square, reads=["x1_%d" % j], writes=["x1n", "sf_ss"], accum_out=s2(3))
                rstd_from_ss(s2(3), D, s2(4), s2(5), "sf_")
                act(x1n[:, :], x1[:, j, :], AF.Copy, reads=["x1_%d" % j, "sf_"], writes=["x1n"], scale=s2(5))
                for k in range(8):
                    tr(psb(5)[:, 128 * k:128 * k + 128], x1n[:, 128 * k:128 * k + 128], identb[:, :],
                       reads=["x1n", "identb"], writes=["ps5"], sig=(k == 7))
                tt("dve", h2T[:, :, 128 * j:128 * j + 128], psb(5)[:, :].rearrange("p (k n) -> p k n", n=128),
                   bcast(vec[:, GF:GF + 8], [128, 8, 128]), ALU.mult, reads=["ps5", "vec"], writes=["h2T"])

            P1(0)
            for j in range(ntile):
                if j + 1 < ntile:
                    P1(j + 1)
                P2(j)
            for i in range(NFC):
                par = gch % 2
                gch += 1
                wkk = "wst%d" % par
                emit_wst(gch)
                banks = (6, 7) if (par == 0 or cfg.get("C_BANKS") == "same") else (0, 5)
                taps = []
                for half in range(2):
                    bank = banks[half]
                    ub = (ugl, uvl)[half][par]
                    cb = (cgl, cvl)[half][par]
                    ubk = ("ug%d", "uv%d")[half] % par
                    cbk = ("cg%d", "cv%d")[half] % par
                    ch = i + NFC * half
                    for k in range(8):
                        mm(ps[bank][:, 0:T], wst[par][:, k, 128 * half:128 * half + 128], h2T[:, k, 0:T], k == 0, k == 7,
                           reads=[wkk, "h2T"], writes=["ps%d" % bank])
                    if not samp:
                        cp("pool", ub[:, 0:2], carry[:, ch, :], reads=["carry"], writes=[ubk])
                        act(ub[:, 2:2 + T], ps[bank][:, 0:T], AF.Copy, reads=["ps%d" % bank], writes=[ubk])
                        cp("pool", carry[:, ch, :], ub[:, T:T + 2], reads=[ubk], writes=["carry"])
                        v0, v1 = ub[:, 0:T], ub[:, 1:1 + T]
                        co = cb[:, 0:T]
                        act(co, ps[bank][:, 0:T], AF.Identity, reads=["ps%d" % bank, "wft"], writes=[cbk],
                            scale=wft[:, ch, 2:3], bias=wft[:, ch, 3:4])
                    else:
                        u3 = ub[:, 0:160].rearrange("p (s j) -> p s j", j=10)
                        cp("pool", u3[:, :, 0:2], carry_s[:, ch, :, :], reads=["carry_s"], writes=[ubk])
                        act(u3[:, :, 2:10], ps[bank][:, 0:128].rearrange("p (s j) -> p s j", j=8), AF.Copy,
                            reads=["ps%d" % bank], writes=[ubk])
                        cp("pool", carry_s[:, ch, :, :], u3[:, :, 8:10], reads=[ubk], writes=["carry_s"])
                        v0, v1 = u3[:, :, 0:8], u3[:, :, 1:9]
                        co = cb[:, 0:128].rearrange("p (s j) -> p s j", j=8)
                        act(co, ps[bank][:, 0:128].rearrange("p (s j) -> p s j", j=8), AF.Identity,
                            reads=["ps%d" % bank, "wft"], writes=[cbk], scale=wft[:, ch, 2:3], bias=wft[:, ch, 3:4])
                    taps.append((co, v0, v1, ch, ubk, cbk))
                for kk_ in (1, 0):
                    for (co, v0, v1, ch, ubk, cbk) in taps:
                        stt("dve", co, (v1 if kk_ == 1 else v0), wft[:, ch, kk_:kk_ + 1], co, ALU.mult, ALU.add,
                            reads=[ubk, "wft", cbk], writes=[cbk])
                cgb, cvb = cgl[par], cvl[par]
                act(cgb[:, 0:T], cgb[:, 0:T], AF.Silu, reads=["cg%d" % par], writes=["cg%d" % par])
                tt(cfg.get("C_TTENG", "dve"), actT[:, i, 0:T], cgb[:, 0:T], cvb[:, 0:T], ALU.mult, reads=["cg%d" % par, "cv%d" % par], writes=["actT"])
            if st == NST - 1:
                for g in range(11):
                    for i in range(4):
                        ch = 4 * g + i
                        tr(ps[0][:2, 128 * i:128 * i + 128], carry[:, ch, :], identf[:, :],
                           reads=["carry", "identf"], writes=["ps0"], sig=(i == 3))
                    cp("act", stfl[g % 2][:2, :], ps[0][:2, :], reads=["ps0"], writes=["stf%d" % (g % 2)])
                    dma("sp", ffn_p[:, 512 * g:512 * g + 512], stfl[g % 2][:2, :], reads=["stf%d" % (g % 2)])
            if samp:
                for g in range(11):
                    for i in range(4):
                        ch = 4 * g + i
                        tr(ps[0][:32, 128 * i:128 * i + 128], carry_s[:, ch, :, :].rearrange("p s j -> p (s j)"), identf[:, :],
                           reads=["carry_s", "identf"], writes=["ps0"], sig=(i == 3))
                    cp("act", stfl[g % 2][:32, :], ps[0][:32, :], reads=["ps0"], writes=["stf%d" % (g % 2)])
                    dma("sp", ffn_s[:, 512 * g:512 * g + 512], stfl[g % 2][:32, :], reads=["stf%d" % (g % 2)])
            for j in range(ntile):
                t = 4 * st + j
                x2 = x2l[j % 2]
                x2k = "x2%d" % (j % 2)
                pb = (1, 2) if j % 2 == 0 else (3, 4)
                for n in range(2):
                    for i in range(NFC):
                        mm(ps[pb[n]][:, :], actT[:, i, 128 * j:128 * j + 128], wdn[:, i, 512 * n:512 * n + 512], i == 0, i == NFC - 1,
                           reads=["actT", "wdn"], writes=["ps%d" % pb[n]])
                for n in range(2):
                    tt("dve", x2[:, 512 * n:512 * n + 512], ps[pb[n]][:, :], x1[:, j, 512 * n:512 * n + 512], ALU.add,
                       reads=["ps%d" % pb[n], "x1_%d" % j], writes=[x2k])
                act(x1n[:, :], x2[:, :], AF.Square, reads=[x2k], writes=["x1n", "sy_ss"], accum_out=s2(6))
                rstd_from_ss(s2(6), D, s2(7), s2(8), "sy_")
                stt("dve", x2[:, :], x2[:, :], s2(8), gfb[:, :], ALU.mult, ALU.mult, reads=[x2k, "sy_", "gfb"], writes=[x2k])
                dma("sp", (y_s if samp else y_p[128 * t:128 * t + 128, :]), x2[:, :], reads=[x2k])


    def phaseC_old():
        A.reset(m_persist)
        P.barrier(lambda e: e.memset(bar_t[:, :], 0.0))
        guardC = []
        wdn = A.alloc([128, NFC, D], BF, "wdn")
        wst = [A.alloc([128, 8, 256], BF, "wst%d" % i) for i in range(2)]
        x1 = A.alloc([128, 4, D], F32, "x1")
        x1_off = A.last
        h2T = A.alloc([128, 8, 512], BF, "h2T")
        actT = A.alloc([128, NFC, 512], BF, "actT")
        wos = A.at([128, 8, D], BF, "wos", A.last)
        ug = A.alloc([128, 640], F32, "ug")
        uv = A.alloc([128, 640], F32, "uv")
        cg = A.alloc([128, 512], F32, "cg")
        cv = A.alloc([128, 512], F32, "cv")
        sg = A.alloc([128, 512], F32, "sg")
        xt2 = A.alloc([128, D], F32, "xt2")
        x1n = A.alloc([128, D], BF, "x1n")
        aT = A.alloc([128, 4, 128], BF, "aT")
        junk2 = A.alloc([128, D], BF, "junk2")
        carry = A.alloc([128, 44, 2], F32, "carry")
        carry_s = A.at([128, 44, SPC, 2], F32, "carry_s", x1_off + 4096)
        wft = A.alloc([128, 44, 4], F32, "wft")
        gfb = A.alloc([128, D], F32, "gfb")
        sm2 = A.alloc([128, 16], F32, "sm2")
        stfl = [A.alloc([32, 512], F32, "stf%d" % i) for i in range(2)]
        x2 = A.alloc([128, D], F32, "x2")
        assert A.mark() < 229000, A.mark()

        dma("pool", wdn[:, :, :], w_dn, reads=guardC, writes=["wdn"])
        dma("sp", wft[:, :, :], wf, reads=guardC, writes=["wft"])
        dma("sp", gfb[:, :], rowv[:, 1280:2304].partition_broadcast(128), reads=guardC, writes=["gfb"])
        memset("pool", carry[:, :, :], 0.0, writes=["carry"])
        s2 = lambda i: sm2[:, i:i + 1]
        GC = guardC

        for st in range(NST + 1):
            samp = (st == NST)
            ntile = 1 if samp else 4
            T = 128 * ntile
            dma("pool", wos[:, :, :], w_o, reads=GC, writes=["actT"])
            if samp:
                for g in range(11):
                    sb_ = stfl[g % 2]
                    sk_ = "stf%d" % (g % 2)
                    dma("sp", sb_[:, :], stf[:, 512 * g:512 * g + 512], reads=GC, writes=[sk_])
                    for i in range(4):
                        tr(ps[0][:, 32 * i:32 * i + 32], sb_[:, 128 * i:128 * i + 128], identf[:32, :32],
                           reads=[sk_, "identf"], writes=["ps0"], sig=(i == 3))
                    cp("dve", carry_s[:, 4 * g:4 * g + 4, :, :].rearrange("p c s j -> p c (s j)"),
                       ps[0][:, 0:128].rearrange("p (c n) -> p c n", n=32), reads=["ps0"] + GC, writes=["carry_s", "x1_1", "x1_2"])
            for j in range(ntile):
                t = 4 * st + j
                dma("sp", xt2[:, :], (xs if samp else xp[128 * t:128 * t + 128, :]), reads=GC, writes=["xt2"])
                act(junk2[:, 0:512], attn_tok[:, t, :], AF.Square, reads=["attn_tok"] + GC, writes=["junk2", "sa_ss"], accum_out=s2(0))
                rstd_from_ss(s2(0), 512, s2(1), s2(2), "sa_")
                for c in range(4):
                    tr(psb(0)[:, 128 * c:128 * c + 128], attn_tok[:, t, 128 * c:128 * c + 128], identb[:, :],
                       reads=["attn_tok", "identb"], writes=["ps0"], sig=(c == 3))
                tt("dve", aT[:, :, :], psb(0)[:, 0:512].rearrange("p (k n) -> p k n", n=128),
                   bcast(vec[:, GAO:GAO + 4], [128, 4, 128]), ALU.mult, reads=["ps0", "vec"] + GC, writes=["aT"])
                for n in range(2):
                    for c in range(4):
                        mm(ps[1 + n][:, :], aT[:, c, :], wos[:, c, 512 * n:512 * n + 512], c == 0, c == 3,
                           reads=["aT", "actT"], writes=["ps%d" % (1 + n)])
                for n in range(2):
                    for c in range(4):
                        mm(ps[3 + n][:, :], convT[:, c, 128 * t:128 * t + 128], wos[:, 4 + c, 512 * n:512 * n + 512], c == 0, c == 3,
                           reads=["convT", "actT"], writes=["ps%d" % (3 + n)])
                for n in range(2):
                    stt("dve", x1[:, j, 512 * n:512 * n + 512], ps[1 + n][:, :], s2(2), xt2[:, 512 * n:512 * n + 512], ALU.mult, ALU.add,
                        reads=["ps%d" % (1 + n), "sa_", "xt2"] + GC, writes=["x1_%d" % j])
                    stt("dve", x1[:, j, 512 * n:512 * n + 512], ps[3 + n][:, :], rstd_c[:, t:t + 1], x1[:, j, 512 * n:512 * n + 512],
                        ALU.mult, ALU.add, reads=["ps%d" % (3 + n), "ss3", "x1_%d" % j], writes=["x1_%d" % j])
                act(junk2[:, :], x1[:, j, :], AF.Square, reads=["x1_%d" % j], writes=["junk2", "sf_ss"], accum_out=s2(3))
                rstd_from_ss(s2(3), D, s2(4), s2(5), "sf_")
                act(x1n[:, :], x1[:, j, :], AF.Copy, reads=["x1_%d" % j, "sf_"], writes=["x1n"], scale=s2(5))
                for k in range(8):
                    tr(psb(5)[:, 128 * k:128 * k + 128], x1n[:, 128 * k:128 * k + 128], identb[:, :],
                       reads=["x1n", "identb"], writes=["ps5"], sig=(k == 7))
                tt("dve", h2T[:, :, 128 * j:128 * j + 128], psb(5)[:, :].rearrange("p (k n) -> p k n", n=128),
                   bcast(vec[:, GF:GF + 8], [128, 8, 128]), ALU.mult, reads=["ps5", "vec"] + GC, writes=["h2T"])
            for i in range(NFC):
                wsl = (st * NFC + i) % 2
                wkk = "wst%d" % wsl
                dma("pool", wst[wsl][:, :, :], w_up[i], reads=GC, writes=[wkk])
                for half, (bank, ub, ubk, cb, cbk, eng) in enumerate(((6, ug, "ug", cg, "cg", "dve"), (7, uv, "uv", cv, "cv", "pool"))):
                    ch = i + NFC * half
                    for k in range(8):
                        mm(ps[bank][:, 0:T], wst[wsl][:, k, 128 * half:128 * half + 128], h2T[:, k, 0:T], k == 0, k == 7,
                           reads=[wkk, "h2T"], writes=["ps%d" % bank])
                    if not samp:
                        cp("pool", ub[:, 0:2], carry[:, ch, :], reads=["carry", cbk] + GC, writes=[ubk])
                        act(ub[:, 2:2 + T], ps[bank][:, 0:T], AF.Copy, reads=["ps%d" % bank, cbk] + GC, writes=[ubk])
                        cp("pool", carry[:, ch, :], ub[:, T:T + 2], reads=[ubk], writes=["carry"])
                        v0, v1, v2 = ub[:, 0:T], ub[:, 1:1 + T], ub[:, 2:2 + T]
                        co = cb[:, 0:T]
                    else:
                        u3 = ub[:, 0:160].rearrange("p (s j) -> p s j", j=10)
                        cp("pool", u3[:, :, 0:2], carry_s[:, ch, :, :], reads=["carry_s", cbk] + GC, writes=[ubk])
                        act(u3[:, :, 2:10], ps[bank][:, 0:128].rearrange("p (s j) -> p s j", j=8), AF.Copy,
                            reads=["ps%d" % bank, cbk] + GC, writes=[ubk])
                        cp("pool", carry_s[:, ch, :, :], u3[:, :, 8:10], reads=[ubk], writes=["carry_s"])
                        v0, v1, v2 = u3[:, :, 0:8], u3[:, :, 1:9], u3[:, :, 2:10]
                        co = cb[:, 0:128].rearrange("p (s j) -> p s j", j=8)
                    ts(eng, co, v2, wft[:, ch, 2:3], wft[:, ch, 3:4], ALU.mult, ALU.add, reads=[ubk, "wft"], writes=[cbk])
                    stt(eng, co, v1, wft[:, ch, 1:2], co, ALU.mult, ALU.add, reads=[ubk, "wft", cbk], writes=[cbk])
                    stt(eng, co, v0, wft[:, ch, 0:1], co, ALU.mult, ALU.add, reads=[ubk, "wft", cbk], writes=[cbk])
                act(sg[:, 0:T], cg[:, 0:T], AF.Silu, reads=["cg"] + GC, writes=["sg"])
                tt("dve", actT[:, i, 0:T], sg[:, 0:T], cv[:, 0:T], ALU.mult, reads=["sg", "cv"] + GC, writes=["actT"])
            if st == NST - 1:
                for g in range(11):
                    for i in range(4):
                        ch = 4 * g + i
                        tr(ps[0][:2, 128 * i:128 * i + 128], carry[:, ch, :], identf[:, :],
                           reads=["carry", "identf"], writes=["ps0"], sig=(i == 3))
                    cp("act", stfl[g % 2][:2, :], ps[0][:2, :], reads=["ps0"], writes=["stf%d" % (g % 2)])
                    dma("sp", ffn_p[:, 512 * g:512 * g + 512], stfl[g % 2][:2, :], reads=["stf%d" % (g % 2)])
            if samp:
                for g in range(11):
                    for i in range(4):
                        ch = 4 * g + i
                        tr(ps[0][:32, 128 * i:128 * i + 128], carry_s[:, ch, :, :].rearrange("p s j -> p (s j)"), identf[:, :],
                           reads=["carry_s", "identf"], writes=["ps0"], sig=(i == 3))
                    cp("act", stfl[g % 2][:32, :], ps[0][:32, :], reads=["ps0"], writes=["stf%d" % (g % 2)])
                    dma("sp", ffn_s[:, 512 * g:512 * g + 512], stfl[g % 2][:32, :], reads=["stf%d" % (g % 2)])
            for j in range(ntile):
                t = 4 * st + j
                for n in range(2):
                    for i in range(NFC):
                        mm(ps[1 + n][:, :], actT[:, i, 128 * j:128 * j + 128], wdn[:, i, 512 * n:512 * n + 512], i == 0, i == NFC - 1,
                           reads=["actT", "wdn"], writes=["ps%d" % (1 + n)])
                for n in range(2):
                    tt("dve", x2[:, 512 * n:512 * n + 512], ps[1 + n][:, :], x1[:, j, 512 * n:512 * n + 512], ALU.add,
                       reads=["ps%d" % (1 + n), "x1_%d" % j] + GC, writes=["x2"])
                act(junk2[:, :], x2[:, :], AF.Square, reads=["x2"], writes=["junk2", "sy_ss"], accum_out=s2(6))
                rstd_from_ss(s2(6), D, s2(7), s2(8), "sy_")
                stt("dve", x2[:, :], x2[:, :], s2(8), gfb[:, :], ALU.mult, ALU.mult, reads=["x2", "sy_", "gfb"], writes=["x2"])
                dma("sp", (y_s if samp else y_p[128 * t:128 * t + 128, :]), x2[:, :], reads=["x2"])


    if cfg.get("CMODE", "new") == "new":
        phaseC_new()
    else:
        phaseC_old()
    P.emit(nc)
    return nc


def _bf(a):
    return np.ascontiguousarray(a).astype(ml_dtypes.bfloat16)


def host_consts(cfg):
    SEQ, PAST = cfg["SEQ"], cfg["PAST"]
    NT = SEQ + 128
    pos = np.concatenate([np.arange(SEQ), PAST + (np.arange(128) % DSEQ)]).astype(np.float32)
    inv = (np.float32(10000.0) ** (-(np.arange(16, dtype=np.float32) * np.float32(2.0) / np.float32(DR)))).astype(np.float32)
    ang = (pos[:, None] * inv[None, :]).astype(np.float32)
    cos = np.cos(ang).astype(np.float32)
    sin = np.sin(ang).astype(np.float32)
    cstok = np.concatenate([cos, sin], axis=1)
    cosT = np.concatenate([cos, cos], axis=1).T.copy()
    sinT = np.concatenate([-sin, sin], axis=1).T.copy()
    tri = (np.arange(128)[:, None] <= np.arange(128)[None, :]).astype(np.float32)
    k = np.arange(128)
    sm = np.zeros((128, SPC, NH, DSEQ), np.float32)
    for s in range(SPC):
        for t in range(DSEQ):
            sm[(k // DSEQ == s) & (k % DSEQ <= t), s, :, t] = 1.0
    return dict(cstok=cstok, cosT=cosT, sinT=sinT, identf=np.eye(128, dtype=np.float32), identb=_bf(np.eye(128)),
                tri=_bf(tri), smask=_bf(sm.reshape(128, SPC * 64)))


def host_weights(inp):
    f = np.float32
    w_in = inp["w_in"][0].reshape(8, 128, DIN).transpose(1, 0, 2)
    wuq = inp["w_uq"][0].reshape(DCQ, NH * 96)
    wuqs = inp["w_uq"][0].copy()
    wuqs[:, :, 64:80] = inp["w_uq"][0][:, :, 80:96]
    wuqs[:, :, 80:96] = inp["w_uq"][0][:, :, 64:80]
    wuqs = wuqs.reshape(DCQ, NH * 96)
    wk = np.zeros((3, 128, NH, 96), f)
    wk[0, :, :, 0:64] = inp["w_uk"][0][0:128]
    wk[1, :, :, 0:64] = inp["w_uk"][0][128:256]
    for h in range(NH):
        wk[2, 64:96, h, 64:96] = np.eye(32, dtype=f)
    vec = np.zeros((128, 64), f)
    vec[:, 0:8] = inp["g_attn_norm"][0].reshape(8, 128).T
    vec[:, 8:11] = inp["g_q_norm"][0].reshape(3, 128).T
    vec[:, 11:15] = inp["g_conv_out"][0].reshape(4, 128).T
    vec[:, 15:19] = inp["g_attn_out"][0].reshape(4, 128).T
    vec[:, 19:27] = inp["g_ffn_norm"][0].reshape(8, 128).T
    vec[:, 27:31] = inp["b_dw"][0].reshape(4, 128).T
    wfv = np.zeros((128, 44, 4), f)
    wfv[:, :, 0:3] = inp["w_ffn_dw"][0].reshape(3, 44, 128).transpose(2, 1, 0)
    wfv[:, :, 3] = inp["b_ffn_dw"][0].reshape(44, 128).T
    rowv = np.zeros((1, 3072), f)
    rowv[0, 0:256] = inp["g_kv_norm"][0]
    rowv[0, 256:768] = inp["g_conv_ln"][0]
    rowv[0, 768:1280] = inp["b_conv_ln"][0]
    rowv[0, 1280:2304] = inp["g_final"]
    rowv[0, 2304:2816] = inp["b_dw"][0]
    wup = inp["w_up"][0].reshape(8, 128, 2, NFC, 128).transpose(3, 1, 0, 2, 4).reshape(NFC, 128, 8, 256)
    c = np.ascontiguousarray
    return dict(
        w_in=c(w_in), w_uq=c(wuq.reshape(3, 128, 768).transpose(1, 0, 2)), w_uqs=c(wuqs.reshape(3, 128, 768).transpose(1, 0, 2)),
        w_k=c(wk.reshape(3, 128, 768).transpose(1, 0, 2)), w_ukT=c(inp["w_uk"][0].transpose(2, 1, 0)),
        w_uv=c(inp["w_uv"][0].reshape(2, 128, 512).transpose(1, 0, 2)),
        w_o=c(inp["w_o"][0].reshape(8, 128, D).transpose(1, 0, 2)), w_up=c(wup),
        w_dn=c(inp["w_down"][0].reshape(NFC, 128, D).transpose(1, 0, 2)),
        wdw=c(inp["w_dw"][0].reshape(CW, 4, 128).transpose(2, 1, 0)), vecs=vec, wf=wfv, rowv=rowv)


_NC_CACHE = {}


def run(inp, cfg):
    SEQ, NPG, NPOOL = cfg["SEQ"], cfg["NPG"], cfg["NPOOL"]
    key = tuple(sorted(cfg.items()))
    if key not in _NC_CACHE:
        _NC_CACHE[key] = build(cfg)
    nc = _NC_CACHE[key]
    consts = host_consts(cfg)
    wts = host_weights(inp)
    cache2d = np.ascontiguousarray(inp["cache_kv_latent"][0]).reshape(NPOOL * PAGE, DC + DR)
    in_maps = []
    for c in range(8):
        m = dict(consts)
        m.update(wts)
        m["xp"] = np.ascontiguousarray(inp["x_prompt"][c])
        m["xs"] = np.ascontiguousarray(inp["x_sample"][SPC * c:SPC * c + SPC]).reshape(128, D)
        m["cache"] = cache2d
        m["stc"] = np.ascontiguousarray(inp["state_conv"][0, SPC * c:SPC * c + SPC]).reshape(SPC * 30, DCONV)
        m["stf"] = np.ascontiguousarray(inp["state_ffn_conv"][0, SPC * c:SPC * c + SPC]).reshape(SPC * 2, 2 * DFF)
        m["ptab"] = np.ascontiguousarray(inp["page_table"][SPC * c:SPC * c + SPC]).reshape(1, SPC * NPG).astype(np.int32)
        in_maps.append(m)
    res = run_bass_kernel_spmd(nc, in_maps, core_ids=list(range(8)))
    R = res.results
    y_p = np.stack([R[c]["y_p"] for c in range(8)])
    y_s = np.concatenate([R[c]["y_s"].reshape(SPC, DSEQ, D) for c in range(8)])
    kv_p = np.stack([R[c]["kv_p"] for c in range(8)])[None]
    conv_p = np.stack([R[c]["conv_p"] for c in range(8)])[None]
    ffn_p = np.stack([R[c]["ffn_p"] for c in range(8)])[None]
    kv_s = np.concatenate([R[c]["kv_s"].reshape(SPC, DSEQ, DC + DR) for c in range(8)])[None]
    conv_s = np.concatenate([R[c]["conv_s"] for c in range(8)])[None]
    ffn_s = np.concatenate([R[c]["ffn_s"].reshape(SPC, 2, 2 * DFF) for c in range(8)])[None]
    return tuple(np.asarray(a, np.float32) for a in (y_p, y_s, kv_p, conv_p, ffn_p, kv_s, conv_s, ffn_s))


def kernel(**inputs):
    inp = {k: np.asarray(v) for k, v in inputs.items()}
    return run(inp, CFG_FULL)
```

```python
import numpy as np
import ml_dtypes
import concourse.bass as bass
import concourse.mybir as mybir
from concourse.bass_utils import run_bass_kernel_spmd

F32 = mybir.dt.float32
BF = mybir.dt.bfloat16
I32 = mybir.dt.int32
AF = mybir.ActivationFunctionType
ALU = mybir.AluOpType

D = 1024
NH = 8
DC = 256
DCQ = 384
DR = 32
DCONV = 512
CW = 31
DIN = 1696
DFF = 2816
NFC = 22
EPS = 1e-6
SCALE = 96.0 ** -0.5
SPC = 16
DSEQ = 8
PAGE = 128

CFG_FULL = dict(SEQ=4096, NPG=128, NPOOL=20480, PAST=16384)


class Prog:
    ENGS = ("pe", "act", "dve", "pool", "sp")
    RING = 8

    def __init__(self):
        self.ops = []
        self.lastw = {}
        self.readers = {}
        self.cnt = {e: 0 for e in self.ENGS}
        self.dcnt = {e: 0 for e in self.ENGS}
        self.pending_pe = []
        self.cur_barrier = None

    def barrier(self, fn):
        keys = list(set(self.lastw) | set(self.readers))
        o = self.op("pool", fn, writes=keys)
        self.cur_barrier = o["idx"]

    def op(self, eng, fn, reads=(), writes=(), dma=False, sig=True):
        o = dict(eng=eng, fn=fn, dma=dma, sig=sig, deps=set(), idx=len(self.ops))
        deps = set()
        if self.cur_barrier is not None:
            deps.add(self.cur_barrier)
        for r in reads:
            if r in self.lastw:
                deps.add(self.lastw[r])
        for w in writes:
            if w in self.lastw:
                deps.add(self.lastw[w])
            for rd in self.readers.get(w, ()):
                deps.add(rd)
        deps.discard(o["idx"])
        o["deps"] = deps
        if dma:
            i = self.dcnt[eng]
            self.dcnt[eng] += 1
            o["dsem"] = (eng, i % self.RING)
            o["dval"] = 16 * (i // self.RING + 1)
            o["dprev"] = 16 * (i // self.RING)
        else:
            if sig:
                self.cnt[eng] += 1
                o["sval"] = self.cnt[eng]
                if eng == "pe":
                    for p in self.pending_pe:
                        p["sval"] = self.cnt[eng]
                    self.pending_pe = []
            else:
                assert eng == "pe"
                self.pending_pe.append(o)
        self.ops.append(o)
        for w in writes:
            self.lastw[w] = o["idx"]
            self.readers[w] = []
        for r in reads:
            if r not in writes:
                self.readers.setdefault(r, []).append(o["idx"])
        return o

    def check(self):
        assert not self.pending_pe
        ops = self.ops
        queues = {e: [o for o in ops if o["eng"] == e] for e in self.ENGS}
        pos = {e: 0 for e in self.ENGS}
        done = [False] * len(ops)
        sigdone = {e: 0 for e in self.ENGS}
        ddone = {}
        dq = {e: [o for o in queues[e] if o["dma"]] for e in self.ENGS}
        progress = True
        while progress:
            progress = False
            for e in self.ENGS:
                while pos[e] < len(queues[e]):
                    o = queues[e][pos[e]]
                    ok = True
                    for d in o["deps"]:
                        do = ops[d]
                        if do["dma"]:
                            if not done[d]:
                                ok = False
                                break
                        elif do["eng"] == "pe" and e == "pe":
                            continue
                        elif sigdone[do["eng"]] < do["sval"]:
                            ok = False
                            break
                    if not ok:
                        break
                    done[o["idx"]] = True
                    if not o["dma"] and o["sig"]:
                        sigdone[e] = o["sval"]
                    pos[e] += 1
                    progress = True
        stuck = {e: (pos[e], len(queues[e])) for e in self.ENGS if pos[e] < len(queues[e])}
        if stuck:
            for e in stuck:
                o = queues[e][pos[e]]
                print("STUCK", e, o["idx"], [(d, ops[d]["eng"], ops[d].get("sval"), done[d]) for d in o["deps"] if not (done[d] and (ops[d]["dma"] or sigdone[ops[d]["eng"]] >= ops[d].get("sval", 0)))][:6])
            raise RuntimeError("deadlock in program: %s" % stuck)

    def emit(self, nc):
        assert not self.pending_pe
        self.check()
        import contextlib
        with contextlib.ExitStack() as es:
            csem = {e: es.enter_context(nc.semaphore("c_" + e)) for e in ("pe", "act", "dve", "pool")}
            dsem = {}
            for e in ("sp", "pool", "act"):
                if self.dcnt[e]:
                    for r in range(self.RING):
                        dsem[(e, r)] = es.enter_context(nc.semaphore("d_%s%d" % (e, r)))
            block = es.enter_context(nc.Block())
            ops = self.ops

            def run(engname, eng):
                waited = {}

                def wait(sem, key, val):
                    if waited.get(key, 0) >= val:
                        return
                    waited[key] = val
                    eng.wait_ge(sem, val)

                for o in ops:
                    if o["eng"] != engname:
                        continue
                    for d in sorted(o["deps"]):
                        do = ops[d]
                        if do["dma"]:
                            wait(dsem[do["dsem"]], do["dsem"], do["dval"])
                        else:
                            if do["eng"] == "pe" and engname == "pe":
                                continue
                            wait(csem[do["eng"]], do["eng"], do["sval"])
                    if o["dma"]:
                        if o["dprev"] > 0:
                            wait(dsem[o["dsem"]], o["dsem"], o["dprev"])
                        ins = o["fn"](eng)
                        ins.then_inc(dsem[o["dsem"]], 16)
                    else:
                        ins = o["fn"](eng)
                        if o["sig"]:
                            ins.then_inc(csem[engname], 1)
                if engname == "sp":
                    for (e, r), s in dsem.items():
                        n = self.dcnt[e]
                        k = (n - 1 - r) // self.RING + 1 if n > r else 0
                        if k > 0:
                            eng.wait_ge(s, 16 * k)
                    for e in ("pe", "act", "dve", "pool"):
                        if self.cnt[e]:
                            eng.wait_ge(csem[e], self.cnt[e])

            @block.sync
            def _(e):
                run("sp", e)

            @block.tensor
            def _(e):
                run("pe", e)

            @block.scalar
            def _(e):
                run("act", e)

            @block.vector
            def _(e):
                run("dve", e)

            @block.gpsimd
            def _(e):
                run("pool", e)


def build(cfg):
    SEQ, NPG, NPOOL, PAST = cfg["SEQ"], cfg["NPG"], cfg["NPOOL"], cfg["PAST"]
    NPT = SEQ // 128
    NTL = NPT + 1
    NT = NTL * 128
    NST = SEQ // 512
    NPAGES = SPC * NPG

    nc = bass.Bass("TRN2", target_bir_lowering=False)
    P = Prog()

    def din(name, shape, dt=F32):
        return nc.dram_tensor(name, list(shape), dt, kind="ExternalInput").ap()

    def dout(name, shape, dt=F32):
        return nc.dram_tensor(name, list(shape), dt, kind="ExternalOutput").ap()

    xp = din("xp", [SEQ, D])
    xs = din("xs", [128, D])
    cache = din("cache", [NPOOL * PAGE, DC + DR])
    stc = din("stc", [SPC * 30, DCONV])
    stf = din("stf", [SPC * 2, 2 * DFF])
    ptab = din("ptab", [1, NPAGES], I32)
    w_in = din("w_in", [128, 8, DIN])
    w_uq = din("w_uq", [128, 3, 768])
    w_uqs = din("w_uqs", [128, 3, 768])
    w_k = din("w_k", [128, 3, 768])
    w_ukT = din("w_ukT", [64, NH, DC])
    w_uv = din("w_uv", [128, 2, 512])
    w_o = din("w_o", [128, 8, D])
    w_up = din("w_up", [NFC, 128, 8, 256])
    w_dn = din("w_dn", [128, NFC, D])
    wdw = din("wdw", [128, 4, CW])
    vecs = din("vecs", [128, 64])
    wf = din("wf", [128, 44, 4])
    rowv = din("rowv", [1, 3072])
    cstok = din("cstok", [NT, 32])
    cosT = din("cosT", [32, NT])
    sinT = din("sinT", [32, NT])
    identf_d = din("identf", [128, 128])
    identb_d = din("identb", [128, 128], BF)
    tri_d = din("tri", [128, 128], BF)
    smask_d = din("smask", [128, SPC * 64], BF)

    y_p = dout("y_p", [SEQ, D])
    y_s = dout("y_s", [128, D])
    kv_p = dout("kv_p", [SEQ, DC + DR])
    conv_p = dout("conv_p", [30, DCONV])
    ffn_p = dout("ffn_p", [2, 2 * DFF])
    kv_s = dout("kv_s", [128, DC + DR])
    conv_s = dout("conv_s", [SPC, 30, DCONV])
    ffn_s = dout("ffn_s", [SPC * 2, 2 * DFF])

    class Arena:
        def __init__(self):
            self.off = 16512
            self.n = 0

        def alloc(self, shape, dt, name=None):
            esz = 4 if dt in (F32, I32) else 2
            nbytes = int(np.prod(shape[1:])) * esz
            self.off = (self.off + 63) // 64 * 64
            self.n += 1
            t = nc.alloc_sbuf_tensor_at("sb%d_%s" % (self.n, name or ""), list(shape), dt, offset=self.off)
            self.last = self.off
            self.off += nbytes
            assert self.off <= 229376, ("SBUF overflow", self.off, name)
            return t

        def at(self, shape, dt, name, offset):
            self.n += 1
            return nc.alloc_sbuf_tensor_at("sb%d_%s" % (self.n, name), list(shape), dt, offset=offset)

        def mark(self):
            return self.off

        def reset(self, m):
            self.off = m

    A = Arena()
    identf = A.alloc([128, 128], F32, "identf")
    identb = A.alloc([128, 128], BF, "identb")
    tri = A.alloc([128, 128], BF, "tri")
    vec = A.alloc([128, 64], F32, "vec")
    epsT = A.alloc([128, 1], F32, "eps")
    rstd_c = A.alloc([128, NTL], F32, "rstdc")
    bar_t = A.alloc([128, 2], F32, "bar")
    GA, GQ, GCO, GAO, GF, BDW = 0, 8, 11, 15, 19, 27
    convT = A.alloc([128, 4, NT], BF, "convT")
    attn_tok = A.alloc([128, max(NTL, CW), 512], BF, "attn_tok")
    attn_off = A.last
    m_persist = A.mark()
    cqnT = A.alloc([128, 3, NT], BF, "cqnT")
    kvT = A.alloc([128, 3, NT], BF, "kvT")
    kvbf_s = A.alloc([128, 289], BF, "kvbf_s")
    m_ab = A.mark()

    ps = [nc.alloc_psum_tensor("ps%d" % i, [128, 512], F32) for i in range(8)]

    def psb(i):
        return ps[i][:, :].bitcast(BF)

    def dma(q, out, in_, reads=(), writes=()):
        def f(e):
            return e.dma_start(out=out, in_=in_)
        return P.op(q, f, reads=reads, writes=writes, dma=True)

    def mm(out, lhsT, rhs, start, stop, reads=(), writes=(), sig=None):
        def f(e):
            return e.matmul(out, lhsT, rhs, start=start, stop=stop)
        return P.op("pe", f, reads=reads, writes=writes, sig=(stop if sig is None else sig))

    def tr(out, in_, ident, reads=(), writes=(), sig=True):
        def f(e):
            return e.transpose(out, in_, ident)
        return P.op("pe", f, reads=reads, writes=writes, sig=sig)

    def act(out, in_, func, reads=(), writes=(), **kw):
        def f(e):
            return e.activation(out=out, in_=in_, func=func, **kw)
        return P.op("act", f, reads=reads, writes=writes)

    def tt(eng, out, in0, in1, op, reads=(), writes=()):
        def f(e):
            return e.tensor_tensor(out=out, in0=in0, in1=in1, op=op)
        return P.op(eng, f, reads=reads, writes=writes)

    def ts(eng, out, in0, s1, s2, op0, op1=None, reads=(), writes=()):
        def f(e):
            if op1 is None:
                return e.tensor_scalar(out=out, in0=in0, scalar1=s1, scalar2=None, op0=op0)
            return e.tensor_scalar(out=out, in0=in0, scalar1=s1, scalar2=s2, op0=op0, op1=op1)
        return P.op(eng, f, reads=reads, writes=writes)

    def stt(eng, out, in0, scalar, in1, op0, op1, reads=(), writes=()):
        eng = "dve"

        def f(e):
            return e.scalar_tensor_tensor(out=out, in0=in0, scalar=scalar, in1=in1, op0=op0, op1=op1)
        return P.op(eng, f, reads=reads, writes=writes)

    def cp(eng, out, in_, reads=(), writes=()):
        if eng == "act":
            return act(out, in_, AF.Copy, reads=reads, writes=writes)

        def f(e):
            return e.tensor_copy(out=out, in_=in_)
        return P.op(eng, f, reads=reads, writes=writes)

    def memset(eng, ap, val, writes=()):
        def f(e):
            return e.memset(ap, val)
        return P.op(eng, f, writes=writes)

    def recip(out, in_, reads=(), writes=()):
        def f(e):
            return e.reciprocal(out=out, in_=in_)
        return P.op("dve", f, reads=reads, writes=writes)

    def rstd_from_ss(ss, n, tmp, out, key):
        act(tmp, ss, AF.Sqrt, reads=[key + "ss", "eps"], writes=[key + "tmp"], scale=1.0 / n, bias=epsT[:, 0:1])
        recip(out, tmp, reads=[key + "tmp"], writes=[key])

    def bcast(v, shape):
        return v.unsqueeze(2).to_broadcast(list(shape))

    dma("sp", identf[:, :], identf_d, writes=["identf"])
    dma("sp", identb[:, :], identb_d, writes=["identb"])
    dma("sp", tri[:, :], tri_d, writes=["tri"])
    dma("sp", vec[:, :], vecs, writes=["vec"])
    memset("dve", epsT[:, :], EPS, writes=["eps"])

    win = A.alloc([128, 8, DIN], BF, "win")
    xt = [A.alloc([128, D], F32, "xt%d" % i) for i in range(2)]
    junk = A.alloc([128, D], BF, "junk")
    xnb = A.alloc([128, D], BF, "xnb")
    hT = A.alloc([128, 8, 128], BF, "hT")
    small = A.alloc([128, 32], F32, "small")
    cqnb = A.alloc([128, DCQ], BF, "cqnb")
    kvrow = [A.alloc([128, 288], F32, "kvrow%d" % i) for i in range(2)]
    kvbf = A.alloc([128, 288], BF, "kvbf")
    krs = A.alloc([128, 32], F32, "krs")
    rtmp = A.alloc([128, 64], F32, "rtmp")
    cs = A.alloc([128, NTL, 32], F32, "cs")
    gkvb = A.alloc([128, 256], F32, "gkvb")
    glnb = A.alloc([128, 512], F32, "glnb")
    blnb = A.alloc([128, 512], F32, "blnb")
    sig_t = A.alloc([128, 4, 128], F32, "sig")
    gluT = A.alloc([128, 4, 30 + 128], F32, "gluT")
    gluS = A.alloc([128, 4, SPC, 38], F32, "gluS")
    yT = A.alloc([128, 4, 128], F32, "yT")
    zt = A.alloc([128, 512], F32, "zt")
    s32 = A.alloc([128, 512], F32, "s32")
    sbf = A.alloc([128, 512], BF, "sbf")
    wdw_t = A.alloc([128, 4, CW], F32, "wdw")
    stt_t = A.alloc([120, 512], F32, "stt")
    gtok = A.alloc([128, 512], F32, "gtok")
    bnst = A.alloc([128, 8], F32, "bnst")
    diag = A.at([128, 4, CW, 128], BF, "diag", attn_off)
    gluTbl = [A.alloc([128, 4, 30 + 128], BF, "gluTb%d" % i) for i in range(2)]
    ones1 = A.alloc([1, 128], BF, "ones1")
    bdwf = A.alloc([1, 512], F32, "bdwf")
    bdwt = A.alloc([1, 512], F32, "bdwt")
    bhi = A.alloc([1, 512], BF, "bhi")
    blo = A.alloc([1, 512], BF, "blo")
    assert A.mark() <= 229376, A.mark()

    dma("pool", win[:, :, :], w_in, writes=["win"])
    dma("sp", cs[:, :, :], cstok.rearrange("(t p) c -> p t c", p=128), writes=["cs"])
    dma("sp", gkvb[:, :], rowv[:, 0:256].partition_broadcast(128), writes=["gkvb"])
    dma("sp", glnb[:, :], rowv[:, 256:768].partition_broadcast(128), writes=["glnb"])
    dma("sp", blnb[:, :], rowv[:, 768:1280].partition_broadcast(128), writes=["blnb"])
    dma("sp", wdw_t[:, :, :], wdw, writes=["wdw"])
    memset("pool", gluT[:, :, :], 0.0, writes=["gluT"])
    for i in range(2):
        memset("pool", gluTbl[i][:, :, :], 0.0, writes=["gluTb%d" % i])
    memset("pool", ones1[:, :], 1.0, writes=["ones1"])
    dma("sp", bdwf[:, :], rowv[:, 2304:2816], writes=["bdwf"])
    cp("dve", bhi[:, :], bdwf[:, :], reads=["bdwf"], writes=["bhi"])
    tt("dve", bdwt[:, :], bdwf[:, :], bhi[:, :], ALU.subtract, reads=["bdwf", "bhi"], writes=["bdwt"])
    cp("dve", blo[:, :], bdwt[:, :], reads=["bdwt"], writes=["blo"])
    for c in range(4):
        for k in range(CW):
            if k % 2:
                act(diag[:, c, k, :], identb[:, :], AF.Copy, reads=["identb", "wdw"], writes=["diag"], scale=wdw_t[:, c, k:k + 1])
            else:
                ts("dve", diag[:, c, k, :], identb[:, :], wdw_t[:, c, k:k + 1], None, ALU.mult,
                   reads=["identb", "wdw"], writes=["diag"])

    for r in range(4):
        dma("sp", stt_t[:, :], stc[120 * r:120 * r + 120, :], writes=["stt"])
        for c in range(4):
            tr(ps[0][:, 128 * c:128 * c + 120], stt_t[:, 128 * c:128 * c + 128], identf[:120, :120],
               reads=["stt", "identf"], writes=["ps0"], sig=(c == 3))
        for c in range(4):
            cp("dve", gluS[:, c, 4 * r:4 * r + 4, 0:30],
               ps[0][:, 128 * c:128 * c + 120].rearrange("p (s j) -> p s j", j=30),
               reads=["ps0"], writes=["gluS"])
    dma("sp", conv_s[:, 0:22, :], stc.rearrange("(s j) c -> s j c", j=30)[:, 8:30, :])

    sm = lambda i: small[:, i:i + 1]

    def frontA(t):
        samp = (t == NPT)
        xsrc = xs if samp else xp[128 * t:128 * t + 128, :]
        x_t = xt[t % 2]
        xk = "xt%d" % (t % 2)
        dma("sp", x_t[:, :], xsrc, writes=[xk])
        act(junk[:, :], x_t[:, :], AF.Square, reads=[xk], writes=["junk", "ss0ss"], accum_out=sm(0))
        rstd_from_ss(sm(0), D, sm(1), sm(2), "ss0")
        act(xnb[:, :], x_t[:, :], AF.Copy, reads=[xk, "ss0"], writes=["xnb"], scale=sm(2))
        for k in range(8):
            tr(psb(0)[:, 128 * k:128 * k + 128], xnb[:, 128 * k:128 * k + 128], identb[:, :],
               reads=["xnb", "identb"], writes=["ps0"], sig=(k == 7))
        tt("dve", hT[:, :, :], psb(0)[:, :].rearrange("p (k n) -> p k n", n=128),
           bcast(vec[:, GA:GA + 8], [128, 8, 128]), ALU.mult, reads=["ps0", "vec"], writes=["hT"])
        for k in range(8):
            mm(ps[1][:, 0:384], hT[:, k, :], win[:, k, 0:384], k == 0, k == 7, reads=["hT", "win"], writes=["ps1"])
        for k in range(8):
            mm(ps[2][:, 0:288], hT[:, k, :], win[:, k, 384:672], k == 0, k == 7, reads=["hT", "win"], writes=["ps2"])
        for c in range(8):
            bank = 3 if c < 4 else 4
            cc = c % 4
            for k in range(8):
                mm(ps[bank][:, 128 * cc:128 * cc + 128], win[:, k, 672 + 128 * c:672 + 128 * c + 128], hT[:, k, :],
                   k == 0, k == 7, reads=["hT", "win"], writes=["ps%d" % bank], sig=(k == 7 and cc == 3))
        act(junk[:, 0:384], ps[1][:, 0:384], AF.Square, reads=["ps1"], writes=["junk", "ss1ss"], accum_out=sm(3))
        rstd_from_ss(sm(3), DCQ, sm(4), sm(5), "ss1")
        act(cqnb[:, :], ps[1][:, 0:384], AF.Copy, reads=["ps1", "ss1"], writes=["cqnb"], scale=sm(5))
        for k in range(3):
            tr(psb(5)[:, 128 * k:128 * k + 128], cqnb[:, 128 * k:128 * k + 128], identb[:, :],
               reads=["cqnb", "identb"], writes=["ps5"], sig=(k == 2))
        tt("dve", cqnT[:, :, 128 * t:128 * t + 128], psb(5)[:, 0:384].rearrange("p (k n) -> p k n", n=128),
           bcast(vec[:, GQ:GQ + 3], [128, 3, 128]), ALU.mult, reads=["ps5", "vec"], writes=["cqnT"])
        kr_ = kvrow[t % 2]
        kk = "kvrow%d" % (t % 2)
        act(junk[:, 0:256], ps[2][:, 0:256], AF.Square, reads=["ps2"], writes=["junk", "ss2ss"], accum_out=sm(6))
        rstd_from_ss(sm(6), DC, sm(7), sm(8), "ss2")
        stt("dve", kr_[:, 0:256], ps[2][:, 0:256], sm(8), gkvb[:, :], ALU.mult, ALU.mult,
            reads=["ps2", "ss2", "gkvb"], writes=[kk])
        cp("act", krs[:, :], ps[2][:, 256:288], reads=["ps2"], writes=["krs"])
        cst = cs[:, t, :]
        tt("pool", rtmp[:, 0:16], krs[:, 0:16], cst[:, 0:16], ALU.mult, reads=["krs", "cs"], writes=["rt0"])
        tt("pool", rtmp[:, 16:32], krs[:, 16:32], cst[:, 16:32], ALU.mult, reads=["krs", "cs"], writes=["rt1"])
        tt("pool", kr_[:, 256:272], rtmp[:, 0:16], rtmp[:, 16:32], ALU.subtract, reads=["rt0", "rt1"], writes=[kk])
        tt("pool", rtmp[:, 32:48], krs[:, 16:32], cst[:, 0:16], ALU.mult, reads=["krs", "cs"], writes=["rt2"])
        tt("pool", rtmp[:, 48:64], krs[:, 0:16], cst[:, 16:32], ALU.mult, reads=["krs", "cs"], writes=["rt3"])
        tt("pool", kr_[:, 272:288], rtmp[:, 32:48], rtmp[:, 48:64], ALU.add, reads=["rt2", "rt3"], writes=[kk])
        dma("sp", (kv_s if samp else kv_p[128 * t:128 * t + 128, :]), kr_[:, :], reads=[kk])
        cp("act", kvbf[:, :], kr_[:, :], reads=[kk], writes=["kvbf"])
        if samp:
            cp("act", kvbf_s[:, 0:288], kr_[:, :], reads=[kk], writes=["kvbf_s"])
            memset("pool", kvbf_s[:, 288:289], 1.0, writes=["kvbf_s1"])
        tr(psb(5)[:, 0:128], kvbf[:, 0:128], identb[:, :], reads=["kvbf", "identb"], writes=["ps5"], sig=False)
        tr(psb(5)[:, 128:256], kvbf[:, 128:256], identb[:, :], reads=["kvbf", "identb"], writes=["ps5"], sig=False)
        tr(psb(5)[:96, 256:384], kvbf[:, 192:288], identb[:, :], reads=["kvbf", "identb"], writes=["ps5"])
        cp("dve", kvT[:, 0:2, 128 * t:128 * t + 128], psb(5)[:, 0:256].rearrange("p (k n) -> p k n", n=128),
           reads=["ps5"], writes=["kvT"])
        cp("dve", kvT[:96, 2, 128 * t:128 * t + 128], psb(5)[:96, 256:384], reads=["ps5"], writes=["kvT"])
        act(sig_t[:, :, :], ps[4][:, :].rearrange("p (c n) -> p c n", n=128), AF.Sigmoid, reads=["ps4"], writes=["sig"])
        if not samp:
            tt("dve", gluT[:, :, 30:158], ps[3][:, :].rearrange("p (c n) -> p c n", n=128), sig_t[:, :, :], ALU.mult,
               reads=["ps3", "sig"], writes=["gluT"])
            gb, gbk = gluTbl[t % 2], "gluTb%d" % (t % 2)
            cp("pool", gb[:, :, 30:158], gluT[:, :, 30:158], reads=["gluT"], writes=[gbk])
            if t > 0:
                cp("pool", gb[:, :, 0:30], gluTbl[(t - 1) % 2][:, :, 128:158], reads=["gluTb%d" % ((t - 1) % 2)], writes=[gbk])
        else:
            for c in range(4):
                tt("dve", gluS[:, c, :, 30:38], ps[3][:, 128 * c:128 * c + 128].rearrange("p (s j) -> p s j", j=8),
                   sig_t[:, c, :].rearrange("p (s j) -> p s j", j=8), ALU.mult, reads=["ps3", "sig"], writes=["gluS"])
        if t == NPT - 1:
            for c in range(4):
                tr(ps[6][:30, 128 * c:128 * c + 128], gluT[:, c, 128:158], identf[:, :],
                   reads=["gluT", "identf"], writes=["ps6"], sig=(c == 3))
            cp("act", gtok[:30, :], ps[6][:30, :], reads=["ps6"], writes=["gtok"])
            dma("sp", conv_p, gtok[:30, :], reads=["gtok"])
        if samp:
            for c in range(4):
                cp("pool", sig_t[:, c, :].rearrange("p (s j) -> p s j", j=8), gluS[:, c, :, 30:38],
                   reads=["gluS", "sig"], writes=["sig"])
            for c in range(4):
                tr(ps[6][:, 128 * c:128 * c + 128], sig_t[:, c, :], identf[:, :],
                   reads=["sig", "identf"], writes=["ps6"], sig=(c == 3))
            cp("act", gtok[:, :], ps[6][:, :], reads=["ps6"], writes=["gtok"])
            for s in range(SPC):
                dma("sp", conv_s[s, 22:30, :], gtok[8 * s:8 * s + 8, :], reads=["gtok"])

    def backA(t):
        samp = (t == NPT)
        if not samp:
            mm(ps[7][:, 0:512], ones1[0:1, :], bhi[0:1, :], True, False, reads=["ones1", "bhi"], writes=["ps7"], sig=False)
            mm(ps[7][:, 0:512], ones1[0:1, :], blo[0:1, :], False, False, reads=["ones1", "blo"], writes=["ps7"], sig=False)
            for c in range(4):
                for k in range(CW):
                    last = (c == 3 and k == CW - 1)
                    mm(ps[7][:, 128 * c:128 * c + 128], gluTbl[t % 2][:, c, k:k + 128], diag[:, c, k, :], False, last,
                       reads=["gluTb%d" % (t % 2), "diag"], writes=["ps7"], sig=last)
        else:
            gk = "gluS"

            def csrc(c, k):
                return gluS[:, c, :, k:k + 8]

            def cdst(c):
                return yT[:, c, :].rearrange("p (s j) -> p s j", j=8)
            for c in range(4):
                ts("dve", cdst(c), csrc(c, 0), wdw_t[:, c, 0:1], vec[:, BDW + c:BDW + c + 1], ALU.mult, ALU.add,
                   reads=[gk, "wdw", "vec"], writes=["yT%d" % c])
            for k in range(1, CW):
                for c in range(4):
                    stt("dve", cdst(c), csrc(c, k), wdw_t[:, c, k:k + 1], cdst(c), ALU.mult, ALU.add,
                        reads=[gk, "wdw", "yT%d" % c], writes=["yT%d" % c])
        if samp:
            for c in range(4):
                tr(ps[7][:, 128 * c:128 * c + 128], yT[:, c, :], identf[:, :],
                   reads=["yT%d" % c, "identf"], writes=["ps7"], sig=(c == 3))
        P.op("dve", lambda e: e.bn_stats(out=bnst[:, 0:6], in_=ps[7][:, :]), reads=["ps7"], writes=["bnst"])
        P.op("dve", lambda e: e.bn_aggr(out=bnst[:, 6:8], in_=bnst[:, 0:6]), reads=["bnst"], writes=["mv"])
        act(sm(9), bnst[:, 7:8], AF.Sqrt, reads=["mv", "eps"], writes=["lntmp"], scale=1.0, bias=epsT[:, 0:1])
        recip(sm(10), sm(9), reads=["lntmp"], writes=["lnr"])
        ts("dve", sm(11), bnst[:, 6:7], -1.0, sm(10), ALU.mult, ALU.mult, reads=["mv", "lnr"], writes=["lnb"])
        act(zt[:, :], ps[7][:, :], AF.Identity, reads=["ps7", "lnr", "lnb"], writes=["zt"], scale=sm(10), bias=sm(11))
        tt("dve", zt[:, :], zt[:, :], glnb[:, :], ALU.mult, reads=["zt", "glnb"], writes=["zt"])
        tt("pool", zt[:, :], zt[:, :], blnb[:, :], ALU.add, reads=["zt", "blnb"], writes=["zt"])
        act(s32[:, :], zt[:, :], AF.Silu, reads=["zt"], writes=["s32"])
        act(junk[:, 0:512], s32[:, :], AF.Square, reads=["s32"], writes=["junk", "ss3ss"], accum_out=sm(12))
        rstd_from_ss(sm(12), DCONV, sm(13), rstd_c[:, t:t + 1], "ss3")
        cp("dve", sbf[:, :], s32[:, :], reads=["s32"], writes=["sbf"])
        for c in range(4):
            tr(psb(5)[:, 512 + 128 * c:512 + 128 * c + 128], sbf[:, 128 * c:128 * c + 128], identb[:, :],
               reads=["sbf", "identb"], writes=["ps5b"], sig=(c == 3))
        tt("dve", convT[:, :, 128 * t:128 * t + 128], psb(5)[:, 512:1024].rearrange("p (k n) -> p k n", n=128),
           bcast(vec[:, GCO:GCO + 4], [128, 4, 128]), ALU.mult, reads=["ps5b", "vec"], writes=["convT"])


    frontA(0)
    for t in range(NTL):
        if t + 1 < NTL:
            frontA(t + 1)
        backA(t)

    A.reset(m_ab)
    P.barrier(lambda e: e.memset(bar_t[:, :], 0.0))
    wuv = A.alloc([128, 2, 512], BF, "wuv")
    QL = A.alloc([128, 2, NH, 128], BF, "QL")
    QR = A.alloc([128, NH, 128], BF, "QR")
    m_b = A.mark()
    wuq = A.alloc([128, 3, 768], BF, "wuq")
    wuqs = A.alloc([128, 3, 768], BF, "wuqs")
    wk = A.alloc([128, 3, 768], BF, "wk")
    wukT = A.alloc([64, NH, DC], BF, "wukT")
    Khl = [A.alloc([128, SEQ], BF, "Kh%d" % i) for i in range(2)]
    Vhl = [A.alloc([128, NPT, 65], BF, "Vh%d" % i) for i in range(2)]
    Qh = [A.alloc([128, 512], BF, "Qh%d" % i) for i in range(2)]
    ctl = [A.alloc([128, 512], F32, "ct%d" % i) for i in range(2)]
    snl = [A.alloc([128, 512], F32, "sn%d" % i) for i in range(2)]
    t1l = [A.alloc([128, 512], F32, "t1%d" % i) for i in range(2)]
    t2l = [A.alloc([128, 512], F32, "t2%d" % i) for i in range(2)]
    PT = [A.alloc([128, 512], BF, "PT%d" % i) for i in range(4)]
    Osbl = [A.alloc([128, 512], F32, "Osb%d" % i) for i in range(2)]
    rl4l = [A.alloc([128, 4], F32, "rl4%d" % i) for i in range(2)]

    dma("pool", wuq[:, :, :], w_uq, writes=["wuq"])
    dma("pool", wuqs[:, :, :], w_uqs, writes=["wuqs"])
    dma("pool", wk[:, :, :], w_k, writes=["wk"])
    dma("pool", wuv[:, :, :], w_uv, writes=["wuv"])
    dma("pool", wukT[:, :, :], w_ukT, writes=["wukT"])
    for i in range(2):
        memset("pool", Vhl[i][:, :, 64:65], 1.0, writes=["Vh1_%d" % i])

    def emitKV(h):
        Kh, Vh = Khl[h % 2], Vhl[h % 2]
        kk, vk = "Kh%d" % (h % 2), "Vh%d" % (h % 2)
        for g in range(0, NPT, 8):
            n = min(8, NPT - g)
            for i in range(n):
                kt = g + i
                for k in range(2):
                    mm(ps[1][:, 64 * i:64 * i + 64], kvT[:, k, 128 * kt:128 * kt + 128], wuv[:, k, 64 * h:64 * h + 64],
                       k == 0, k == 1, reads=["kvT", "wuv"], writes=["ps1"], sig=(k == 1 and i == n - 1))
            cp("act", Vh[:, g:g + n, 0:64], ps[1][:, 0:64 * n].rearrange("p (i v) -> p i v", v=64),
               reads=["ps1"], writes=[vk])
        for st in range(NST):
            for k in range(3):
                lo = 64 if k == 2 else 0
                hi = 96 if k == 2 else 128
                mm(ps[1][:96, :], wk[lo:hi, k, 96 * h:96 * h + 96], kvT[lo:hi, k, 512 * st:512 * st + 512],
                   k == 0, k == 2, reads=["kvT", "wk"], writes=["ps1"])
            cp("dve", Kh[:96, 512 * st:512 * st + 512], ps[1][:96, :], reads=["ps1"], writes=[kk])

    units = [(h, qs) for h in range(NH) for qs in range(NST + 1)]

    def emitQ(ui):
        h, qs = units[ui]
        samp = (qs == NST)
        T = 128 if samp else 512
        q0g = 512 * qs
        u = ui % 2
        ct, sn, Q, t1, t2 = ctl[u], snl[u], Qh[u], t1l[u], t2l[u]
        dma("sp", ct[64:96, 0:T], cosT[:, q0g:q0g + T], writes=["ct%d" % u])
        dma("sp", sn[64:96, 0:T], sinT[:, q0g:q0g + T], writes=["sn%d" % u])
        for k in range(3):
            mm(ps[2][:96, 0:T], wuq[:, k, 96 * h:96 * h + 96], cqnT[:, k, q0g:q0g + T], k == 0, k == 2,
               reads=["cqnT", "wuq"], writes=["ps2"])
        for k in range(3):
            mm(ps[3][:96, 0:T], wuqs[:, k, 96 * h:96 * h + 96], cqnT[:, k, q0g:q0g + T], k == 0, k == 2,
               reads=["cqnT", "wuqs"], writes=["ps3"])
        qk = "Q%d" % u
        cp("act", Q[0:64, 0:T], ps[2][0:64, 0:T], reads=["ps2"], writes=[qk])
        tt("dve", t1[64:96, 0:T], ps[2][64:96, 0:T], ct[64:96, 0:T], ALU.mult, reads=["ps2", "ct%d" % u], writes=["t1%d" % u])
        tt("dve", t2[64:96, 0:T], ps[3][64:96, 0:T], sn[64:96, 0:T], ALU.mult, reads=["ps3", "sn%d" % u], writes=["t2%d" % u])
        tt("pool", Q[64:96, 0:T], t1[64:96, 0:T], t2[64:96, 0:T], ALU.add, reads=["t1%d" % u, "t2%d" % u], writes=[qk])

    emitKV(0)
    emitQ(0)
    gkt = 0
    for ui, (h, qs) in enumerate(units):
        samp = (qs == NST)
        u = ui % 2
        Q = Qh[u]
        qk = "Q%d" % u
        Kh, Vh = Khl[h % 2], Vhl[h % 2]
        kk, vk, v1k = "Kh%d" % (h % 2), "Vh%d" % (h % 2), "Vh1_%d" % (h % 2)
        if samp:
            for c in range(2):
                mm(ps[1][:, 128 * c:128 * c + 128], wukT[:, h, 128 * c:128 * c + 128], Q[0:64, 0:128], True, True,
                   reads=["wukT", qk], writes=["ps1"], sig=(c == 1))
            cp("act", QL[:, :, h, :], ps[1][:, 0:256].rearrange("p (c n) -> p c n", n=128), reads=["ps1"], writes=["QL"])
            cp("pool", QR[64:96, h, :], Q[64:96, 0:128], reads=[qk], writes=["QR"])
            if ui + 1 < len(units):
                emitQ(ui + 1)
            continue
        nkt = 4 * qs + 4
        ob = 6 if (ui % 2 == 0) else 0
        obk = "ps%d" % ob

        def stS(kt):
            j = kt - 4 * qs
            q0 = 128 * j if j > 0 else 0
            sb = 4 + ((gkt + kt) % 2)
            sl = (gkt + kt) % 4
            mm(ps[sb][:, q0:512], Kh[:96, 128 * kt:128 * kt + 128], Q[:96, q0:512], True, True,
               reads=[kk, qk], writes=["ps%d" % sb])
            act(PT[sl][:, q0:512], ps[sb][:, q0:512], AF.Exp, reads=["ps%d" % sb], writes=["PT%d" % sl], scale=SCALE)
            if j >= 0:
                tt("pool", PT[sl][:, q0:q0 + 128], PT[sl][:, q0:q0 + 128], tri[:, :], ALU.mult,
                   reads=["PT%d" % sl, "tri"], writes=["PT%d" % sl])

        def stPV(kt):
            j = kt - 4 * qs
            q0 = 128 * j if j > 0 else 0
            sl = (gkt + kt) % 4
            mm(ps[ob][:65, q0:512], Vh[:, kt, 0:65], PT[sl][:, q0:512], kt == 0, kt == nkt - 1,
               reads=[vk, v1k, "PT%d" % sl], writes=[obk], sig=True)

        stS(0)
        for kt in range(nkt):
            if kt + 1 < nkt:
                stS(kt + 1)
            if kt == 1:
                if ui + 1 < len(units):
                    emitQ(ui + 1)
                if qs == 0 and h + 1 < NH:
                    emitKV(h + 1)
            stPV(kt)
        gkt += nkt
        Osb, rl4 = Osbl[ui % 2], rl4l[ui % 2]
        ok_, rk_ = "Osb%d" % (ui % 2), "rl4%d" % (ui % 2)
        cp("act", Osb[:65, :], ps[ob][:65, :], reads=[obk], writes=[ok_])
        for jj in range(4):
            tr(ps[7][:, 65 * jj:65 * jj + 65], Osb[:65, 128 * jj:128 * jj + 128], identf[:65, :65],
               reads=[ok_, "identf"], writes=["ps7"], sig=(jj == 3))
        ov = ps[7][:, 0:260].rearrange("p (j v) -> p j v", v=65)
        recip(rl4[:, :], ov[:, :, 64], reads=["ps7"], writes=[rk_])
        tt("dve", attn_tok[:, 4 * qs:4 * qs + 4, 64 * h:64 * h + 64], ov[:, :, 0:64],
           bcast(rl4[:, :], [128, 4, 64]), ALU.mult, reads=["ps7", rk_], writes=["attn_tok"])

    if cfg.get("STOP") == "B":
        P.emit(nc)
        return nc
    A.reset(m_b)
    P.barrier(lambda e: e.memset(bar_t[:, :], 0.0))
    NSL = 8
    NKS = 4
    pg = [A.alloc([128, 289], BF, "pg%d" % i) for i in range(NSL)]
    KT = [A.alloc([128, 384], BF, "KT%d" % i) for i in range(NKS)]
    PTs = [A.alloc([128, 64], BF, "PTs%d" % i) for i in range(NKS)]
    ptb = A.alloc([128, NPAGES], I32, "ptb")
    idx = A.alloc([128, NPAGES], I32, "idx")
    iot = A.alloc([128, 1], I32, "iota")
    smask = A.alloc([128, SPC * 64], BF, "smask")
    olatl = [A.alloc([64, 256], BF, "olat%d" % i) for i in range(2)]
    rlsl = [A.alloc([64, 1], F32, "rls%d" % i) for i in range(2)]
    OLT = A.alloc([128, 2, NH, 128], BF, "OLT")
    dma("sp", ptb[:, :], ptab.partition_broadcast(128), writes=["ptb"])
    dma("sp", smask[:, :], smask_d, writes=["smask"])
    P.op("pool", lambda e: e.iota(iot[:, :], pattern=[[0, 1]], base=0, channel_multiplier=1), writes=["iota"])
    ts("pool", idx[:, :], ptb[:, :], PAGE, iot[:, 0:1], ALU.mult, ALU.add, reads=["ptb", "iota"], writes=["idx"])
    for i in range(NSL):
        memset("pool", pg[i][:, 288:289], 1.0, writes=["pg1_%d" % i])
    pages = []
    gi = 0
    for s in range(SPC):
        for p in range(NPG + 1):
            new = (p == NPG)
            pages.append(dict(s=s, p=p, new=new, gi=(None if new else gi), fi=(None if new else s * NPG + p)))
            if not new:
                gi += 1
    NPGS = len(pages)
    real = [pgd for pgd in pages if not pgd["new"]]

    def stG(g):
        if g >= len(real):
            return
        pgd = real[g]
        gs = g % NSL
        pgt = pg[gs]
        fi = pgd["fi"]

        def gat(e, pgt=pgt, fi=fi):
            return e.indirect_dma_start(out=pgt[:, 0:288], out_offset=None, in_=cache,
                                        in_offset=bass.IndirectOffsetOnAxis(ap=idx[:, fi:fi + 1], axis=0))
        P.op("pool", gat, reads=["idx"], writes=["pg%d" % gs], dma=True)

    def stT(i):
        if i >= NPGS or pages[i]["new"]:
            return
        g = pages[i]["gi"]
        gs = g % NSL
        pgk = "pg%d" % gs
        pgt = pg[gs]
        kb = g % 3
        ksl = g % NKS
        tr(psb(kb)[:, 0:128], pgt[:, 0:128], identb[:, :], reads=[pgk, "identb"], writes=["ps%d" % kb], sig=False)
        tr(psb(kb)[:, 128:256], pgt[:, 128:256], identb[:, :], reads=[pgk, "identb"], writes=["ps%d" % kb], sig=False)
        tr(psb(kb)[:96, 256:384], pgt[:, 192:288], identb[:, :], reads=[pgk, "identb"], writes=["ps%d" % kb])
        cp("dve", KT[ksl][:, 0:384], psb(kb)[:, 0:384], reads=["ps%d" % kb], writes=["KT%d" % ksl])

    def stQK(i):
        if i >= NPGS:
            return
        pgd = pages[i]
        s = pgd["s"]
        if not pgd["new"]:
            ksl = pgd["gi"] % NKS
            k0, k1, k2 = KT[ksl][:, 0:128], KT[ksl][:, 128:256], KT[ksl][64:96, 256:384]
            kreads = ["KT%d" % ksl]
        else:
            c0 = NPT * 128
            k0, k1, k2 = kvT[:, 0, c0:c0 + 128], kvT[:, 1, c0:c0 + 128], kvT[64:96, 2, c0:c0 + 128]
            kreads = ["kvT"]
        sbk = 3 + (i % 2)
        so = 64 * ((i // 2) % 2)
        pso = ps[sbk][:, so:so + 64]
        psk = "ps%d_%d" % (sbk, so)
        psl = i % NKS
        mm(pso, k0, QL[:, 0, :, 8 * s:8 * s + 8], True, False, reads=kreads + ["QL"], writes=[psk])
        mm(pso, k1, QL[:, 1, :, 8 * s:8 * s + 8], False, False, reads=kreads + ["QL"], writes=[psk])
        mm(pso, k2, QR[64:96, :, 8 * s:8 * s + 8], False, True, reads=kreads + ["QR"], writes=[psk])
        act(PTs[psl][:, :], pso, AF.Exp, reads=[psk], writes=["PTs%d" % psl], scale=SCALE)
        if pgd["new"]:
            tt("pool", PTs[psl][:, :], PTs[psl][:, :], smask[:, 64 * s:64 * s + 64], ALU.mult,
               reads=["PTs%d" % psl, "smask"], writes=["PTs%d" % psl])

    def stPV(i):
        pgd = pages[i]
        s = pgd["s"]
        psl = i % NKS
        ob = 5 if s % 2 == 0 else 7
        if not pgd["new"]:
            gs = pgd["gi"] % NSL
            vrhs = pg[gs][:, 0:289]
            vreads = ["pg%d" % gs, "pg1_%d" % gs]
        else:
            vrhs = kvbf_s[:, 0:289]
            vreads = ["kvbf_s", "kvbf_s1"]
        mm(ps[ob][:64, 0:289], PTs[psl][:, :], vrhs, pgd["p"] == 0, pgd["new"], reads=["PTs%d" % psl] + vreads,
           writes=["ps%d" % ob], sig=True)
        if pgd["new"]:
            olat, rls = olatl[s % 2], rlsl[s % 2]
            olk, rlk = "olat%d" % (s % 2), "rls%d" % (s % 2)
            recip(rls[:, :], ps[ob][:64, 288:289], reads=["ps%d" % ob], writes=[rlk])
            ts("dve", olat[:, :], ps[ob][:64, 0:256], rls[:, 0:1], None, ALU.mult, reads=["ps%d" % ob, rlk], writes=[olk])
            for c in range(2):
                tr(psb(6)[:, 64 * c:64 * c + 64], olat[:, 128 * c:128 * c + 128], identb[:64, :64],
                   reads=[olk, "identb"], writes=["ps6"], sig=(c == 1))
            cp("act", OLT[:, :, :, 8 * s:8 * s + 8], psb(6)[:, 0:128].rearrange("p (c h j) -> p c h j", c=2, h=NH),
               reads=["ps6"], writes=["OLT"])

    GD = 5
    for g in range(GD):
        stG(g)
    stT(0)
    stT(1)
    stQK(0)
    for i in range(NPGS):
        if not pages[i]["new"]:
            stG(pages[i]["gi"] + GD)
        stT(i + 2)
        stQK(i + 1)
        stPV(i)
    for h in range(NH):
        for c in range(2):
            mm(ps[1][:, 64 * h:64 * h + 64], OLT[:, c, h, :], wuv[:, c, 64 * h:64 * h + 64], c == 0, c == 1,
               reads=["OLT", "wuv"], writes=["ps1"], sig=(c == 1 and h == NH - 1))
    cp("act", attn_tok[:, NPT, :], ps[1][:, :], reads=["ps1"], writes=["attn_tok"])

    if cfg.get("STOP") == "S":
        P.emit(nc)
        return nc
    def phaseC_new():
        A.reset(m_persist)
        P.barrier(lambda e: e.memset(bar_t[:, :], 0.0))
        guardC = []
        wdn = A.alloc([128, NFC, D], BF, "wdn")
        wst = [A.alloc([128, 8, 256], BF, "wst%d" % i) for i in range(2)]
        x1 = A.alloc([128, 4, D], F32, "x1")
        x1_off = A.last
        h2T = A.alloc([128, 8, 512], BF, "h2T")
        actT = A.alloc([128, NFC, 512], BF, "actT")
        ugl = [A.alloc([128, 640], F32, "ug%d" % i) for i in range(2)]
        wos = A.at([128, 8, D], BF, "wos", A.last - 2560)
        uvl = [A.alloc([128, 640], F32, "uv%d" % i) for i in range(2)]
        cgl = [A.alloc([128, 512], F32, "cg%d" % i) for i in range(2)]
        cvl = [A.alloc([128, 512], F32, "cv%d" % i) for i in range(2)]
        xt2l = [A.alloc([128, D], F32, "xt2%d" % i) for i in range(1)]
        x1n = A.alloc([128, D], BF, "x1n")
        aT = A.alloc([128, 4, 128], BF, "aT")
        carry = A.alloc([128, 44, 2], F32, "carry")
        carry_s = A.at([128, 44, SPC, 2], F32, "carry_s", x1_off + 4096)
        wft = A.alloc([128, 44, 4], F32, "wft")
        gfb = A.alloc([128, D], F32, "gfb")
        sm2 = A.alloc([128, 16], F32, "sm2")
        stfl = [A.alloc([32, 512], F32, "stf%d" % i) for i in range(2)]
        x2l = [A.alloc([128, D], F32, "x2%d" % i) for i in range(2)]

        WOSK = ["ug0", "ug1", "uv0", "uv1", "cg0", "cg1", "cv0", "cv1"]
        NCH = (NST + 1) * NFC

        def emit_wst(g):
            if g < NCH:
                dma("pool", wst[g % 2][:, :, :], w_up[g % NFC], writes=["wst%d" % (g % 2)])
        dma("pool", wos[:, :, :], w_o, writes=WOSK)
        emit_wst(0)
        dma("sp", wft[:, :, :], wf, writes=["wft"])
        dma("sp", gfb[:, :], rowv[:, 1280:2304].partition_broadcast(128), writes=["gfb"])
        memset("pool", carry[:, :, :], 0.0, writes=["carry"])
        s2 = lambda i: sm2[:, i:i + 1]
        gch = 0
        gtl = 0

        for st in range(NST + 1):
            samp = (st == NST)
            ntile = 1 if samp else 4
            T = 128 * ntile
            if st > 0:
                dma("pool", wos[:, :, :], w_o, writes=WOSK)
            if samp:
                for g in range(11):
                    sb_ = stfl[g % 2]
                    sk_ = "stf%d" % (g % 2)
                    dma("sp", sb_[:, :], stf[:, 512 * g:512 * g + 512], writes=[sk_])
                    for i in range(4):
                        tr(ps[0][:, 32 * i:32 * i + 32], sb_[:, 128 * i:128 * i + 128], identf[:32, :32],
                           reads=[sk_, "identf"], writes=["ps0"], sig=(i == 3))
                    cp("dve", carry_s[:, 4 * g:4 * g + 4, :, :].rearrange("p c s j -> p c (s j)"),
                       ps[0][:, 0:128].rearrange("p (c n) -> p c n", n=32), reads=["ps0"], writes=["carry_s", "x1_1", "x1_2"])
            def P1(j):
                t = 4 * st + j
                xt2 = xt2l[0]
                xk2 = "xt20"
                sb = 0 if j % 2 == 0 else 9
                sak = "sa%d_" % (j % 2)
                dma("sp", xt2[:, :], (xs if samp else xp[128 * t:128 * t + 128, :]), writes=[xk2])
                act(aT[:, :, :].rearrange("p k n -> p (k n)"), attn_tok[:, t, :], AF.Square, reads=["attn_tok"],
                    writes=["aT", sak + "ss"], accum_out=s2(sb))
                rstd_from_ss(s2(sb), 512, s2(sb + 1), s2(sb + 2), sak)
                for c in range(4):
                    tr(psb(0)[:, 128 * c:128 * c + 128], attn_tok[:, t, 128 * c:128 * c + 128], identb[:, :],
                       reads=["attn_tok", "identb"], writes=["ps0"], sig=(c == 3))
                tt("dve", aT[:, :, :], psb(0)[:, 0:512].rearrange("p (k n) -> p k n", n=128),
                   bcast(vec[:, GAO:GAO + 4], [128, 4, 128]), ALU.mult, reads=["ps0", "vec"], writes=["aT"])
                for n in range(2):
                    for c in range(4):
                        mm(ps[1 + n][:, :], aT[:, c, :], wos[:, c, 512 * n:512 * n + 512], c == 0, c == 3,
                           reads=["aT"] + WOSK, writes=["ps%d" % (1 + n)])
                for n in range(2):
                    for c in range(4):
                        mm(ps[3 + n][:, :], convT[:, c, 128 * t:128 * t + 128], wos[:, 4 + c, 512 * n:512 * n + 512], c == 0, c == 3,
                           reads=["convT"] + WOSK, writes=["ps%d" % (3 + n)])
                for n in range(2):
                    stt("dve", x1[:, j, 512 * n:512 * n + 512], ps[1 + n][:, :], s2(sb + 2), xt2[:, 512 * n:512 * n + 512], ALU.mult, ALU.add,
                        reads=["ps%d" % (1 + n), sak, xk2], writes=["x1_%d" % j])
                for n in range(2):
                    stt("dve", x1[:, j, 512 * n:512 * n + 512], ps[3 + n][:, :], rstd_c[:, t:t + 1], x1[:, j, 512 * n:512 * n + 512],
                        ALU.mult, ALU.add, reads=["ps%d" % (3 + n), "ss3", "x1_%d" % j], writes=["x1_%d" % j])

            def P2(j):
                act(x1n[:, :], x1[:, j, :], AF.Square, reads=["x1_%d" % j], writes=["x1n", "sf_ss"], accum_out=s2(3))
                rstd_from_ss(s2(3), D, s2(4), s2(5), "sf_")
                act(x1n[:, :], x1[:, j, :], AF.Copy, reads=["x1_%d" % j, "sf_"], writes=["x1n"], scale=s2(5))
                for k in range(8):
                    tr(psb(5)[:, 128 * k:128 * k + 128], x1n[:, 128 * k:128 * k + 128], identb[:, :],
                       reads=["x1n", "identb"], writes=["ps5"], sig=(k == 7))
                tt("dve", h2T[:, :, 128 * j:128 * j + 128], psb(5)[:, :].rearrange("p (k n) -> p k n", n=128),
                   bcast(vec[:, GF:GF + 8], [128, 8, 128]), ALU.mult, reads=["ps5", "vec"], writes=["h2T"])

            P1(0)
            for j in range(ntile):
                if j + 1 < ntile:
                    P1(j + 1)
                P2(j)
            for i in range(NFC):
                par = gch % 2
                gch += 1
                wkk = "wst%d" % par
                emit_wst(gch)
                if st == 0:
                    dma("pool", wdn[:, i, :], w_dn[:, i, :], writes=["wdn%d" % i])
                banks = (6, 7) if (par == 0 or cfg.get("C_BANKS") == "same") else (0, 5)
                taps = []
                for half in range(2):
                    bank = banks[half]
                    ub = (ugl, uvl)[half][par]
                    cb = (cgl, cvl)[half][par]
                    ubk = ("ug%d", "uv%d")[half] % par
                    cbk = ("cg%d", "cv%d")[half] % par
                    ch = i + NFC * half
                    for k in range(8):
                        mm(ps[bank][:, 0:T], wst[par][:, k, 128 * half:128 * half + 128], h2T[:, k, 0:T], k == 0, k == 7,
                           reads=[wkk, "h2T"], writes=["ps%d" % bank])
                    if not samp:
                        cp("pool", ub[:, 0:2], carry[:, ch, :], reads=["carry"], writes=[ubk])
                        act(ub[:, 2:2 + T], ps[bank][:, 0:T], AF.Copy, reads=["ps%d" % bank], writes=[ubk])
                        cp("pool", carry[:, ch, :], ub[:, T:T + 2], reads=[ubk], writes=["carry"])
                        v0, v1 = ub[:, 0:T], ub[:, 1:1 + T]
                        co = cb[:, 0:T]
                        act(co, ps[bank][:, 0:T], AF.Identity, reads=["ps%d" % bank, "wft"], writes=[cbk],
                            scale=wft[:, ch, 2:3], bias=wft[:, ch, 3:4])
                    else:
                        u3 = ub[:, 0:160].rearrange("p (s j) -> p s j", j=10)
                        cp("pool", u3[:, :, 0:2], carry_s[:, ch, :, :], reads=["carry_s"], writes=[ubk])
                        act(u3[:, :, 2:10], ps[bank][:, 0:128].rearrange("p (s j) -> p s j", j=8), AF.Copy,
                            reads=["ps%d" % bank], writes=[ubk])
                        cp("pool", carry_s[:, ch, :, :], u3[:, :, 8:10], reads=[ubk], writes=["carry_s"])
                        v0, v1 = u3[:, :, 0:8], u3[:, :, 1:9]
                        co = cb[:, 0:128].rearrange("p (s j) -> p s j", j=8)
                        act(co, ps[bank][:, 0:128].rearrange("p (s j) -> p s j", j=8), AF.Identity,
                            reads=["ps%d" % bank, "wft"], writes=[cbk], scale=wft[:, ch, 2:3], bias=wft[:, ch, 3:4])
                    taps.append((co, v0, v1, ch, ubk, cbk))
                for kk_ in (1, 0):
                    for (co, v0, v1, ch, ubk, cbk) in taps:
                        stt("dve", co, (v1 if kk_ == 1 else v0), wft[:, ch, kk_:kk_ + 1], co, ALU.mult, ALU.add,
                            reads=[ubk, "wft", cbk], writes=[cbk])
                cgb, cvb = cgl[par], cvl[par]
                act(cgb[:, 0:T], cgb[:, 0:T], AF.Silu, reads=["cg%d" % par], writes=["cg%d" % par])
                tt(cfg.get("C_TTENG", "dve"), actT[:, i, 0:T], cgb[:, 0:T], cvb[:, 0:T], ALU.mult, reads=["cg%d" % par, "cv%d" % par], writes=["actT"])
            if st == NST - 1:
                for g in range(11):
                    for i in range(4):
                        ch = 4 * g + i
                        tr(ps[0][:2, 128 * i:128 * i + 128], carry[:, ch, :], identf[:, :],
                           reads=["carry", "identf"], writes=["ps0"], sig=(i == 3))
                    cp("act", stfl[g % 2][:2, :], ps[0][:2, :], reads=["ps0"], writes=["stf%d" % (g % 2)])
                    dma("sp", ffn_p[:, 512 * g:512 * g + 512], stfl[g % 2][:2, :], reads=["stf%d" % (g % 2)])
            if samp:
                for g in range(11):
                    for i in range(4):
                        ch = 4 * g + i
                        tr(ps[0][:32, 128 * i:128 * i + 128], carry_s[:, ch, :, :].rearrange("p s j -> p (s j)"), identf[:, :],
                           reads=["carry_s", "identf"], writes=["ps0"], sig=(i == 3))
                    cp("act", stfl[g % 2][:32, :], ps[0][:32, :], reads=["ps0"], writes=["stf%d" % (g % 2)])
                    dma("sp", ffn_s[:, 512 * g:512 * g + 512], stfl[g % 2][:32, :], reads=["stf%d" % (g % 2)])
            for j in range(ntile):
                t = 4 * st + j
                x2 = x2l[j % 2]
                x2k = "x2%d" % (j % 2)
                pb = (1, 2) if j % 2 == 0 else (3, 4)
                for n in range(2):
                    for i in range(NFC):
                        mm(ps[pb[n]][:, :], actT[:, i, 128 * j:128 * j + 128], wdn[:, i, 512 * n:512 * n + 512], i == 0, i == NFC - 1,
                           reads=["actT", "wdn%d" % i], writes=["ps%d" % pb[n]])
                for n in range(2):
                    tt("dve", x2[:, 512 * n:512 * n + 512], ps[pb[n]][:, :], x1[:, j, 512 * n:512 * n + 512], ALU.add,
                       reads=["ps%d" % pb[n], "x1_%d" % j], writes=[x2k])
                act(x1n[:, :], x2[:, :], AF.Square, reads=[x2k], writes=["x1n", "sy_ss"], accum_out=s2(6))
                rstd_from_ss(s2(6), D, s2(7), s2(8), "sy_")
                stt("dve", x2[:, :], x2[:, :], s2(8), gfb[:, :], ALU.mult, ALU.mult, reads=[x2k, "sy_", "gfb"], writes=[x2k])
                dma("sp", (y_s if samp else y_p[128 * t:128 * t + 128, :]), x2[:, :], reads=[x2k])


    def phaseC_old():
        A.reset(m_persist)
        P.barrier(lambda e: e.memset(bar_t[:, :], 0.0))
        guardC = []
        wdn = A.alloc([128, NFC, D], BF, "wdn")
        wst = [A.alloc([128, 8, 256], BF, "wst%d" % i) for i in range(2)]
        x1 = A.alloc([128, 4, D], F32, "x1")
        x1_off = A.last
        h2T = A.alloc([128, 8, 512], BF, "h2T")
        actT = A.alloc([128, NFC, 512], BF, "actT")
        wos = A.at([128, 8, D], BF, "wos", A.last)
        ug = A.alloc([128, 640], F32, "ug")
        uv = A.alloc([128, 640], F32, "uv")
        cg = A.alloc([128, 512], F32, "cg")
        cv = A.alloc([128, 512], F32, "cv")
        sg = A.alloc([128, 512], F32, "sg")
        xt2 = A.alloc([128, D], F32, "xt2")
        x1n = A.alloc([128, D], BF, "x1n")
        aT = A.alloc([128, 4, 128], BF, "aT")
        junk2 = A.alloc([128, D], BF, "junk2")
        carry = A.alloc([128, 44, 2], F32, "carry")
        carry_s = A.at([128, 44, SPC, 2], F32, "carry_s", x1_off + 4096)
        wft = A.alloc([128, 44, 4], F32, "wft")
        gfb = A.alloc([128, D], F32, "gfb")
        sm2 = A.alloc([128, 16], F32, "sm2")
        stfl = [A.alloc([32, 512], F32, "stf%d" % i) for i in range(2)]
        x2 = A.alloc([128, D], F32, "x2")
        assert A.mark() < 229000, A.mark()

        dma("pool", wdn[:, :, :], w_dn, reads=guardC, writes=["wdn"])
        dma("sp", wft[:, :, :], wf, reads=guardC, writes=["wft"])
        dma("sp", gfb[:, :], rowv[:, 1280:2304].partition_broadcast(128), reads=guardC, writes=["gfb"])
        memset("pool", carry[:, :, :], 0.0, writes=["carry"])
        s2 = lambda i: sm2[:, i:i + 1]
        GC = guardC

        for st in range(NST + 1):
            samp = (st == NST)
            ntile = 1 if samp else 4
            T = 128 * ntile
            dma("pool", wos[:, :, :], w_o, reads=GC, writes=["actT"])
            if samp:
                for g in range(11):
                    sb_ = stfl[g % 2]
                    sk_ = "stf%d" % (g % 2)
                    dma("sp", sb_[:, :], stf[:, 512 * g:512 * g + 512], reads=GC, writes=[sk_])
                    for i in range(4):
                        tr(ps[0][:, 32 * i:32 * i + 32], sb_[:, 128 * i:128 * i + 128], identf[:32, :32],
                           reads=[sk_, "identf"], writes=["ps0"], sig=(i == 3))
                    cp("dve", carry_s[:, 4 * g:4 * g + 4, :, :].rearrange("p c s j -> p c (s j)"),
                       ps[0][:, 0:128].rearrange("p (c n) -> p c n", n=32), reads=["ps0"] + GC, writes=["carry_s", "x1_1", "x1_2"])
            for j in range(ntile):
                t = 4 * st + j
                dma("sp", xt2[:, :], (xs if samp else xp[128 * t:128 * t + 128, :]), reads=GC, writes=["xt2"])
                act(junk2[:, 0:512], attn_tok[:, t, :], AF.Square, reads=["attn_tok"] + GC, writes=["junk2", "sa_ss"], accum_out=s2(0))
                rstd_from_ss(s2(0), 512, s2(1), s2(2), "sa_")
                for c in range(4):
                    tr(psb(0)[:, 128 * c:128 * c + 128], attn_tok[:, t, 128 * c:128 * c + 128], identb[:, :],
                       reads=["attn_tok", "identb"], writes=["ps0"], sig=(c == 3))
                tt("dve", aT[:, :, :], psb(0)[:, 0:512].rearrange("p (k n) -> p k n", n=128),
                   bcast(vec[:, GAO:GAO + 4], [128, 4, 128]), ALU.mult, reads=["ps0", "vec"] + GC, writes=["aT"])
                for n in range(2):
                    for c in range(4):
                        mm(ps[1 + n][:, :], aT[:, c, :], wos[:, c, 512 * n:512 * n + 512], c == 0, c == 3,
                           reads=["aT", "actT"], writes=["ps%d" % (1 + n)])
                for n in range(2):
                    for c in range(4):
                        mm(ps[3 + n][:, :], convT[:, c, 128 * t:128 * t + 128], wos[:, 4 + c, 512 * n:512 * n + 512], c == 0, c == 3,
                           reads=["convT", "actT"], writes=["ps%d" % (3 + n)])
                for n in range(2):
                    stt("dve", x1[:, j, 512 * n:512 * n + 512], ps[1 + n][:, :], s2(2), xt2[:, 512 * n:512 * n + 512], ALU.mult, ALU.add,
                        reads=["ps%d" % (1 + n), "sa_", "xt2"] + GC, writes=["x1_%d" % j])
                    stt("dve", x1[:, j, 512 * n:512 * n + 512], ps[3 + n][:, :], rstd_c[:, t:t + 1], x1[:, j, 512 * n:512 * n + 512],
                        ALU.mult, ALU.add, reads=["ps%d" % (3 + n), "ss3", "x1_%d" % j], writes=["x1_%d" % j])
                act(junk2[:, :], x1[:, j, :], AF.Square, reads=["x1_%d" % j], writes=["junk2", "sf_ss"], accum_out=s2(3))
                rstd_from_ss(s2(3), D, s2(4), s2(5), "sf_")
                act(x1n[:, :], x1[:, j, :], AF.Copy, reads=["x1_%d" % j, "sf_"], writes=["x1n"], scale=s2(5))
                for k in range(8):
                    tr(psb(5)[:, 128 * k:128 * k + 128], x1n[:, 128 * k:128 * k + 128], identb[:, :],
                       reads=["x1n", "identb"], writes=["ps5"], sig=(k == 7))
                tt("dve", h2T[:, :, 128 * j:128 * j + 128], psb(5)[:, :].rearrange("p (k n) -> p k n", n=128),
                   bcast(vec[:, GF:GF + 8], [128, 8, 128]), ALU.mult, reads=["ps5", "vec"] + GC, writes=["h2T"])
            for i in range(NFC):
                wsl = (st * NFC + i) % 2
                wkk = "wst%d" % wsl
                dma("pool", wst[wsl][:, :, :], w_up[i], reads=GC, writes=[wkk])
                for half, (bank, ub, ubk, cb, cbk, eng) in enumerate(((6, ug, "ug", cg, "cg", "dve"), (7, uv, "uv", cv, "cv", "pool"))):
                    ch = i + NFC * half
                    for k in range(8):
                        mm(ps[bank][:, 0:T], wst[wsl][:, k, 128 * half:128 * half + 128], h2T[:, k, 0:T], k == 0, k == 7,
                           reads=[wkk, "h2T"], writes=["ps%d" % bank])
                    if not samp:
                        cp("pool", ub[:, 0:2], carry[:, ch, :], reads=["carry", cbk] + GC, writes=[ubk])
                        act(ub[:, 2:2 + T], ps[bank][:, 0:T], AF.Copy, reads=["ps%d" % bank, cbk] + GC, writes=[ubk])
                        cp("pool", carry[:, ch, :], ub[:, T:T + 2], reads=[ubk], writes=["carry"])
                        v0, v1, v2 = ub[:, 0:T], ub[:, 1:1 + T], ub[:, 2:2 + T]
                        co = cb[:, 0:T]
                    else:
                        u3 = ub[:, 0:160].rearrange("p (s j) -> p s j", j=10)
                        cp("pool", u3[:, :, 0:2], carry_s[:, ch, :, :], reads=["carry_s", cbk] + GC, writes=[ubk])
                        act(u3[:, :, 2:10], ps[bank][:, 0:128].rearrange("p (s j) -> p s j", j=8), AF.Copy,
                            reads=["ps%d" % bank, cbk] + GC, writes=[ubk])
                        cp("pool", carry_s[:, ch, :, :], u3[:, :, 8:10], reads=[ubk], writes=["carry_s"])
                        v0, v1, v2 = u3[:, :, 0:8], u3[:, :, 1:9], u3[:, :, 2:10]
                        co = cb[:, 0:128].rearrange("p (s j) -> p s j", j=8)
                    ts(eng, co, v2, wft[:, ch, 2:3], wft[:, ch, 3:4], ALU.mult, ALU.add, reads=[ubk, "wft"], writes=[cbk])
                    stt(eng, co, v1, wft[:, ch, 1:2], co, ALU.mult, ALU.add, reads=[ubk, "wft", cbk], writes=[cbk])
                    stt(eng, co, v0, wft[:, ch, 0:1], co, ALU.mult, ALU.add, reads=[ubk, "wft", cbk], writes=[cbk])
                act(sg[:, 0:T], cg[:, 0:T], AF.Silu, reads=["cg"] + GC, writes=["sg"])
                tt("dve", actT[:, i, 0:T], sg[:, 0:T], cv[:, 0:T], ALU.mult, reads=["sg", "cv"] + GC, writes=["actT"])
            if st == NST - 1:
                for g in range(11):
                    for i in range(4):
                        ch = 4 * g + i
                        tr(ps[0][:2, 128 * i:128 * i + 128], carry[:, ch, :], identf[:, :],
                           reads=["carry", "identf"], writes=["ps0"], sig=(i == 3))
                    cp("act", stfl[g % 2][:2, :], ps[0][:2, :], reads=["ps0"], writes=["stf%d" % (g % 2)])
                    dma("sp", ffn_p[:, 512 * g:512 * g + 512], stfl[g % 2][:2, :], reads=["stf%d" % (g % 2)])
            if samp:
                for g in range(11):
                    for i in range(4):
                        ch = 4 * g + i
                        tr(ps[0][:32, 128 * i:128 * i + 128], carry_s[:, ch, :, :].rearrange("p s j -> p (s j)"), identf[:, :],
                           reads=["carry_s", "identf"], writes=["ps0"], sig=(i == 3))
                    cp("act", stfl[g % 2][:32, :], ps[0][:32, :], reads=["ps0"], writes=["stf%d" % (g % 2)])
                    dma("sp", ffn_s[:, 512 * g:512 * g + 512], stfl[g % 2][:32, :], reads=["stf%d" % (g % 2)])
            for j in range(ntile):
                t = 4 * st + j
                for n in range(2):
                    for i in range(NFC):
                        mm(ps[1 + n][:, :], actT[:, i, 128 * j:128 * j + 128], wdn[:, i, 512 * n:512 * n + 512], i == 0, i == NFC - 1,
                           reads=["actT", "wdn"], writes=["ps%d" % (1 + n)])
                for n in range(2):
                    tt("dve", x2[:, 512 * n:512 * n + 512], ps[1 + n][:, :], x1[:, j, 512 * n:512 * n + 512], ALU.add,
                       reads=["ps%d" % (1 + n), "x1_%d" % j] + GC, writes=["x2"])
                act(junk2[:, :], x2[:, :], AF.Square, reads=["x2"], writes=["junk2", "sy_ss"], accum_out=s2(6))
                rstd_from_ss(s2(6), D, s2(7), s2(8), "sy_")
                stt("dve", x2[:, :], x2[:, :], s2(8), gfb[:, :], ALU.mult, ALU.mult, reads=["x2", "sy_", "gfb"], writes=["x2"])
                dma("sp", (y_s if samp else y_p[128 * t:128 * t + 128, :]), x2[:, :], reads=["x2"])


    if cfg.get("CMODE", "new") == "new":
        phaseC_new()
    else:
        phaseC_old()
    P.emit(nc)
    return nc


def _bf(a):
    return np.ascontiguousarray(a).astype(ml_dtypes.bfloat16)


def host_consts(cfg):
    SEQ, PAST = cfg["SEQ"], cfg["PAST"]
    NT = SEQ + 128
    pos = np.concatenate([np.arange(SEQ), PAST + (np.arange(128) % DSEQ)]).astype(np.float32)
    inv = (np.float32(10000.0) ** (-(np.arange(16, dtype=np.float32) * np.float32(2.0) / np.float32(DR)))).astype(np.float32)
    ang = (pos[:, None] * inv[None, :]).astype(np.float32)
    cos = np.cos(ang).astype(np.float32)
    sin = np.sin(ang).astype(np.float32)
    cstok = np.concatenate([cos, sin], axis=1)
    cosT = np.concatenate([cos, cos], axis=1).T.copy()
    sinT = np.concatenate([-sin, sin], axis=1).T.copy()
    tri = (np.arange(128)[:, None] <= np.arange(128)[None, :]).astype(np.float32)
    k = np.arange(128)
    sm = np.zeros((128, SPC, NH, DSEQ), np.float32)
    for s in range(SPC):
        for t in range(DSEQ):
            sm[(k // DSEQ == s) & (k % DSEQ <= t), s, :, t] = 1.0
    return dict(cstok=cstok, cosT=cosT, sinT=sinT, identf=np.eye(128, dtype=np.float32), identb=_bf(np.eye(128)),
                tri=_bf(tri), smask=_bf(sm.reshape(128, SPC * 64)))


def host_weights(inp):
    f = np.float32
    w_in = inp["w_in"][0].reshape(8, 128, DIN).transpose(1, 0, 2)
    wuq = inp["w_uq"][0].reshape(DCQ, NH * 96)
    wuqs = inp["w_uq"][0].copy()
    wuqs[:, :, 64:80] = inp["w_uq"][0][:, :, 80:96]
    wuqs[:, :, 80:96] = inp["w_uq"][0][:, :, 64:80]
    wuqs = wuqs.reshape(DCQ, NH * 96)
    wk = np.zeros((3, 128, NH, 96), f)
    wk[0, :, :, 0:64] = inp["w_uk"][0][0:128]
    wk[1, :, :, 0:64] = inp["w_uk"][0][128:256]
    for h in range(NH):
        wk[2, 64:96, h, 64:96] = np.eye(32, dtype=f)
    vec = np.zeros((128, 64), f)
    vec[:, 0:8] = inp["g_attn_norm"][0].reshape(8, 128).T
    vec[:, 8:11] = inp["g_q_norm"][0].reshape(3, 128).T
    vec[:, 11:15] = inp["g_conv_out"][0].reshape(4, 128).T
    vec[:, 15:19] = inp["g_attn_out"][0].reshape(4, 128).T
    vec[:, 19:27] = inp["g_ffn_norm"][0].reshape(8, 128).T
    vec[:, 27:31] = inp["b_dw"][0].reshape(4, 128).T
    wfv = np.zeros((128, 44, 4), f)
    wfv[:, :, 0:3] = inp["w_ffn_dw"][0].reshape(3, 44, 128).transpose(2, 1, 0)
    wfv[:, :, 3] = inp["b_ffn_dw"][0].reshape(44, 128).T
    rowv = np.zeros((1, 3072), f)
    rowv[0, 0:256] = inp["g_kv_norm"][0]
    rowv[0, 256:768] = inp["g_conv_ln"][0]
    rowv[0, 768:1280] = inp["b_conv_ln"][0]
    rowv[0, 1280:2304] = inp["g_final"]
    rowv[0, 2304:2816] = inp["b_dw"][0]
    wup = inp["w_up"][0].reshape(8, 128, 2, NFC, 128).transpose(3, 1, 0, 2, 4).reshape(NFC, 128, 8, 256)
    c = np.ascontiguousarray
    return dict(
        w_in=c(w_in), w_uq=c(wuq.reshape(3, 128, 768).transpose(1, 0, 2)), w_uqs=c(wuqs.reshape(3, 128, 768).transpose(1, 0, 2)),
        w_k=c(wk.reshape(3, 128, 768).transpose(1, 0, 2)), w_ukT=c(inp["w_uk"][0].transpose(2, 1, 0)),
        w_uv=c(inp["w_uv"][0].reshape(2, 128, 512).transpose(1, 0, 2)),
        w_o=c(inp["w_o"][0].reshape(8, 128, D).transpose(1, 0, 2)), w_up=c(wup),
        w_dn=c(inp["w_down"][0].reshape(NFC, 128, D).transpose(1, 0, 2)),
        wdw=c(inp["w_dw"][0].reshape(CW, 4, 128).transpose(2, 1, 0)), vecs=vec, wf=wfv, rowv=rowv)


_NC_CACHE = {}


def run(inp, cfg):
    SEQ, NPG, NPOOL = cfg["SEQ"], cfg["NPG"], cfg["NPOOL"]
    key = tuple(sorted(cfg.items()))
    if key not in _NC_CACHE:
        _NC_CACHE[key] = build(cfg)
    nc = _NC_CACHE[key]
    consts = host_consts(cfg)
    wts = host_weights(inp)
    cache2d = np.ascontiguousarray(inp["cache_kv_latent"][0]).reshape(NPOOL * PAGE, DC + DR)
    in_maps = []
    for c in range(8):
        m = dict(consts)
        m.update(wts)
        m["xp"] = np.ascontiguousarray(inp["x_prompt"][c])
        m["xs"] = np.ascontiguousarray(inp["x_sample"][SPC * c:SPC * c + SPC]).reshape(128, D)
        m["cache"] = cache2d
        m["stc"] = np.ascontiguousarray(inp["state_conv"][0, SPC * c:SPC * c + SPC]).reshape(SPC * 30, DCONV)
        m["stf"] = np.ascontiguousarray(inp["state_ffn_conv"][0, SPC * c:SPC * c + SPC]).reshape(SPC * 2, 2 * DFF)
        m["ptab"] = np.ascontiguousarray(inp["page_table"][SPC * c:SPC * c + SPC]).reshape(1, SPC * NPG).astype(np.int32)
        in_maps.append(m)
    res = run_bass_kernel_spmd(nc, in_maps, core_ids=list(range(8)))
    R = res.results
    y_p = np.stack([R[c]["y_p"] for c in range(8)])
    y_s = np.concatenate([R[c]["y_s"].reshape(SPC, DSEQ, D) for c in range(8)])
    kv_p = np.stack([R[c]["kv_p"] for c in range(8)])[None]
    conv_p = np.stack([R[c]["conv_p"] for c in range(8)])[None]
    ffn_p = np.stack([R[c]["ffn_p"] for c in range(8)])[None]
    kv_s = np.concatenate([R[c]["kv_s"].reshape(SPC, DSEQ, DC + DR) for c in range(8)])[None]
    conv_s = np.concatenate([R[c]["conv_s"] for c in range(8)])[None]
    ffn_s = np.concatenate([R[c]["ffn_s"].reshape(SPC, 2, 2 * DFF) for c in range(8)])[None]
    return tuple(np.asarray(a, np.float32) for a in (y_p, y_s, kv_p, conv_p, ffn_p, kv_s, conv_s, ffn_s))


def kernel(**inputs):
    inp = {k: np.asarray(v) for k, v in inputs.items()}
    return run(inp, CFG_FULL)
```
